# Optimizing a Trainium2 kernel written in Bass

```python
import jax
import jax.numpy as jnp
from jax import lax
import numpy as np

D_MODEL = 1024
BATCH = 2
SEQ = 8192
DEPTH = 2
DEC_BATCH = 32
DEC_SEQ = 8
PAST_LEN = 8192
PAGE_SIZE = 128

N_META = 16
N_BRANCHES = 3
SB_HEADS = 8
SB_HEAD_DIM = 64
SB_WIDTH = SB_HEADS * SB_HEAD_DIM
ML_HEADS = 8
ML_HEAD_DIM = 64
ML_WIDTH = ML_HEADS * ML_HEAD_DIM
CONV_DIM = 512
CONV_WIDTH = 3
D_FF = 2816
Q_BLOCK = 128
ML_CHUNK = 128
RMS_EPS = 1e-6
IN_SIZES = (SB_WIDTH, SB_WIDTH, SB_WIDTH, ML_WIDTH, ML_WIDTH, ML_WIDTH, ML_HEADS, ML_HEADS, ML_WIDTH,
            CONV_DIM, CONV_DIM, CONV_DIM, N_BRANCHES * D_MODEL)
D_IN = sum(IN_SIZES)

kernel_name = "hybrid_stickbreak_mlstm_shortconv_macaron_step"


def rms_norm(x, gain):
    xf = x.astype(jnp.float32)
    y = xf * lax.rsqrt(jnp.mean(xf * xf, axis=-1, keepdims=True) + RMS_EPS)
    return (y * gain.astype(jnp.float32)).astype(x.dtype)


def swiglu(x, w_up, w_down):
    gate, up = jnp.split(x @ w_up, 2, axis=-1)
    return (jax.nn.silu(gate) * up) @ w_down


def _split_cols(h):
    idx, acc = [], 0
    for s in IN_SIZES[:-1]:
        acc += s
        idx.append(acc)
    return jnp.split(h, idx, axis=-1)


def _sb_block(q, k, v, q_pos, k_pos, bias):
    z = (jnp.einsum('bqhd,bkhd->bhqk', q, k).astype(jnp.float32) * (SB_HEAD_DIM ** -0.5)
         + bias.astype(jnp.float32)[None, :, None, None])
    valid = k_pos[None, :] < q_pos[:, None]
    log_beta = jax.nn.log_sigmoid(z)
    log_keep = jnp.where(valid, jax.nn.log_sigmoid(-z), 0.0)
    log_rest = lax.cumsum(log_keep, axis=3, reverse=True) - log_keep
    w = jnp.where(valid, jnp.exp(log_beta + log_rest), 0.0)
    return jnp.einsum('bhqk,bkhd->bqhd', w.astype(v.dtype), v)


def sb_attention(q, k, v, q_pos, k_pos, bias):
    bsz, tq, nh, hd = q.shape
    if tq <= Q_BLOCK:
        return _sb_block(q, k, v, q_pos, k_pos, bias)
    n_blk = -(-tq // Q_BLOCK)
    pad = n_blk * Q_BLOCK - tq
    qp = jnp.pad(q, ((0, 0), (0, pad), (0, 0), (0, 0)))
    posp = jnp.concatenate([q_pos, q_pos[-1] + 1 + jnp.arange(pad, dtype=jnp.int32)])
    qb = jnp.moveaxis(qp.reshape(bsz, n_blk, Q_BLOCK, nh, hd), 1, 0)
    pb = posp.reshape(n_blk, Q_BLOCK)
    out = lax.map(lambda a: _sb_block(a[0], k, v, a[1], k_pos, bias), (qb, pb))
    return jnp.moveaxis(out, 0, 1).reshape(bsz, n_blk * Q_BLOCK, nh, hd)[:, :tq]


def _mlstm_chunk(state, chunk):
    C, n, m = state
    q, k, v, log_i, log_f = chunk
    L = q.shape[1]
    b = jnp.cumsum(log_f, axis=1).transpose(0, 2, 1)
    li = log_i.transpose(0, 2, 1)
    causal = jnp.tril(jnp.ones((L, L), dtype=bool))
    log_d = jnp.where(causal, b[..., :, None] - b[..., None, :] + li[..., None, :], -jnp.inf)
    log_inter = b + m[..., None]
    m_t = jnp.maximum(log_inter, jnp.max(log_d, axis=-1))
    w_intra = jnp.exp(log_d - m_t[..., None])
    w_inter = jnp.exp(log_inter - m_t)
    s = jnp.einsum('blhd,bshd->bhls', q, k) * w_intra
    num = jnp.einsum('bhls,bshe->bhle', s, v) + w_inter[..., None] * jnp.einsum('blhd,bhde->bhle', q, C)
    den = jnp.sum(s, axis=-1) + w_inter * jnp.einsum('blhd,bhd->bhl', q, n)
    h = num / jnp.maximum(jnp.abs(den), jnp.exp(-m_t))[..., None]
    m_new = m_t[..., -1]
    w_end = jnp.exp(b[..., -1:] - b + li - m_new[..., None])
    decay = jnp.exp(b[..., -1] + m - m_new)
    C_new = decay[..., None, None] * C + jnp.einsum('bhs,bshd,bshe->bhde', w_end, k, v)
    n_new = decay[..., None] * n + jnp.einsum('bhs,bshd->bhd', w_end, k)
    return (C_new, n_new, m_new), h.transpose(0, 2, 1, 3)


def mlstm_run(state, q, k, v, log_i, log_f, lead):
    xs = (q, k, v, log_i, log_f)
    outs = []
    if lead > 0:
        state, h0 = _mlstm_chunk(state, tuple(a[:, :lead] for a in xs))
        outs.append(h0)
        xs = tuple(a[:, lead:] for a in xs)
    T = xs[0].shape[1]
    L = ML_CHUNK if T % ML_CHUNK == 0 else T
    n_ch = T // L
    chunks = tuple(jnp.moveaxis(a.reshape((a.shape[0], n_ch, L) + a.shape[2:]), 1, 0) for a in xs)
    state, h = lax.scan(_mlstm_chunk, state, chunks)
    h = jnp.moveaxis(h, 0, 1).reshape((h.shape[1], T) + h.shape[3:])
    outs.append(h)
    return state, jnp.concatenate(outs, axis=1)


def short_conv(u, buf, w):
    T = u.shape[1]
    up = jnp.concatenate([buf.astype(u.dtype), u], axis=1)
    y = w[0] * up[:, 0:T]
    for j in range(1, CONV_WIDTH):
        y = y + w[j] * up[:, j:j + T]
    return y, up[:, T:]


def token_mixer(xn, past_k, past_v, q_pos, k_pos, ml_state, conv_buf, lead,
                w_in, sb_bias, ml_i_bias, ml_f_bias, ml_head_g, conv_w, w_br_sb, w_br_ml, w_br_cv, w_out):
    bsz, T, _ = xn.shape
    f32 = jnp.float32
    (sq, sk, sv, mq, mk, mv, mi, mf, mo, cb, cc, ch, gates) = _split_cols(xn @ w_in)
    heads = lambda a, nh: a.reshape(bsz, T, nh, -1)
    k_new, v_new = heads(sk, SB_HEADS), heads(sv, SB_HEADS)
    if past_k is None:
        k_all, v_all = k_new, v_new
    else:
        k_all = jnp.concatenate([past_k.astype(k_new.dtype), k_new], axis=1)
        v_all = jnp.concatenate([past_v.astype(v_new.dtype), v_new], axis=1)
    y_sb = sb_attention(heads(sq, SB_HEADS), k_all, v_all, q_pos, k_pos, sb_bias).reshape(bsz, T, SB_WIDTH)
    log_i = mi.astype(f32) + ml_i_bias.astype(f32)
    log_f = jax.nn.log_sigmoid(mf.astype(f32) + ml_f_bias.astype(f32))
    ml_state, h = mlstm_run(ml_state, heads(mq, ML_HEADS).astype(f32),
                            heads(mk, ML_HEADS).astype(f32) * (ML_HEAD_DIM ** -0.5),
                            heads(mv, ML_HEADS).astype(f32), log_i, log_f, lead)
    h = h * lax.rsqrt(jnp.mean(h * h, axis=-1, keepdims=True) + RMS_EPS)
    y_ml = (jax.nn.sigmoid(mo.astype(f32)) * h.reshape(bsz, T, ML_WIDTH) * ml_head_g.astype(f32)).astype(xn.dtype)
    conv_y, conv_buf = short_conv(cc * ch, conv_buf, conv_w)
    y_cv = cb * conv_y
    g = jax.nn.sigmoid(gates.astype(f32)).astype(xn.dtype).reshape(bsz, T, N_BRANCHES, D_MODEL)
    merged = (g[..., 0, :] * (y_sb @ w_br_sb) + g[..., 1, :] * (y_ml @ w_br_ml)
              + g[..., 2, :] * (y_cv @ w_br_cv))
    return merged @ w_out, (k_new, v_new, ml_state[0], ml_state[1], ml_state[2], conv_buf)


def layer(x, lw, past_k, past_v, q_pos, k_pos, ml_state, conv_buf, lead):
    (g_ff1, ff1_up, ff1_down, g_mix, w_in, sb_bias, ml_i_bias, ml_f_bias, ml_head_g, conv_w,
     w_br_sb, w_br_ml, w_br_cv, w_out, g_ff2, ff2_up, ff2_down) = lw
    x = x + 0.5 * swiglu(rms_norm(x, g_ff1), ff1_up, ff1_down)
    y, new_state = token_mixer(rms_norm(x, g_mix), past_k, past_v, q_pos, k_pos, ml_state, conv_buf, lead,
                               w_in, sb_bias, ml_i_bias, ml_f_bias, ml_head_g, conv_w,
                               w_br_sb, w_br_ml, w_br_cv, w_out)
    x = x + y
    x = x + 0.5 * swiglu(rms_norm(x, g_ff2), ff2_up, ff2_down)
    return x, new_state


def setup_inputs(seed: int = 0) -> dict:
    key = jax.random.key(seed)
    ks = jax.random.split(key, 32)
    nrm = lambda k, shape, scale: scale * jax.random.normal(k, shape, jnp.float32)
    n_pages = PAST_LEN // PAGE_SIZE
    n_used = DEC_BATCH * n_pages
    n_phys = n_used + n_used // 4
    page_table = jax.random.permutation(ks[4], n_phys)[:n_used].reshape(DEC_BATCH, n_pages).astype(jnp.int32)
    kv_shape = (DEPTH, n_phys, PAGE_SIZE, SB_HEADS, SB_HEAD_DIM)
    return {
        "x_prompt": nrm(ks[0], (BATCH, SEQ, D_MODEL), 1.0),
        "x_sample": nrm(ks[1], (DEC_BATCH, DEC_SEQ, D_MODEL), 1.0),
        "cache_sb_k": nrm(ks[2], kv_shape, 1.0),
        "cache_sb_v": nrm(ks[3], kv_shape, 1.0),
        "page_table": page_table,
        "state_ml_C": nrm(ks[5], (DEPTH, DEC_BATCH, ML_HEADS, ML_HEAD_DIM, ML_HEAD_DIM), 0.5),
        "state_ml_n": nrm(ks[6], (DEPTH, DEC_BATCH, ML_HEADS, ML_HEAD_DIM), 1.0),
        "state_ml_m": nrm(ks[7], (DEPTH, DEC_BATCH, ML_HEADS), 0.5),
        "state_conv": nrm(ks[8], (DEPTH, DEC_BATCH, CONV_WIDTH - 1, CONV_DIM), 1.0),
        "meta_tokens": nrm(ks[9], (N_META, D_MODEL), 1.0),
        "norm_ff1": 1.0 + nrm(ks[10], (DEPTH, D_MODEL), 0.01),
        "ffn1_w_up": nrm(ks[11], (DEPTH, D_MODEL, 2 * D_FF), D_MODEL ** -0.5),
        "ffn1_w_down": nrm(ks[12], (DEPTH, D_FF, D_MODEL), D_FF ** -0.5),
        "norm_mix": 1.0 + nrm(ks[13], (DEPTH, D_MODEL), 0.01),
        "w_in": nrm(ks[14], (DEPTH, D_MODEL, D_IN), D_MODEL ** -0.5),
        "sb_logit_bias": jnp.linspace(-9.0, -6.0, SB_HEADS, dtype=jnp.float32)[None, :] + nrm(ks[27], (DEPTH, SB_HEADS), 0.1),
        "ml_igate_bias": nrm(ks[15], (DEPTH, ML_HEADS), 0.1),
        "ml_fgate_bias": jnp.linspace(3.0, 6.0, ML_HEADS, dtype=jnp.float32)[None, :] + nrm(ks[16], (DEPTH, ML_HEADS), 0.1),
        "ml_head_norm": 1.0 + nrm(ks[17], (DEPTH, ML_WIDTH), 0.01),
        "conv_w": nrm(ks[18], (DEPTH, CONV_WIDTH, CONV_DIM), CONV_WIDTH ** -0.5),
        "w_branch_sb": nrm(ks[19], (DEPTH, SB_WIDTH, D_MODEL), SB_WIDTH ** -0.5),
        "w_branch_ml": nrm(ks[20], (DEPTH, ML_WIDTH, D_MODEL), ML_WIDTH ** -0.5),
        "w_branch_cv": nrm(ks[21], (DEPTH, CONV_DIM, D_MODEL), CONV_DIM ** -0.5),
        "w_out": nrm(ks[22], (DEPTH, D_MODEL, D_MODEL), D_MODEL ** -0.5),
        "norm_ff2": 1.0 + nrm(ks[23], (DEPTH, D_MODEL), 0.01),
        "ffn2_w_up": nrm(ks[24], (DEPTH, D_MODEL, 2 * D_FF), D_MODEL ** -0.5),
        "ffn2_w_down": nrm(ks[25], (DEPTH, D_FF, D_MODEL), D_FF ** -0.5),
        "norm_final": 1.0 + nrm(ks[26], (D_MODEL,), 0.01),
    }


def reference(x_prompt, x_sample, cache_sb_k, cache_sb_v, page_table, state_ml_C, state_ml_n, state_ml_m,
              state_conv, meta_tokens, norm_ff1, ffn1_w_up, ffn1_w_down, norm_mix, w_in, sb_logit_bias,
              ml_igate_bias, ml_fgate_bias, ml_head_norm, conv_w, w_branch_sb, w_branch_ml, w_branch_cv, w_out,
              norm_ff2, ffn2_w_up, ffn2_w_down, norm_final):
    f32 = jnp.float32
    bp = x_prompt.shape[0]
    bs, ts = x_sample.shape[0], x_sample.shape[1]
    meta = jnp.broadcast_to(meta_tokens.astype(x_prompt.dtype)[None], (bp, N_META, D_MODEL))
    xp = jnp.concatenate([meta, x_prompt], axis=1)
    pos_p = jnp.arange(xp.shape[1], dtype=jnp.int32)
    past_len = page_table.shape[1] * PAGE_SIZE
    pos_sq = past_len + jnp.arange(ts, dtype=jnp.int32)
    pos_sk = jnp.arange(past_len + ts, dtype=jnp.int32)
    xs = x_sample
    new_p, new_s = [], []
    for l in range(DEPTH):
        lw = (norm_ff1[l], ffn1_w_up[l], ffn1_w_down[l], norm_mix[l], w_in[l], sb_logit_bias[l], ml_igate_bias[l],
              ml_fgate_bias[l], ml_head_norm[l], conv_w[l], w_branch_sb[l], w_branch_ml[l], w_branch_cv[l], w_out[l],
              norm_ff2[l], ffn2_w_up[l], ffn2_w_down[l])
        ml0 = (jnp.zeros((bp, ML_HEADS, ML_HEAD_DIM, ML_HEAD_DIM), f32),
               jnp.zeros((bp, ML_HEADS, ML_HEAD_DIM), f32), jnp.zeros((bp, ML_HEADS), f32))
        conv0 = jnp.zeros((bp, CONV_WIDTH - 1, CONV_DIM), xp.dtype)
        xp, st_p = layer(xp, lw, None, None, pos_p, pos_p, ml0, conv0, N_META)
        past_k = cache_sb_k[l][page_table].reshape(bs, past_len, SB_HEADS, SB_HEAD_DIM)
        past_v = cache_sb_v[l][page_table].reshape(bs, past_len, SB_HEADS, SB_HEAD_DIM)
        ml_s = (state_ml_C[l].astype(f32), state_ml_n[l].astype(f32), state_ml_m[l].astype(f32))
        xs, st_s = layer(xs, lw, past_k, past_v, pos_sq, pos_sk, ml_s, state_conv[l], 0)
        new_p.append(st_p)
        new_s.append(st_s)
    y_prompt = rms_norm(xp, norm_final)[:, N_META:]
    y_sample = rms_norm(xs, norm_final)
    stk = lambda states, i: jnp.stack([s[i] for s in states])
    return (y_prompt, y_sample,
            stk(new_p, 0), stk(new_p, 1), stk(new_p, 2), stk(new_p, 3), stk(new_p, 4), stk(new_p, 5),
            stk(new_s, 0), stk(new_s, 1), stk(new_s, 2), stk(new_s, 3), stk(new_s, 4), stk(new_s, 5))
```

```python
import numpy as np
import concourse.bass as bass
import concourse.mybir as mybir
from concourse.bass_utils import run_bass_kernel_spmd

F32 = mybir.dt.float32
BF16 = mybir.dt.bfloat16
I32 = mybir.dt.int32
AF = mybir.ActivationFunctionType
ALU = mybir.AluOpType

COMPUTE = ("pe", "act", "dve", "pool")
OWN_SYNC = True
NROT = 8
D = 1024
KC = 8
DFF = 2816
NFF = 22
HW = 512
DIN = 8208
EPS = 1e-6
OFF = dict(sq=0, sk=512, sv=1024, mq=1536, mk=2048, mv=2560, mi=3072, mf=3080, mo=3088, cb=3600, cc=4112, ch=4624, g=5136)


class Prog:
    def __init__(self, nc):
        self.nc = nc
        self.ops = []
        self.cnt = {e: 0 for e in COMPUTE}
        self.dcnt = {"sp": 0, "pool": 0, "act": 0}
        self.lastw = {}
        self.readers = {}

    def _deps(self, reads, writes):
        deps = []
        for k in reads:
            if k in self.lastw:
                deps.append(self.lastw[k])
        for k in writes:
            if k in self.lastw:
                deps.append(self.lastw[k])
            deps.extend(self.readers.get(k, ()))
        return deps

    def _commit(self, tok, reads, writes):
        for k in reads:
            self.readers.setdefault(k, []).append(tok)
        for k in writes:
            self.lastw[k] = tok
            self.readers[k] = []

    def op(self, eng, fn, reads=(), writes=()):
        writes = list(writes) + [k for k in reads if k.startswith("ps") and k not in writes]
        deps = self._deps(reads, writes)
        self.cnt[eng] += 1
        tok = ("c", eng, self.cnt[eng])
        self.ops.append((eng, fn, deps, tok))
        self._commit(tok, reads, writes)

    def dma(self, eng, fn, reads=(), writes=()):
        deps = self._deps(reads, writes)
        k = self.dcnt[eng]
        self.dcnt[eng] += 1
        tok = ("d", eng, k % NROT, 16 * (k // NROT + 1))
        if k >= NROT:
            deps.append(("d", eng, k % NROT, 16 * (k // NROT)))
        self.ops.append((eng, fn, deps, tok))
        self._commit(tok, reads, writes)

    def emit(self, final_wait_eng="sp"):
        nc = self.nc
        sem_c = {e: nc.alloc_semaphore(f"c_{e}") for e in COMPUTE}
        sem_d = {e: [nc.alloc_semaphore(f"d_{e}{i}") for i in range(NROT)] for e in self.dcnt}
        streams = {"pe": [], "act": [], "dve": [], "pool": [], "sp": []}
        for o in self.ops:
            streams[o[0]].append(o)

        def run(ename, e):
            waited = {}
            for (_, fn, deps, tok) in streams[ename]:
                need = {}
                for d in deps:
                    if d[0] == "c":
                        if d[1] == ename and (ename == "pe" or not OWN_SYNC):
                            continue
                        key = ("c", d[1]); val = d[2]
                    else:
                        key = ("d", d[1], d[2]); val = d[3]
                    if waited.get(key, 0) >= val:
                        continue
                    need[key] = max(need.get(key, 0), val)
                for key, val in need.items():
                    s = sem_c[key[1]] if key[0] == "c" else sem_d[key[1]][key[2]]
                    e.wait_ge(s, val)
                    waited[key] = val
                ins = fn(e)
                if tok[0] == "c":
                    ins.then_inc(sem_c[tok[1]], 1)
                else:
                    ins.then_inc(sem_d[tok[1]][tok[2]], 16)
            if ename == final_wait_eng:
                for en in COMPUTE:
                    if self.cnt[en]:
                        e.wait_ge(sem_c[en], self.cnt[en])
                for en, k in self.dcnt.items():
                    for i in range(NROT):
                        n = (k - i + NROT - 1) // NROT if k > i else 0
                        if n:
                            e.wait_ge(sem_d[en][i], 16 * n)

        with nc.Block() as block:
            @block.tensor
            def _(e): run("pe", e)

            @block.scalar
            def _(e): run("act", e)

            @block.vector
            def _(e): run("dve", e)

            @block.gpsimd
            def _(e): run("pool", e)

            @block.sync
            def _(e): run("sp", e)


class _Stop(Exception):
    pass


def _ck(k):
    return None


def _ckq(k):
    return False


class Rot:
    def __init__(self, name, n):
        self.name, self.n, self.i = name, n, 0

    def next(self):
        i = self.i % self.n
        self.i += 1
        return i


def build(cfg):
    TP, NSEQ, NPG, NPHYS, DEPTH = cfg["TP"], cfg["NSEQ"], cfg["NPG"], cfg["NPHYS"], cfg["DEPTH"]
    DS = 8
    NS = NSEQ * DS
    NBLK = (TP + 127) // 128
    nc = bass.Bass("TRN2", target_bir_lowering=False)
    P = Prog(nc)

    def din(name, shape, dt=F32):
        return nc.dram_tensor(name, list(shape), dt, kind="ExternalInput").ap()

    def dout(name, shape, dt=F32):
        return nc.dram_tensor(name, list(shape), dt, kind="ExternalOutput").ap()

    xp = din("xp", [TP, D]); xs = din("xs", [NS, D])
    ck = din("ck", [DEPTH * NPHYS * 128, HW]); cv = din("cv", [DEPTH * NPHYS * 128, HW])
    pt_in = din("pt", [NSEQ, NPG], I32)
    sC = din("sC", [DEPTH, NSEQ, 8, 64, 64]); sn = din("sn", [DEPTH, NSEQ, 8, 64]); sm = din("sm", [DEPTH, NSEQ, 8])
    scv = din("scv", [DEPTH, NSEQ, 2, HW])
    w_g1 = din("g1", [DEPTH, D]); w_up1 = din("up1", [DEPTH, D, 2 * DFF]); w_dn1 = din("dn1", [DEPTH, DFF, D])
    w_gm = din("gm", [DEPTH, D]); w_in = din("win", [DEPTH, D, DIN])
    w_sbb = din("sbb", [DEPTH, 8]); w_ib = din("ib", [DEPTH, 8]); w_fb = din("fb", [DEPTH, 8]); w_hg = din("hg", [DEPTH, HW])
    w_cw = din("cw", [DEPTH, 3, HW])
    w_br = [din("brs", [DEPTH, HW, D]), din("brm", [DEPTH, HW, D]), din("brc", [DEPTH, HW, D])]
    w_o = din("wo", [DEPTH, D, D])
    w_g2 = din("g2", [DEPTH, D]); w_up2 = din("up2", [DEPTH, D, 2 * DFF]); w_dn2 = din("dn2", [DEPTH, DFF, D])
    w_gf = din("gf", [D])
    yp = dout("yp", [TP, D]); ys = dout("ys", [NS, D])
    kp = dout("kp", [DEPTH, TP, HW]); vp = dout("vp", [DEPTH, TP, HW])
    Cp = dout("Cp", [DEPTH, 8, 64, 64]); np_ = dout("np", [DEPTH, 8, 64]); mp = dout("mp", [DEPTH, 8]); cvp = dout("cvp", [DEPTH, 2, HW])
    ks = dout("ks", [DEPTH, NS, HW]); vs = dout("vs", [DEPTH, NS, HW])
    Cs = dout("Cs", [DEPTH, NSEQ, 8, 64, 64]); ns_ = dout("ns", [DEPTH, NSEQ, 8, 64]); ms_ = dout("ms", [DEPTH, NSEQ, 8])
    cvs = dout("cvs", [DEPTH, NSEQ, 2, HW])
    ktc = nc.dram_tensor("ktc", [DEPTH, NBLK, 128, 4, 128], BF16, kind="Internal").ap()
    vc = nc.dram_tensor("vc", [DEPTH, NBLK, 128, HW], BF16, kind="Internal").ap()

    def sb(name, shape, dt=F32):
        return nc.alloc_sbuf_tensor(name, list(shape), dt)

    NTM = 512
    xT = sb("xT", [128, KC, NTM]); xnT = sb("xnT", [128, KC, NTM], BF16)
    hT = sb("hT", [128, 8, NTM], BF16)
    sqT = sb("sqT", [128, 4, NTM], BF16); skT = sb("skT", [128, 4, NTM], BF16)
    mqT = sb("mqT", [128, 4, NTM], BF16); mkT = sb("mkT", [128, 4, NTM], BF16)
    sv_tok = sb("sv_tok", [128, 4, HW], BF16); mk_tok = sb("mk_tok", [128, 4, HW], BF16)
    mv_ext = sb("mv_ext", [128, 4, 4, 132], BF16)
    moS = sb("moS", [128, 4, NTM], BF16); cbT = sb("cbT", [128, 4, NTM], BF16)
    uext_1 = sb("uext_1", [128, 4, 2 + NTM])
    uext_p = [uext_1 for l in range(DEPTH)]
    halo = [sb(f"halo{l}", [128, 4, 2]) for l in range(DEPTH)]
    uext_s = sb("uext_s", [128, 4, 2 + DS])
    ybT = [sb(f"ybT{j}", [128, 4, NTM], BF16) for j in range(3)]
    mergedT = hT
    liT = sb("liT", [8, NTM]); lfT = sb("lfT", [8, NTM]); mT = sb("mT", [8, NTM]); bT = sb("bT", [8, NTM])
    zeros8 = sb("zeros8", [8, NTM])
    piece = [sb(f"piece{i}", [128, 2304], BF16) for i in range(2)]
    tmpf = [sb(f"tmpf{i}", [128, 512]) for i in range(5)]
    tmpb = [sb(f"tmpb{i}", [128, 512], BF16) for i in range(4)]
    xin = [sb(f"xin{i}", [128, D]) for i in range(1)]
    ident = sb("ident", [128, 128]); ones_f = sb("ones_f", [128, 128])
    onesm = sb("onesm", [128, 128], BF16)
    ones_b = sb("ones_b", [128, 128], BF16)
    blk64 = sb("blk64", [128, 128], BF16)
    BDf = sb("BDf", [128, 128])
    mask_lt = sb("mask_lt", [128, 128])
    mask_le = sb("mask_le", [128, 128])
    UI = sb("UI", [128, 128], BF16)
    CM = sb("CM", [128, 128], BF16)
    zeros_b = sb("zeros_b", [128, 512], BF16)
    selH = sb("selH", [8, 8, 128]); selP = sb("selP", [8, 4, 128])
    gcol = sb("gcol", [128, 3 * DEPTH + 1, KC])
    hgcol = sb("hgcol", [128, DEPTH, 4]); cwcol = sb("cwcol", [128, DEPTH, 4, 3])
    b8 = sb("b8", [8, DEPTH, 4])
    sbb1 = sb("sbb1", [1, DEPTH, 8]); sbhi = sb("sbhi", [1, DEPTH, 8], BF16); sblo = sb("sblo", [1, DEPTH, 8], BF16)
    sbt = sb("sbt", [1, DEPTH, 8])
    bias2 = [sb(f"bias2_{l}", [2, 8 * 128], BF16) for l in range(DEPTH)]
    bias2s = [sb(f"bias2s_{l}", [2, 8 * 16], BF16) for l in range(DEPTH)]
    ones2 = sb("ones2", [2, 128], BF16)
    epsc = sb("epsc", [128, 1])
    blo_tmp = sb("blo_tmp", [1, 1024], BF16)
    Qbd = sb("Qbd", [128, 4, 256], BF16)
    ktb = [sb(f"ktb{i}", [128, 4, 128], BF16) for i in range(3)]
    vb = [sb(f"vb{i}", [128, HW], BF16) for i in range(3)]
    kpg = [sb(f"kpg{i}", [128, HW]) for i in range(1)]; vpg = [sb(f"vpg{i}", [128, HW]) for i in range(1)]
    Et = [sb(f"Et{i}", [128, 512]) for i in range(2)]; Xt = [sb(f"Xt{i}", [128, 512]) for i in range(1)]
    Lt = [sb(f"Lt{i}", [128, 512], BF16) for i in range(2)]; Wt = [sb(f"Wt{i}", [128, 512], BF16) for i in range(2)]
    Cpad = [sb(f"Cpad{i}", [128, 4, 128]) for i in range(DEPTH + 1)]
    Cb = [sb(f"Cb{i}", [128, 4, 128], BF16) for i in range(DEPTH + 1)]
    ncol = [sb(f"ncol{i}", [128, 4]) for i in range(DEPTH + 1)]
    Npad = [sb(f"Npad{i}", [128, 4, 128], BF16) for i in range(DEPTH + 1)]
    m8 = [sb(f"m8_{i}", [8, 1]) for i in range(DEPTH + 1)]
    rp8 = sb("rp8", [8, 1])
    acol = sb("acol", [128, 8]); wend = sb("wend", [128, 8]); rend = sb("rend", [128, 8])
    r_bc = sb("r_bc", [128, 8, 128]); rp_bc = sb("rp_bc", [128, 4, 128]); nm_bc = sb("nm_bc", [128, 4, 128])
    nrprev = sb("nrprev", [128, 4]); decay = sb("decay", [128, 4])
    Dt = sb("Dt", [128, 2, 128]); SWt = sb("SWt", [128, 2, 128], BF16)
    kw = sb("kw", [128, HW], BF16)
    ptb = sb("ptb", [128, NSEQ, NPG], I32); ptf = sb("ptf", [128, NSEQ, NPG]); pidx = sb("pidx", [128, DEPTH, NSEQ, NPG], I32)
    iota_p = sb("iota_p", [128, 1])
    ps = [nc.alloc_psum_tensor(f"ps{i}", [128, 512], F32) for i in range(8)]
    psrot = Rot("ps", 4)
    rot = {n: Rot(n, k) for n, k in [("piece", 2), ("tmpf", 5), ("tmpb", 4), ("xin", 1), ("ktb", 3), ("vb", 3),
                                     ("pg", 1), ("E", 2), ("X", 1), ("L", 2), ("W", 2)]}

    def PS():
        i = psrot.next()
        return ps[i], f"ps{i}"

    def TF():
        i = rot["tmpf"].next()
        return tmpf[i], f"tmpf{i}"

    def TB():
        i = rot["tmpb"].next()
        return tmpb[i], f"tmpb{i}"

    def fin():
        P.emit()
        return nc

    P.op("pool", lambda e: e.memset(ones_f[:], 1.0), writes=["ones_f"])
    P.op("pool", lambda e: e.memset(zeros_b[:], 0.0), writes=["zeros_b"])
    P.op("pool", lambda e: e.memset(zeros8[:], 0.0), writes=["zeros8"])
    P.op("pool", lambda e: e.memset(onesm[:], 1.0 / 1024), writes=["onesm"])
    P.op("pool", lambda e: e.memset(ones_b[:], 1.0), writes=["ones_b"])
    P.op("pool", lambda e: e.memset(ones2[:], 1.0), writes=["ones2"])
    P.op("pool", lambda e: e.memset(epsc[:], EPS), writes=["epsc"])

    def asel(out, pattern, op, cm, key, base=0, src=None, skey="ones_f"):
        src = ones_f[:] if src is None else src
        P.op("pool", lambda e: e.affine_select(out=out, in_=src, pattern=pattern, compare_op=op, fill=0.0, base=base,
                                               channel_multiplier=cm), reads=[skey], writes=[key])
    asel(ident[:], [[-1, 128]], ALU.is_equal, 1, "ident")
    asel(mask_lt[:], [[1, 128]], ALU.is_gt, -1, "mask_lt")
    asel(mask_le[:], [[1, 128]], ALU.is_ge, -1, "mask_le")
    asel(UI[:], [[-1, 128]], ALU.is_ge, 1, "UI")
    asel(CM[:], [[1, 128]], ALU.is_gt, -1, "CM")
    P.op("pool", lambda e: e.memset(BDf[:], 0.0), writes=["BDf"])
    P.op("pool", lambda e: e.memset(BDf[0:64, 0:64], 1.0), writes=["BDf"])
    P.op("pool", lambda e: e.memset(BDf[64:128, 64:128], 1.0), writes=["BDf"])
    P.op("pool", lambda e: e.tensor_scalar(out=blk64[:], in0=BDf[:], scalar1=1.0 / 64, scalar2=None, op0=ALU.mult),
         reads=["BDf"], writes=["blk64"])
    for h in range(8):
        asel(selH[:, h, :], [[0, 128]], ALU.is_equal, 1, "selH", base=-h, src=ones_f[0:8, :])
    for p in range(4):
        asel(selP[:, p, 0:64], [[0, 64]], ALU.is_equal, 1, "selP", base=-2 * p, src=ones_f[0:8, 0:64])
        asel(selP[:, p, 64:128], [[0, 64]], ALU.is_equal, 1, "selP", base=-2 * p - 1, src=ones_f[0:8, 0:64])
    P.op("pool", lambda e: e.iota(iota_p[:], pattern=[[0, 1]], base=0, channel_multiplier=1, allow_small_or_imprecise_dtypes=True),
         writes=["iota_p"])
    if _ckq(-1): return fin()
    gl = []
    for l in range(DEPTH):
        gl += [w_g1[l], w_gm[l], w_g2[l]]
    gl.append(w_gf)
    for i, g in enumerate(gl):
        P.dma("sp", lambda e, i=i, g=g: e.dma_start(out=gcol[:, i, :], in_=g.rearrange("(k p) -> p k", p=128), allow_slow_non_contiguous=True),
              writes=["gcol"])
    for l in range(DEPTH):
        P.dma("sp", lambda e, l=l: e.dma_start(out=hgcol[:, l, :], in_=w_hg[l].rearrange("(k p) -> p k", p=128), allow_slow_non_contiguous=True),
              writes=["hgcol"])
        for j in range(3):
            P.dma("sp", lambda e, l=l, j=j: e.dma_start(out=cwcol[:, l, :, j], in_=w_cw[l, j].rearrange("(k p) -> p k", p=128), allow_slow_non_contiguous=True),
                  writes=["cwcol"])
        P.dma("sp", lambda e, l=l: e.dma_start(out=b8[:, l, 0:1], in_=w_ib[l].rearrange("(h o) -> h o", o=1)), writes=["b8"])
        P.dma("sp", lambda e, l=l: e.dma_start(out=b8[:, l, 1:2], in_=w_fb[l].rearrange("(h o) -> h o", o=1)), writes=["b8"])
        P.dma("sp", lambda e, l=l: e.dma_start(out=sbb1[:, l, :], in_=w_sbb[l].rearrange("(o h) -> o h", o=1)), writes=["sbb1"])
    P.op("dve", lambda e: e.tensor_scalar(out=b8[:, :, 2:3], in0=b8[:, :, 1:2], scalar1=-1.0, scalar2=None, op0=ALU.mult),
         reads=["b8"], writes=["b8"])
    if _ckq(-2): return fin()
    P.op("dve", lambda e: e.tensor_scalar(out=sbt[:], in0=sbb1[:], scalar1=8.0, scalar2=None, op0=ALU.mult), reads=["sbb1"], writes=["sbt"])
    P.op("dve", lambda e: e.tensor_copy(out=sbhi[:], in_=sbt[:]), reads=["sbt"], writes=["sbhi"])
    P.op("dve", lambda e: e.tensor_tensor(out=sbt[:], in0=sbt[:], in1=sbhi[:], op=ALU.subtract), reads=["sbhi", "sbt"], writes=["sbt"])
    P.op("dve", lambda e: e.tensor_copy(out=sblo[:], in_=sbt[:]), reads=["sbt"], writes=["sblo"])
    for l in range(DEPTH):
        for (bt, L) in ((bias2[l], 128), (bias2s[l], 16)):
            v = bt[0:1, :].rearrange("p (h l) -> p h l", l=L)
            P.op("dve", lambda e, v=v, l=l, L=L: e.tensor_copy(out=v, in_=sbhi[0:1, l, :].unsqueeze(2).to_broadcast([1, 8, L])),
                 reads=["sbhi"], writes=[f"bias2_{l}_{L}"])
            tb, tk = blo_tmp, "blo_tmp"
            v2 = tb[0:1, 0:8 * L].rearrange("p (h l) -> p h l", l=L)
            P.op("dve", lambda e, v2=v2, l=l, L=L: e.tensor_copy(out=v2, in_=sblo[0:1, l, :].unsqueeze(2).to_broadcast([1, 8, L])),
                 reads=["sblo"], writes=[tk])
            P.dma("sp", lambda e, bt=bt, tb=tb, L=L: e.dma_start(out=bt[1:2, :], in_=tb[0:1, 0:8 * L]), reads=[tk], writes=[f"bias2_{l}_{L}"])
    if _ckq(-3): return fin()
    P.dma("sp", lambda e: e.dma_start(out=ptb[:].rearrange("p s j -> p (s j)"),
                                      in_=pt_in.rearrange("s j -> (s j)").partition_broadcast(128)), writes=["ptb"])
    P.op("dve", lambda e: e.tensor_copy(out=ptf[:], in_=ptb[:]), reads=["ptb"], writes=["ptf"])
    P.op("dve", lambda e: e.tensor_scalar(out=ptf[:], in0=ptf[:], scalar1=128.0, scalar2=iota_p[:, 0:1], op0=ALU.mult, op1=ALU.add),
         reads=["ptf", "iota_p"], writes=["ptf"])
    for l in range(DEPTH):
        P.op("dve", lambda e, l=l: e.tensor_copy(out=pidx[:, l], in_=ptf[:]), reads=["ptf"], writes=["pidx"])
        P.op("dve", lambda e: e.tensor_scalar(out=ptf[:], in0=ptf[:], scalar1=float(NPHYS * 128), scalar2=None, op0=ALU.add),
             reads=["ptf", "pidx"], writes=["ptf"])
    for l in range(DEPTH):
        P.op("pool", lambda e, l=l: e.memset(mv_ext[:, :, :, 128:132], 1.0), writes=["mv_ext"])

    if _ckq(-4): return fin()
    def wpiece(srcs):
        n = srcs[0].shape[1]
        kt = sum(s.shape[0] // 128 for s in srcs)
        pi = rot["piece"].next()
        pv = piece[pi][:, 0:kt * n].rearrange("p (k n) -> p k n", n=n)
        ko = 0
        for s in srcs:
            kk = s.shape[0] // 128
            P.dma("pool", lambda e, s=s, ko=ko, kk=kk: e.dma_start(out=pv[:, ko:ko + kk, :], in_=s.rearrange("(k p) n -> p k n", p=128)),
                  writes=[f"piece{pi}"])
            ko += kk
        return pv, f"piece{pi}"

    def mm_fm(pst, pkey, pv, pkey_w, k0, nk, m0, mw, act, akey, NT, first=True, last=True, outp=None):
        for k in range(nk):
            o = pst[0:mw, 0:NT] if outp is None else outp
            P.op("pe", lambda e, k=k, o=o: e.matmul(o, lhsT=pv[:, k0 + k, m0:m0 + mw], rhs=act[:, k, 0:NT],
                                                    start=(first and k == 0), stop=(last and k == nk - 1)),
                 reads=[pkey_w, akey], writes=[pkey])

    def rmsnorm(gi, NT, inplace=False):
        pst, pk = PS()
        for k in range(KC):
            tb, tk = TB()
            P.op("act", lambda e, k=k, tb=tb: e.activation(out=tb[:, 0:NT], in_=xT[:, k, 0:NT], func=AF.Square), reads=["xT"], writes=[tk])
            P.op("pe", lambda e, k=k, tb=tb: e.matmul(pst[:, 0:NT], lhsT=onesm[:], rhs=tb[:, 0:NT], start=(k == 0), stop=(k == KC - 1)),
                 reads=[tk, "onesm"], writes=[pk])
        tf, tfk = TF()
        P.op("act", lambda e: e.activation(out=tf[:, 0:NT], in_=pst[:, 0:NT], func=AF.Sqrt, bias=epsc[:, 0:1], scale=1.0), reads=[pk, "epsc"], writes=[tfk])
        P.op("dve", lambda e: e.reciprocal(out=tf[:, 0:NT], in_=tf[:, 0:NT]), reads=[tfk], writes=[tfk])
        for k in range(KC):
            o = xT[:, k, 0:NT] if inplace else xnT[:, k, 0:NT]
            P.op("dve", lambda e, k=k, o=o: e.scalar_tensor_tensor(out=o, in0=xT[:, k, 0:NT], scalar=gcol[:, gi, k:k + 1], in1=tf[:, 0:NT],
                                                                   op0=ALU.mult, op1=ALU.mult),
                 reads=["xT", "gcol", tfk], writes=["xT" if inplace else "xnT"])

    def ffn(l, w_up, w_dn, NT):
        f0 = 0
        for ng in (6, 6, 6, 4):
            for fl in range(ng):
                f = f0 + fl
                pv, pk = wpiece([w_up[l][:, f * 128:(f + 1) * 128], w_up[l][:, DFF + f * 128:DFF + (f + 1) * 128]])
                pg, pgk = PS()
                mm_fm(pg, pgk, pv, pk, 0, KC, 0, 128, xnT, "xnT", NT)
                pu, puk = PS()
                mm_fm(pu, puk, pv, pk, KC, KC, 0, 128, xnT, "xnT", NT)
                tf, tfk = TF()
                P.op("act", lambda e, tf=tf, pg=pg: e.activation(out=tf[:, 0:NT], in_=pg[:, 0:NT], func=AF.Silu), reads=[pgk], writes=[tfk])
                P.op("dve", lambda e, tf=tf, pu=pu, fl=fl: e.tensor_tensor(out=hT[:, fl, 0:NT], in0=tf[:, 0:NT], in1=pu[:, 0:NT], op=ALU.mult),
                     reads=[tfk, puk], writes=["hT"])
            for c in range(KC):
                pv, pk = wpiece([w_dn[l][f0 * 128:(f0 + ng) * 128, c * 128:(c + 1) * 128]])
                po, pok = PS()
                mm_fm(po, pok, pv, pk, 0, ng, 0, 128, hT, "hT", NT)
                P.op("dve", lambda e, c=c, po=po: e.scalar_tensor_tensor(out=xT[:, c, 0:NT], in0=po[:, 0:NT], scalar=0.5, in1=xT[:, c, 0:NT],
                                                                         op0=ALU.mult, op1=ALU.add),
                     reads=[pok, "xT"], writes=["xT"])
            f0 += ng

    def proj_fm(l, col0, nchunk, NT, evac):
        i = 0
        while i < nchunk:
            g = min(2, nchunk - i)
            pv, pk = wpiece([w_in[l][:, col0 + i * 128: col0 + (i + g) * 128]])
            for j in range(g):
                pst, pkey = PS()
                mm_fm(pst, pkey, pv, pk, 0, KC, j * 128, 128, xnT, "xnT", NT)
                evac(i + j, pst, pkey)
            i += g

    def proj_tm(l, col0, chunks, evac):
        for half in range(2):
            pv, pk = wpiece([w_in[l][:, col0 + half * 256: col0 + (half + 1) * 256]])
            for ci, (c0, L) in enumerate(chunks):
                pst, pkey = PS()
                for k in range(KC):
                    P.op("pe", lambda e, k=k, c0=c0, L=L, pst=pst, pv=pv: e.matmul(pst[0:L, 0:256], lhsT=xnT[:, k, c0:c0 + L], rhs=pv[:, k, :],
                                                                           start=(k == 0), stop=(k == KC - 1)),
                         reads=[pk, "xnT"], writes=[pkey])
                evac(ci, half, pst, pkey)

    def attention(l, c0, L, blocks, NT, pre=None):
        NP = 2 if L > 64 else 4
        NU = NP * 2 * L
        nunit = 4 // NP
        LB = 128 if L == 128 else 16
        b2 = bias2[l] if L == 128 else bias2s[l]
        b2k = f"bias2_{l}_{LB}"
        assert L <= 16 or L == 128
        P.op("pool", lambda e: e.memset(Qbd[:], 0.0), writes=["Qbd"])
        P.op("pool", lambda e: e.tensor_copy(out=Qbd[0:64, :, 0:L], in_=sqT[0:64, :, c0:c0 + L]), reads=["sqT"], writes=["Qbd"])
        P.op("pool", lambda e: e.tensor_copy(out=Qbd[64:128, :, L:2 * L], in_=sqT[64:128, :, c0:c0 + L]), reads=["sqT"], writes=["Qbd"])
        Tb = [ps[4], ps[5]]
        Ob = ps[6]
        P.op("pe", lambda e: e.matmul(Ob[:, 0:512], lhsT=zeros_b[:, 0:128], rhs=zeros_b[:, 0:512], start=True, stop=True),
             reads=["zeros_b"], writes=["ps6"])

        def unit(bi, blk, u):
            kt, v, Lk = blk["kt"], blk["v"], blk["Lk"]
            kkey, vkey, diag = blk["kkey"], blk["vkey"], blk["diag"]
            pu = u * NP
            S, Sk = PS()
            if L == LB:
                P.op("pe", lambda e: e.matmul(S[0:Lk, 0:NU], lhsT=ones2[0:2, 0:Lk], rhs=b2[0:2, pu * 2 * L: pu * 2 * L + NU], start=True, stop=False),
                     reads=[b2k, "ones2"], writes=[Sk])
            else:
                for hh in range(2 * NP):
                    P.op("pe", lambda e, hh=hh: e.matmul(S[0:Lk, hh * L:(hh + 1) * L], lhsT=ones2[0:2, 0:Lk],
                                                         rhs=b2[0:2, ((pu * 2 + hh) * LB):((pu * 2 + hh) * LB) + L], start=(hh == 0), stop=False),
                         reads=[b2k, "ones2"], writes=[Sk])
            for j in range(NP):
                P.op("pe", lambda e, j=j: e.matmul(S[0:Lk, j * 2 * L:(j + 1) * 2 * L], lhsT=kt[:, pu + j, 0:Lk], rhs=Qbd[:, pu + j, 0:2 * L],
                                                   start=False, stop=(j == NP - 1)),
                     reads=[kkey, "Qbd"], writes=[Sk])
            ei = rot["E"].next(); li_ = rot["L"].next(); xi = rot["X"].next(); wi = rot["W"].next()
            E, Lp, X, Wm = Et[ei], Lt[li_], Xt[xi], Wt[wi]
            P.op("act", lambda e: e.activation(out=E[0:Lk, 0:NU], in_=S[0:Lk, 0:NU], func=AF.Exp, scale=0.125), reads=[Sk], writes=[f"E{ei}"])
            P.op("act", lambda e: e.activation(out=Lp[0:Lk, 0:NU], in_=E[0:Lk, 0:NU], func=AF.Ln, bias=1.0, scale=1.0),
                 reads=[f"E{ei}"], writes=[f"L{li_}"])
            if diag:
                mk3 = mask_lt[0:Lk, 0:L].unsqueeze(1).to_broadcast([Lk, 2 * NP, L])
                P.op("pool", lambda e: e.tensor_tensor(out=Lp[0:Lk, 0:NU].rearrange("p (h l) -> p h l", l=L),
                                                       in0=Lp[0:Lk, 0:NU].rearrange("p (h l) -> p h l", l=L), in1=mk3, op=ALU.mult),
                     reads=[f"L{li_}", "mask_lt"], writes=[f"L{li_}"])
                P.op("pool", lambda e: e.tensor_tensor(out=E[0:Lk, 0:NU].rearrange("p (h l) -> p h l", l=L),
                                                       in0=E[0:Lk, 0:NU].rearrange("p (h l) -> p h l", l=L), in1=mk3, op=ALU.mult),
                     reads=[f"E{ei}", "mask_lt"], writes=[f"E{ei}"])
            T = Tb[u]
            Tk = f"ps{4 + u}"
            P.op("pe", lambda e: e.matmul(T[:, 0:NU], lhsT=UI[0:Lk, :], rhs=Lp[0:Lk, 0:NU], start=(bi == 0), stop=True, skip_group_check=True),
                 reads=[f"L{li_}", "UI"], writes=[Tk])
            P.op("act", lambda e: e.activation(out=X[0:Lk, 0:NU], in_=T[0:Lk, 0:NU], func=AF.Exp, scale=-1.0), reads=[Tk], writes=[f"X{xi}"])
            P.op("pe", lambda e: e.matmul(T[:, 0:NU], lhsT=CM[0:Lk, :], rhs=Lp[0:Lk, 0:NU], start=False, stop=True, skip_group_check=True),
                 reads=[f"L{li_}", "CM"], writes=[Tk])
            P.op("dve", lambda e: e.tensor_tensor(out=Wm[0:Lk, 0:NU], in0=E[0:Lk, 0:NU], in1=X[0:Lk, 0:NU], op=ALU.mult),
                 reads=[f"E{ei}", f"X{xi}"], writes=[f"W{wi}"])
            for j in range(NP):
                for a in range(2):
                    h = 2 * (pu + j) + a
                    P.op("pe", lambda e, j=j, a=a, h=h: e.matmul(Ob[a * 64:(a + 1) * 64, (pu + j) * L:(pu + j + 1) * L],
                                                                 lhsT=v[0:Lk, h * 64:(h + 1) * 64], rhs=Wm[0:Lk, (2 * j + a) * L:(2 * j + a + 1) * L],
                                                                 start=False, stop=True, skip_group_check=True),
                         reads=[f"W{wi}", vkey], writes=["ps6"])

        for bi, blk in enumerate(blocks):
            if pre is not None:
                blk = pre(blk)
            for u in range(nunit):
                unit(bi, blk, u)
        P.op("act", lambda e: e.activation(out=ybT[0][:, :, c0:c0 + L], in_=Ob[:, 0:4 * L].rearrange("p (k l) -> p k l", l=L), func=AF.Identity),
             reads=["ps6"], writes=["ybT0"])

    def kv_pre(l, b):
        def pre(blk):
            if "cache" in blk:
                ki = rot["ktb"].next(); vi = rot["vb"].next()
                bk = blk["cache"]
                P.dma("sp", lambda e: e.dma_start(out=ktb[ki][:], in_=ktc[l, bk]), reads=[f"kvc{l}"], writes=[f"ktb{ki}"])
                P.dma("sp", lambda e: e.dma_start(out=vb[vi][:], in_=vc[l, bk]), reads=[f"kvc{l}"], writes=[f"vb{vi}"])
                return dict(blk, kt=ktb[ki][:], v=vb[vi][:], kkey=f"ktb{ki}", vkey=f"vb{vi}")
            if "page" in blk:
                ki = rot["ktb"].next(); vi = rot["vb"].next(); gi = rot["pg"].next()
                pg = blk["page"]
                P.dma("pool", lambda e: e.indirect_dma_start(out=kpg[gi][:], out_offset=None, in_=ck,
                                                             in_offset=bass.IndirectOffsetOnAxis(ap=pidx[:, l, b, pg:pg + 1], axis=0)),
                      reads=["pidx"], writes=[f"kpg{gi}"])
                P.dma("pool", lambda e: e.indirect_dma_start(out=vpg[gi][:], out_offset=None, in_=cv,
                                                             in_offset=bass.IndirectOffsetOnAxis(ap=pidx[:, l, b, pg:pg + 1], axis=0)),
                      reads=["pidx"], writes=[f"vpg{gi}"])
                pst, pk = PS()
                for p in range(4):
                    P.op("pe", lambda e, p=p: e.transpose(out=pst[:, p * 128:(p + 1) * 128], in_=kpg[gi][:, p * 128:(p + 1) * 128], identity=ident[:]),
                         reads=[f"kpg{gi}", "ident"], writes=[pk])
                P.op("dve", lambda e: e.tensor_copy(out=ktb[ki][:], in_=pst[:, :].rearrange("p (k n) -> p k n", n=128)), reads=[pk], writes=[f"ktb{ki}"])
                P.op("pool", lambda e: e.tensor_copy(out=vb[vi][:], in_=vpg[gi][:]), reads=[f"vpg{gi}"], writes=[f"vb{vi}"])
                return dict(blk, kt=ktb[ki][:], v=vb[vi][:], kkey=f"ktb{ki}", vkey=f"vb{vi}")
            return blk
        return pre

    def mlstm_chunk(l, si, c0, L, ci, first):
        st = f"mst{si}"
        if first:
            P.op("dve", lambda e: e.tensor_scalar(out=rp8[:], in0=m8[si][:], scalar1=-1.0, scalar2=None, op0=ALU.mult), reads=[st + "m"], writes=["rp8"])
        else:
            P.op("dve", lambda e: e.tensor_copy(out=rp8[:], in_=bT[:, c0 - 1:c0]), reads=["bT"], writes=["rp8"])
        P.op("pe", lambda e: e.transpose(out=ps[7][0:L, 0:8], in_=liT[0:8, c0:c0 + L], identity=ident[0:8, 0:8]), reads=["liT", "ident"], writes=["ps7"])
        P.op("dve", lambda e: e.tensor_copy(out=acol[0:L, :], in_=ps[7][0:L, 0:8]), reads=["ps7"], writes=["acol"])
        for hq in range(2):
            pst, pk = PS()
            for hh in range(4):
                h = hq * 4 + hh
                P.op("pe", lambda e, pst=pst, hh=hh, h=h: e.matmul(pst[:, hh * L:(hh + 1) * L], lhsT=selH[:, h, :], rhs=bT[0:8, c0:c0 + L],
                                                                   start=(hh == 0), stop=(hh == 3)), reads=["selH", "bT"], writes=[pk])
            P.op("act", lambda e, pst=pst, hq=hq: e.activation(out=r_bc[:, hq * 4:(hq + 1) * 4, 0:L],
                                                                in_=pst[:, 0:4 * L].rearrange("p (h l) -> p h l", l=L), func=AF.Identity),
                 reads=[pk], writes=["r_bc"])
        for (src, skey, dst, dkey) in ((bT, "bT", rp_bc, "rp_bc"), (mT, "mT", nm_bc, "nm_bc")):
            pst, pk = PS()
            for p in range(4):
                P.op("pe", lambda e, pst=pst, p=p, src=src: e.matmul(pst[:, p * L:(p + 1) * L], lhsT=selP[:, p, :], rhs=src[0:8, c0:c0 + L],
                                                                     start=(p == 0), stop=(p == 3)), reads=["selP", skey], writes=[pk])
            P.op("dve", lambda e, pst=pst, dst=dst: e.tensor_copy(out=dst[:, :, 0:L], in_=pst[:, 0:4 * L].rearrange("p (h l) -> p h l", l=L)),
                 reads=[pk], writes=[dkey])
        pst, pk = PS()
        for p in range(4):
            P.op("pe", lambda e, pst=pst, p=p: e.matmul(pst[:, p:p + 1], lhsT=selP[:, p, :], rhs=rp8[0:8, 0:1], start=(p == 0), stop=(p == 3)),
                 reads=["selP", "rp8"], writes=[pk])
        P.op("dve", lambda e, pst=pst: e.tensor_scalar(out=nrprev[:], in0=pst[:, 0:4], scalar1=-1.0, scalar2=None, op0=ALU.mult),
             reads=[pk], writes=["nrprev"])
        P.op("dve", lambda e: e.tensor_tensor(out=rend[0:L, :], in0=acol[0:L, :], in1=r_bc[0:L, :, L - 1], op=ALU.add),
             reads=["acol", "r_bc"], writes=["rend"])
        P.op("act", lambda e: e.activation(out=wend[0:L, :], in_=rend[0:L, :], func=AF.Exp), reads=["rend"], writes=["wend"])
        P.op("dve", lambda e: e.tensor_tensor(out=kw[0:L, :].rearrange("p (h d) -> p h d", d=64),
                                              in0=mk_tok[0:L, ci, :].rearrange("p (h d) -> p h d", d=64),
                                              in1=wend[0:L, :].unsqueeze(2).to_broadcast([L, 8, 64]), op=ALU.mult),
             reads=["mk_tok", "wend"], writes=["kw"])
        for p in range(4):
            Sa = []
            for a in range(2):
                po = a * 64
                S, Sk = PS()
                Sa.append((S, Sk))
                P.op("pe", lambda e, S=S, a=a, po=po, p=p: e.matmul(S[0:L, 0:L], lhsT=mkT[po:po + 64, p, c0:c0 + L],
                                                                    rhs=mqT[po:po + 64, p, c0:c0 + L], start=True, stop=True),
                     reads=["mkT", "mqT"], writes=[Sk])
                P.op("act", lambda e, a=a, p=p: e.activation(out=Dt[0:L, a, 0:L], in_=r_bc[0:L, 2 * p + a, 0:L], func=AF.Exp,
                                                              bias=acol[0:L, 2 * p + a:2 * p + a + 1], scale=1.0),
                     reads=["r_bc", "acol"], writes=["Dt"])
            mk3 = mask_le[0:L, 0:L].unsqueeze(1).to_broadcast([L, 2, L])
            P.op("pool", lambda e, mk3=mk3: e.tensor_tensor(out=Dt[0:L, :, 0:L], in0=Dt[0:L, :, 0:L], in1=mk3, op=ALU.mult),
                 reads=["Dt", "mask_le"], writes=["Dt"])
            for a in range(2):
                S, Sk = Sa[a]
                P.op("dve", lambda e, S=S, a=a: e.tensor_tensor(out=SWt[0:L, a, 0:L], in0=Dt[0:L, a, 0:L], in1=S[0:L, 0:L], op=ALU.mult),
                     reads=["Dt", Sk], writes=["SWt"])
            wt, wtk = TF()
            P.op("act", lambda e, wt=wt, p=p: e.activation(out=wt[:, 0:L], in_=rp_bc[:, p, 0:L], func=AF.Exp, bias=nrprev[:, p:p + 1], scale=1.0),
                 reads=["rp_bc", "nrprev"], writes=[wtk])
            qw, qwk = TB()
            P.op("dve", lambda e, wt=wt, qw=qw, p=p: e.tensor_tensor(out=qw[:, 0:L], in0=mqT[:, p, c0:c0 + L], in1=wt[:, 0:L], op=ALU.mult),
                 reads=[wtk, "mqT"], writes=[qwk])
            N_, Nk = PS()
            for a in range(2):
                P.op("pe", lambda e, N_=N_, a=a, p=p: e.matmul(N_[a * 64:(a + 1) * 64, 0:L], lhsT=mv_ext[0:L, ci, p, a * 64:(a + 1) * 64],
                                                               rhs=SWt[0:L, a, 0:L], start=True, stop=False, skip_group_check=True),
                     reads=["mv_ext", "SWt"], writes=[Nk])
            P.op("pe", lambda e, N_=N_, qw=qw, p=p: e.matmul(N_[:, 0:L], lhsT=Cb[si][:, p, :], rhs=qw[:, 0:L], start=False, stop=True,
                                                             skip_group_check=True), reads=[st + "Cb", qwk], writes=[Nk])
            Dn, Dk = PS()
            for a in range(2):
                P.op("pe", lambda e, Dn=Dn, a=a: e.matmul(Dn[a * 64:(a + 1) * 64, 0:L], lhsT=ones_b[0:L, 0:64], rhs=SWt[0:L, a, 0:L],
                                                          start=True, stop=False, skip_group_check=True), reads=["ones_b", "SWt"], writes=[Dk])
            P.op("pe", lambda e, Dn=Dn, qw=qw, p=p: e.matmul(Dn[:, 0:L], lhsT=Npad[si][:, p, :], rhs=qw[:, 0:L], start=False, stop=True,
                                                             skip_group_check=True), reads=[st + "Np", qwk], writes=[Dk])
            en, enk = TF()
            P.op("act", lambda e, en=en, p=p: e.activation(out=en[:, 0:L], in_=nm_bc[:, p, 0:L], func=AF.Exp), reads=["nm_bc"], writes=[enk])
            ad, adk = TF()
            P.op("act", lambda e, ad=ad, Dn=Dn: e.activation(out=ad[:, 0:L], in_=Dn[:, 0:L], func=AF.Abs), reads=[Dk], writes=[adk])
            P.op("dve", lambda e, en=en, ad=ad: e.tensor_tensor(out=en[:, 0:L], in0=ad[:, 0:L], in1=en[:, 0:L], op=ALU.max),
                 reads=[adk, enk], writes=[enk])
            P.op("dve", lambda e, en=en: e.reciprocal(out=en[:, 0:L], in_=en[:, 0:L]), reads=[enk], writes=[enk])
            hr, hrk = TF()
            P.op("dve", lambda e, en=en, hr=hr, N_=N_: e.tensor_tensor(out=hr[:, 0:L], in0=N_[:, 0:L], in1=en[:, 0:L], op=ALU.mult),
                 reads=[Nk, enk], writes=[hrk])
            hs, hsk = TB()
            P.op("act", lambda e, hr=hr, hs=hs: e.activation(out=hs[:, 0:L], in_=hr[:, 0:L], func=AF.Square), reads=[hrk], writes=[hsk])
            M_, Mk = PS()
            P.op("pe", lambda e, M_=M_, hs=hs: e.matmul(M_[:, 0:L], lhsT=blk64[:], rhs=hs[:, 0:L], start=True, stop=True), reads=["blk64", hsk], writes=[Mk])
            rs, rsk = TF()
            P.op("act", lambda e, rs=rs, M_=M_: e.activation(out=rs[:, 0:L], in_=M_[:, 0:L], func=AF.Sqrt, bias=epsc[:, 0:1], scale=1.0), reads=[Mk, "epsc"], writes=[rsk])
            P.op("dve", lambda e, rs=rs: e.reciprocal(out=rs[:, 0:L], in_=rs[:, 0:L]), reads=[rsk], writes=[rsk])
            P.op("dve", lambda e, rs=rs, hr=hr: e.tensor_tensor(out=hr[:, 0:L], in0=hr[:, 0:L], in1=rs[:, 0:L], op=ALU.mult), reads=[hrk, rsk], writes=[hrk])
            P.op("dve", lambda e, hr=hr, p=p: e.scalar_tensor_tensor(out=ybT[1][:, p, c0:c0 + L], in0=hr[:, 0:L], scalar=hgcol[:, l, p:p + 1],
                                                                      in1=moS[:, p, c0:c0 + L], op0=ALU.mult, op1=ALU.mult),
                 reads=[hrk, "hgcol", "moS"], writes=["ybT1"])
            Cn, Ck = PS()
            P.op("pe", lambda e, Cn=Cn, p=p: e.matmul(Cn[:, 0:129], lhsT=kw[0:L, p * 128:(p + 1) * 128], rhs=mv_ext[0:L, ci, p, 0:129], start=True, stop=True),
                 reads=["kw", "mv_ext"], writes=[Ck])
            P.op("act", lambda e, p=p: e.activation(out=decay[:, p:p + 1], in_=rp_bc[:, p, L - 1:L], func=AF.Exp, bias=nrprev[:, p:p + 1], scale=1.0),
                 reads=["rp_bc", "nrprev"], writes=["decay"])
            tc_, tck = TF()
            P.op("dve", lambda e, tc_=tc_, Cn=Cn: e.tensor_tensor(out=tc_[:, 0:128], in0=Cn[:, 0:128], in1=BDf[:], op=ALU.mult), reads=[Ck, "BDf"], writes=[tck])
            P.op("dve", lambda e, tc_=tc_, p=p: e.scalar_tensor_tensor(out=Cpad[si][:, p, :], in0=Cpad[si][:, p, :], scalar=decay[:, p:p + 1],
                                                                       in1=tc_[:, 0:128], op0=ALU.mult, op1=ALU.add),
                 reads=[tck, "decay", st + "C"], writes=[st + "C"])
            P.op("act", lambda e, p=p: e.activation(out=Cb[si][:, p, :], in_=Cpad[si][:, p, :], func=AF.Identity), reads=[st + "C"], writes=[st + "Cb"])
            P.op("dve", lambda e, Cn=Cn, p=p: e.scalar_tensor_tensor(out=ncol[si][:, p:p + 1], in0=ncol[si][:, p:p + 1], scalar=decay[:, p:p + 1],
                                                                     in1=Cn[:, 128:129], op0=ALU.mult, op1=ALU.add),
                 reads=[Ck, "decay", st + "n"], writes=[st + "n"])
            P.op("dve", lambda e, p=p: e.tensor_scalar(out=Npad[si][:, p, :], in0=BDf[:], scalar1=ncol[si][:, p:p + 1], scalar2=None, op0=ALU.mult),
                 reads=[st + "n", "BDf"], writes=[st + "Np"])

    def mlstm_state_init(si, zero, l=None, b=None):
        st = f"mst{si}"
        P.op("pool", lambda e: e.memset(Cpad[si][:], 0.0), writes=[st + "C"])
        P.op("pool", lambda e: e.memset(ncol[si][:], 0.0), writes=[st + "n"])
        if zero:
            P.op("pool", lambda e: e.memset(m8[si][:], 0.0), writes=[st + "m"])
        else:
            for h in range(8):
                po = (h % 2) * 64
                P.dma("sp", lambda e, h=h, po=po: e.dma_start(out=Cpad[si][po:po + 64, h // 2, po:po + 64], in_=sC[l, b, h]), writes=[st + "C"])
                P.dma("sp", lambda e, h=h, po=po: e.dma_start(out=ncol[si][po:po + 64, h // 2:h // 2 + 1], in_=sn[l, b, h].rearrange("(d o) -> d o", o=1)),
                      writes=[st + "n"])
            P.dma("sp", lambda e: e.dma_start(out=m8[si][:], in_=sm[l, b].rearrange("(h o) -> h o", o=1)), writes=[st + "m"])
        P.op("act", lambda e: e.activation(out=Cb[si][:], in_=Cpad[si][:], func=AF.Identity), reads=[st + "C"], writes=[st + "Cb"])
        for p in range(4):
            P.op("dve", lambda e, p=p: e.tensor_scalar(out=Npad[si][:, p, :], in0=BDf[:], scalar1=ncol[si][:, p:p + 1], scalar2=None, op0=ALU.mult),
                 reads=[st + "n", "BDf"], writes=[st + "Np"])

    def mlstm_state_store(si, oC, on, om):
        st = f"mst{si}"
        for h in range(8):
            po = (h % 2) * 64
            P.dma("sp", lambda e, h=h, po=po: e.dma_start(out=oC[h], in_=Cpad[si][po:po + 64, h // 2, po:po + 64]), reads=[st + "C"])
            P.dma("sp", lambda e, h=h, po=po: e.dma_start(out=on[h].rearrange("(d o) -> d o", o=1), in_=ncol[si][po:po + 64, h // 2:h // 2 + 1]),
                  reads=[st + "n"])
        P.dma("sp", lambda e: e.dma_start(out=om.rearrange("(h o) -> h o", o=1), in_=m8[si][:]), reads=[st + "m"])

    def run_st(kind, t0, NT, chunks, segs):
        xsrc = xp if kind == "p" else xs
        ydst = yp if kind == "p" else ys
        ntile = (NT + 127) // 128
        for t in range(ntile):
            pt_ = min(128, NT - t * 128)
            xi = rot["xin"].next()
            P.dma("sp", lambda e, t=t, pt_=pt_, xi=xi: e.dma_start(out=xin[xi][0:pt_, :], in_=xsrc[t0 + t * 128:t0 + t * 128 + pt_, :]), writes=[f"xin{xi}"])
            DBG = ""
            for g in range(2):
                if DBG == "a":
                    break
                pst, pk = PS()
                for kk in range(4):
                    k = g * 4 + kk
                    P.op("pe", lambda e, k=k, kk=kk, pst=pst, xi=xi, pt_=pt_: e.transpose(out=pst[:, kk * 128:kk * 128 + pt_], in_=xin[xi][0:pt_, k * 128:(k + 1) * 128],
                                                                                             identity=ident[0:pt_, 0:pt_]),
                         reads=[f"xin{xi}", "ident"], writes=[pk])
                if DBG == "b":
                    continue
                if DBG == "c":
                    P.op("dve", lambda e, g=g, pst=pst, t=t, pt_=pt_: e.tensor_copy(out=xT[:, g * 4:(g + 1) * 4, t * 128:t * 128 + pt_],
                                                                                    in_=pst[:, :].rearrange("p (k n) -> p k n", n=128)[:, :, 0:pt_]),
                         reads=[pk], writes=["xT"])
                    continue
                if DBG == "d":
                    P.op("act", lambda e, g=g, pst=pst, t=t, pt_=pt_: e.activation(out=xT[:, g * 4:(g + 1) * 4, t * 128:t * 128 + pt_],
                                                                                    in_=pst[:, :].rearrange("p (k n) -> p k n", n=128)[:, :, 0:pt_], func=AF.Identity),
                         reads=[pk], writes=["xT"])
                    continue
                if DBG == "e":
                    for kk in range(4):
                        P.op("act", lambda e, g=g, kk=kk, pst=pst, t=t, pt_=pt_: e.activation(out=xT[:, g * 4 + kk, t * 128:t * 128 + pt_],
                                                                                        in_=pst[:, kk * 128:kk * 128 + pt_], func=AF.Identity),
                             reads=[pk], writes=["xT"])
                    continue
                P.op("act", lambda e, g=g, pst=pst, t=t, pt_=pt_: e.activation(out=xT[:, g * 4:(g + 1) * 4, t * 128:t * 128 + pt_],
                                                                                in_=pst[:, :].rearrange("p (k n) -> p k n", n=128)[:, :, 0:pt_], func=AF.Identity),
                     reads=[pk], writes=["xT"])
        _ck(1)
        for l in range(DEPTH):
            run_layer(kind, t0, NT, chunks, segs, l)
        final_store(kind, t0, NT, ntile, ydst)

    def run_layer(kind, t0, NT, chunks, segs, l):
        if True:
            rmsnorm(3 * l, NT)
            ffn(l, w_up1, w_dn1, NT)
            _ck(2)
            rmsnorm(3 * l + 1, NT)
            def ev_bf(dst, dkey, scale=None):
                def f(i, pst, pkey):
                    if scale is None:
                        P.op("act", lambda e: e.activation(out=dst[:, i, 0:NT], in_=pst[:, 0:NT], func=AF.Identity), reads=[pkey], writes=[dkey])
                    else:
                        P.op("act", lambda e: e.activation(out=dst[:, i, 0:NT], in_=pst[:, 0:NT], func=AF.Identity, scale=scale), reads=[pkey], writes=[dkey])
                return f
            proj_fm(l, OFF["sq"], 4, NT, ev_bf(sqT, "sqT"))
            proj_fm(l, OFF["sk"], 4, NT, ev_bf(skT, "skT"))
            proj_fm(l, OFF["mq"], 4, NT, ev_bf(mqT, "mqT"))
            proj_fm(l, OFF["mk"], 4, NT, ev_bf(mkT, "mkT", 0.125))
            _ck(21)
            proj_fm(l, OFF["mo"], 4, NT, lambda i, pst, pkey: P.op("act", lambda e: e.activation(out=moS[:, i, 0:NT], in_=pst[:, 0:NT], func=AF.Sigmoid),
                                                                   reads=[pkey], writes=["moS"]))
            proj_fm(l, OFF["cb"], 4, NT, lambda i, pst, pkey: P.op("act", lambda e: e.activation(out=cbT[:, i, 0:NT], in_=pst[:, 0:NT], func=AF.Identity),
                                                                   reads=[pkey], writes=["cbT"]))
            _ck(22)
            for i in range(4):
                pv, pk = wpiece([w_in[l][:, OFF["cc"] + i * 128:OFF["cc"] + (i + 1) * 128], w_in[l][:, OFF["ch"] + i * 128:OFF["ch"] + (i + 1) * 128]])
                pc, pck = PS()
                mm_fm(pc, pck, pv, pk, 0, KC, 0, 128, xnT, "xnT", NT)
                ph, phk = PS()
                mm_fm(ph, phk, pv, pk, KC, KC, 0, 128, xnT, "xnT", NT)
                tf, tfk = TF()
                P.op("act", lambda e, tf=tf, pc=pc: e.activation(out=tf[:, 0:NT], in_=pc[:, 0:NT], func=AF.Identity), reads=[pck], writes=[tfk])
                if kind == "p":
                    P.op("dve", lambda e, tf=tf, ph=ph, i=i: e.tensor_tensor(out=uext_p[l][:, i, 2:2 + NT], in0=tf[:, 0:NT], in1=ph[:, 0:NT], op=ALU.mult),
                         reads=[tfk, phk], writes=["uext_1"])
                else:
                    P.op("dve", lambda e, tf=tf, ph=ph: e.tensor_tensor(out=tf[:, 0:NT], in0=tf[:, 0:NT], in1=ph[:, 0:NT], op=ALU.mult),
                         reads=[tfk, phk], writes=[tfk])
                    P.op("dve", lambda e, tf=tf, i=i: e.tensor_copy(out=cbT[:, i, NT:2 * NT], in_=tf[:, 0:NT]), reads=[tfk], writes=["cbT"])
            _ck(23)
            pv, pk = wpiece([w_in[l][:, OFF["mi"]:OFF["mi"] + 16]])
            pi_, pik = PS()
            mm_fm(pi_, pik, pv, pk, 0, KC, 0, 8, xnT, "xnT", NT)
            pf_, pfk = PS()
            mm_fm(pf_, pfk, pv, pk, 0, KC, 8, 8, xnT, "xnT", NT)
            P.op("dve", lambda e, pi_=pi_: e.tensor_scalar(out=liT[:, 0:NT], in0=pi_[0:8, 0:NT], scalar1=b8[:, l, 0:1], scalar2=None, op0=ALU.add),
                 reads=[pik, "b8"], writes=["liT"])
            P.op("act", lambda e, pf_=pf_: e.activation(out=lfT[:, 0:NT], in_=pf_[0:8, 0:NT], func=AF.Exp, bias=b8[:, l, 2:3], scale=-1.0),
                 reads=[pfk, "b8"], writes=["lfT"])
            P.op("act", lambda e: e.activation(out=lfT[:, 0:NT], in_=lfT[:, 0:NT], func=AF.Ln, bias=1.0, scale=1.0), reads=["lfT"], writes=["lfT"])
            P.op("dve", lambda e: e.tensor_scalar(out=lfT[:, 0:NT], in0=lfT[:, 0:NT], scalar1=-1.0, scalar2=None, op0=ALU.mult), reads=["lfT"], writes=["lfT"])
            _ck(24)
            kout = kp if kind == "p" else ks
            vout = vp if kind == "p" else vs

            def ev_out(dst):
                def f(ci, half, pst, pkey):
                    c0, L = chunks[ci]
                    tf, tfk = TF()
                    P.op("act", lambda e: e.activation(out=tf[0:L, 0:256], in_=pst[0:L, 0:256], func=AF.Identity), reads=[pkey], writes=[tfk])
                    P.dma("sp", lambda e: e.dma_start(out=dst[l, t0 + c0:t0 + c0 + L, half * 256:(half + 1) * 256], in_=tf[0:L, 0:256]), reads=[tfk])
                return f
            proj_tm(l, OFF["sk"], chunks, ev_out(kout))

            def ev_v(ci, half, pst, pkey):
                c0, L = chunks[ci]
                ev_out(vout)(ci, half, pst, pkey)
                P.op("dve", lambda e: e.tensor_copy(out=sv_tok[0:L, ci, half * 256:(half + 1) * 256], in_=pst[0:L, 0:256]), reads=[pkey], writes=["sv_tok"])
            _ck(25)
            proj_tm(l, OFF["sv"], chunks, ev_v)
            _ck(26)
            proj_tm(l, OFF["mk"], chunks, lambda ci, half, pst, pkey: P.op(
                "act", lambda e: e.activation(out=mk_tok[0:chunks[ci][1], ci, half * 256:(half + 1) * 256], in_=pst[0:chunks[ci][1], 0:256], func=AF.Identity, scale=0.125),
                reads=[pkey], writes=["mk_tok"]))
            proj_tm(l, OFF["mv"], chunks, lambda ci, half, pst, pkey: P.op(
                "dve", lambda e: e.tensor_copy(out=mv_ext[0:chunks[ci][1], ci, half * 2:(half + 1) * 2, 0:128],
                                               in_=pst[0:chunks[ci][1], 0:256].rearrange("p (a n) -> p a n", n=128)),
                reads=[pkey], writes=["mv_ext"]))
            _ck(3)
            if kind == "p":
                for ci, (c0, L) in enumerate(chunks):
                    blocks = [dict(kt=skT[:, :, c0:c0 + L], v=sv_tok[:, ci, :], Lk=L, diag=True, kkey="skT", vkey="sv_tok")]
                    for cj in range(ci - 1, -1, -1):
                        cc0, LL = chunks[cj]
                        blocks.append(dict(kt=skT[:, :, cc0:cc0 + LL], v=sv_tok[:, cj, :], Lk=LL, diag=False, kkey="skT", vkey="sv_tok"))
                    for bk in range(t0 // 128 - 1, -1, -1):
                        blocks.append(dict(cache=bk, Lk=128, diag=False))
                    attention(l, c0, L, blocks, NT, kv_pre(l, None))
                for ci, (c0, L) in enumerate(chunks):
                    if L == 128:
                        bk = (t0 + c0) // 128
                        P.dma("sp", lambda e, bk=bk, c0=c0: e.dma_start(out=ktc[l, bk], in_=skT[:, :, c0:c0 + 128]), reads=["skT"], writes=[f"kvc{l}"])
                        P.dma("sp", lambda e, bk=bk, ci=ci: e.dma_start(out=vc[l, bk], in_=sv_tok[:, ci, :]), reads=["sv_tok"], writes=[f"kvc{l}"])
            else:
                for (s0, Ls, cis, b) in segs:
                    blocks = [dict(kt=skT[:, :, s0:s0 + Ls], v=sv_tok[:, cis[0], :], Lk=Ls, diag=True, kkey="skT", vkey="sv_tok")]
                    for pg in range(NPG - 1, -1, -1):
                        blocks.append(dict(page=pg, Lk=128, diag=False))
                    attention(l, s0, Ls, blocks, NT, kv_pre(l, b))
            _ck(4)
            for (s0, Ls, cis, b) in segs:
                si = l if kind == "p" else DEPTH
                st = f"mst{si}"
                if kind == "s":
                    mlstm_state_init(si, False, l, b)
                elif t0 == 0:
                    mlstm_state_init(si, True)
                P.op("dve", lambda e, s0=s0, Ls=Ls, si=si: e.tensor_tensor_scan(out=mT[:, s0:s0 + Ls], data0=lfT[:, s0:s0 + Ls], data1=liT[:, s0:s0 + Ls],
                                                                               initial=m8[si][:, 0:1], op0=ALU.add, op1=ALU.max),
                     reads=["lfT", "liT", st + "m"], writes=["mT"])
                P.op("dve", lambda e, s0=s0, Ls=Ls: e.tensor_tensor_scan(out=bT[:, s0:s0 + Ls], data0=lfT[:, s0:s0 + Ls], data1=zeros8[:, s0:s0 + Ls],
                                                                        initial=0.0, op0=ALU.add, op1=ALU.add),
                     reads=["lfT", "zeros8"], writes=["bT"])
                P.op("dve", lambda e, s0=s0, Ls=Ls: e.tensor_tensor(out=liT[:, s0:s0 + Ls], in0=liT[:, s0:s0 + Ls], in1=bT[:, s0:s0 + Ls], op=ALU.subtract),
                     reads=["liT", "bT"], writes=["liT"])
                P.op("dve", lambda e, s0=s0, Ls=Ls: e.tensor_tensor(out=bT[:, s0:s0 + Ls], in0=bT[:, s0:s0 + Ls], in1=mT[:, s0:s0 + Ls], op=ALU.subtract),
                     reads=["mT", "bT"], writes=["bT"])
                P.op("dve", lambda e, s0=s0, Ls=Ls: e.tensor_scalar(out=mT[:, s0:s0 + Ls], in0=mT[:, s0:s0 + Ls], scalar1=-1.0, scalar2=None, op0=ALU.mult),
                     reads=["mT"], writes=["mT"])
                first = True
                for ci in cis:
                    c0, L = chunks[ci]
                    mlstm_chunk(l, si, c0, L, ci, first)
                    first = False
                P.op("dve", lambda e, s0=s0, Ls=Ls, si=si: e.tensor_scalar(out=m8[si][:], in0=mT[:, s0 + Ls - 1:s0 + Ls], scalar1=-1.0, scalar2=None, op0=ALU.mult),
                     reads=["mT"], writes=[st + "m"])
                if kind == "s":
                    mlstm_state_store(si, Cs[l, b], ns_[l, b], ms_[l, b])
            _ck(5)
            for (s0, Ls, cis, b) in segs:
                if kind == "p":
                    ue = uext_1; uk = "uext_1"
                    if t0 == 0:
                        P.op("pool", lambda e, ue=ue: e.memset(ue[:, :, 0:2], 0.0), writes=[uk])
                    else:
                        P.op("pool", lambda e, ue=ue: e.tensor_copy(out=ue[:, :, 0:2], in_=halo[l][:]), reads=[f"halo{l}"], writes=[uk])
                else:
                    ue = uext_s; uk = "uext_s"
                    for tt in range(2):
                        P.dma("sp", lambda e, b=b, tt=tt: e.dma_start(out=ue[:, :, tt], in_=scv[l, b, tt].rearrange("(k p) -> p k", p=128), allow_slow_non_contiguous=True),
                              writes=[uk])
                    P.op("pool", lambda e, s0=s0, Ls=Ls: e.tensor_copy(out=ue[:, :, 2:2 + Ls], in_=cbT[:, :, NT + s0:NT + s0 + Ls]), reads=["cbT"], writes=[uk])
                for k in range(4):
                    tf, tfk = TF()
                    P.op("pool", lambda e, tf=tf, k=k, ue=ue, Ls=Ls: e.tensor_scalar(out=tf[:, 0:Ls], in0=ue[:, k, 0:Ls], scalar1=cwcol[:, l, k, 0:1], scalar2=None, op0=ALU.mult),
                         reads=[uk, "cwcol"], writes=[tfk])
                    for j in (1, 2):
                        P.op("dve", lambda e, tf=tf, k=k, j=j, ue=ue, Ls=Ls: e.scalar_tensor_tensor(out=tf[:, 0:Ls], in0=ue[:, k, j:j + Ls], scalar=cwcol[:, l, k, j:j + 1],
                                                                                                    in1=tf[:, 0:Ls], op0=ALU.mult, op1=ALU.add),
                             reads=[uk, "cwcol", tfk], writes=[tfk])
                    P.op("pool", lambda e, tf=tf, k=k, s0=s0, Ls=Ls: e.tensor_tensor(out=ybT[2][:, k, s0:s0 + Ls], in0=tf[:, 0:Ls], in1=cbT[:, k, s0:s0 + Ls], op=ALU.mult),
                         reads=[tfk, "cbT"], writes=["ybT2"])
                last_p = (kind == "p" and t0 + NT == TP)
                if kind == "s" or last_p:
                    dst = cvs[l, b] if kind == "s" else cvp[l]
                    for tt in range(2):
                        P.dma("sp", lambda e, dst=dst, ue=ue, Ls=Ls, tt=tt: e.dma_start(out=dst[tt].rearrange("(k p) -> p k", p=128), in_=ue[:, :, Ls + tt],
                                                                                       allow_slow_non_contiguous=True), reads=[uk])
                if kind == "p" and not last_p:
                    P.op("pool", lambda e, ue=ue, Ls=Ls: e.tensor_copy(out=halo[l][:], in_=ue[:, :, Ls:Ls + 2]), reads=[uk], writes=[f"halo{l}"])
            _ck(6)
            for c in range(KC):
                acc, acck = TF()
                for j in range(3):
                    pv, pk = wpiece([w_in[l][:, OFF["g"] + j * D + c * 128: OFF["g"] + j * D + (c + 1) * 128], w_br[j][l][:, c * 128:(c + 1) * 128]])
                    pg, pgk = PS()
                    mm_fm(pg, pgk, pv, pk, 0, KC, 0, 128, xnT, "xnT", NT)
                    pb, pbk = PS()
                    mm_fm(pb, pbk, pv, pk, KC, 4, 0, 128, ybT[j], f"ybT{j}", NT)
                    sg, sgk = TF()
                    P.op("act", lambda e, sg=sg, pg=pg: e.activation(out=sg[:, 0:NT], in_=pg[:, 0:NT], func=AF.Sigmoid), reads=[pgk], writes=[sgk])
                    if j == 0:
                        P.op("dve", lambda e, sg=sg, pb=pb, acc=acc: e.tensor_tensor(out=acc[:, 0:NT], in0=sg[:, 0:NT], in1=pb[:, 0:NT], op=ALU.mult),
                             reads=[sgk, pbk], writes=[acck])
                    else:
                        P.op("dve", lambda e, sg=sg, pb=pb: e.tensor_tensor(out=sg[:, 0:NT], in0=sg[:, 0:NT], in1=pb[:, 0:NT], op=ALU.mult),
                             reads=[sgk, pbk], writes=[sgk])
                        if j == 1:
                            P.op("dve", lambda e, sg=sg, acc=acc: e.tensor_tensor(out=acc[:, 0:NT], in0=acc[:, 0:NT], in1=sg[:, 0:NT], op=ALU.add),
                                 reads=[sgk, acck], writes=[acck])
                        else:
                            P.op("dve", lambda e, sg=sg, acc=acc, c=c: e.tensor_tensor(out=mergedT[:, c, 0:NT], in0=acc[:, 0:NT], in1=sg[:, 0:NT], op=ALU.add),
                                 reads=[sgk, acck], writes=["hT"])
            for c in range(KC):
                pv, pk = wpiece([w_o[l][:, c * 128:(c + 1) * 128]])
                po, pok = PS()
                mm_fm(po, pok, pv, pk, 0, KC, 0, 128, mergedT, "hT", NT)
                P.op("dve", lambda e, c=c, po=po: e.tensor_tensor(out=xT[:, c, 0:NT], in0=po[:, 0:NT], in1=xT[:, c, 0:NT], op=ALU.add),
                     reads=[pok, "xT"], writes=["xT"])
            _ck(7)
            rmsnorm(3 * l + 2, NT)
            ffn(l, w_up2, w_dn2, NT)
    def final_store(kind, t0, NT, ntile, ydst):
        rmsnorm(3 * DEPTH, NT, inplace=True)
        for t in range(ntile):
            pt_ = min(128, NT - t * 128)
            xi = rot["xin"].next()
            for g in range(2):
                pst, pk = PS()
                for kk in range(4):
                    k = g * 4 + kk
                    P.op("pe", lambda e, k=k, kk=kk, pst=pst, t=t, pt_=pt_: e.transpose(out=pst[0:pt_, kk * 128:(kk + 1) * 128], in_=xT[:, k, t * 128:t * 128 + pt_],
                                                                                         identity=ident[:]),
                         reads=["xT", "ident"], writes=[pk])
                P.op("act", lambda e, g=g, pst=pst, xi=xi, pt_=pt_: e.activation(out=xin[xi][0:pt_, g * 512:(g + 1) * 512], in_=pst[0:pt_, 0:512], func=AF.Identity),
                     reads=[pk], writes=[f"xin{xi}"])
            P.dma("sp", lambda e, t=t, pt_=pt_, xi=xi: e.dma_start(out=ydst[t0 + t * 128:t0 + t * 128 + pt_, :], in_=xin[xi][0:pt_, :]), reads=[f"xin{xi}"])

    def _sched():
        t0 = 0
        while t0 < TP:
            NT = min(512, TP - t0)
            chunks = [(c, min(128, NT - c)) for c in range(0, NT, 128)]
            run_st("p", t0, NT, chunks, [(0, NT, list(range(len(chunks))), None)])
            t0 += NT
        for l in range(DEPTH):
            mlstm_state_store(l, Cp[l], np_[l], mp[l])
        chunks = [(s * DS, DS) for s in range(NSEQ)]
        run_st("s", 0, NS, chunks, [(s * DS, DS, [s], s) for s in range(NSEQ)])

    try:
        _sched()
    except _Stop:
        print('STOPPED early; ops:', len(P.ops))
    P.emit()
    return nc


_NC_CACHE = {}


def _run(inp, n_cores):
    f32 = np.float32
    xpr = np.asarray(inp["x_prompt"], f32); xsa = np.asarray(inp["x_sample"], f32)
    B, SEQ, _ = xpr.shape
    DB, DS_, _ = xsa.shape
    ckk = np.asarray(inp["cache_sb_k"], f32); cvv = np.asarray(inp["cache_sb_v"], f32)
    DEPTH, NPHYS = ckk.shape[0], ckk.shape[1]
    ptab = np.asarray(inp["page_table"], np.int32)
    NPG = ptab.shape[1]
    NSEQ = DB // n_cores
    TP = SEQ + 16
    cfg = dict(TP=TP, NSEQ=NSEQ, NPG=NPG, NPHYS=NPHYS, DEPTH=DEPTH)
    key = tuple(sorted(cfg.items()))
    if key not in _NC_CACHE:
        _NC_CACHE[key] = build(cfg)
    nc = _NC_CACHE[key]
    meta = np.asarray(inp["meta_tokens"], f32)
    ck2 = np.ascontiguousarray(ckk.reshape(DEPTH * NPHYS * 128, 512)); cv2 = np.ascontiguousarray(cvv.reshape(DEPTH * NPHYS * 128, 512))
    A = lambda k: np.ascontiguousarray(np.asarray(inp[k], f32))
    common = dict(ck=ck2, cv=cv2, g1=A("norm_ff1"), up1=A("ffn1_w_up"), dn1=A("ffn1_w_down"), gm=A("norm_mix"), win=A("w_in"),
                  sbb=A("sb_logit_bias"), ib=A("ml_igate_bias"), fb=A("ml_fgate_bias"), hg=A("ml_head_norm"), cw=A("conv_w"),
                  brs=A("w_branch_sb"), brm=A("w_branch_ml"), brc=A("w_branch_cv"), wo=A("w_out"), g2=A("norm_ff2"),
                  up2=A("ffn2_w_up"), dn2=A("ffn2_w_down"), gf=A("norm_final"))
    sC = A("state_ml_C"); sn = A("state_ml_n"); sm = A("state_ml_m"); scv = A("state_conv")
    in_maps = []
    for c in range(n_cores):
        if c < B:
            xpc = np.concatenate([meta, xpr[c]], axis=0)
        else:
            xpc = np.zeros((TP, D), f32)
        sl = slice(c * NSEQ, (c + 1) * NSEQ)
        m = dict(common)
        m.update(xp=np.ascontiguousarray(xpc), xs=np.ascontiguousarray(xsa[sl].reshape(NSEQ * DS_, D)), pt=np.ascontiguousarray(ptab[sl]),
                 sC=np.ascontiguousarray(sC[:, sl]), sn=np.ascontiguousarray(sn[:, sl]), sm=np.ascontiguousarray(sm[:, sl]),
                 scv=np.ascontiguousarray(scv[:, sl]))
        in_maps.append(m)
    res = run_bass_kernel_spmd(nc, in_maps, core_ids=list(range(n_cores))).results
    R = lambda k, cs: [np.asarray(res[c][k], f32) for c in cs]
    pb = list(range(B)); ac = list(range(n_cores))
    y_prompt = np.stack([r[16:] for r in R("yp", pb)])
    y_sample = np.concatenate([r.reshape(NSEQ, DS_, D) for r in R("ys", ac)])
    kpo = np.stack(R("kp", pb), axis=1).reshape(DEPTH, B, TP, 8, 64)
    vpo = np.stack(R("vp", pb), axis=1).reshape(DEPTH, B, TP, 8, 64)
    Cpo = np.stack(R("Cp", pb), axis=1); npo = np.stack(R("np", pb), axis=1); mpo = np.stack(R("mp", pb), axis=1)
    cvpo = np.stack(R("cvp", pb), axis=1)
    kso = np.concatenate([r.reshape(DEPTH, NSEQ, DS_, 8, 64) for r in R("ks", ac)], axis=1)
    vso = np.concatenate([r.reshape(DEPTH, NSEQ, DS_, 8, 64) for r in R("vs", ac)], axis=1)
    Cso = np.concatenate(R("Cs", ac), axis=1); nso = np.concatenate(R("ns", ac), axis=1); mso = np.concatenate(R("ms", ac), axis=1)
    cvso = np.concatenate(R("cvs", ac), axis=1)
    return (y_prompt, y_sample, kpo, vpo, Cpo, npo, mpo, cvpo, kso, vso, Cso, nso, mso, cvso)


def kernel(**inputs):
    return _run(inputs, 8)
```

```python
import numpy as np
import concourse.bass as bass
import concourse.mybir as mybir
from concourse.bass_utils import run_bass_kernel_spmd

F32 = mybir.dt.float32
BF16 = mybir.dt.bfloat16
I32 = mybir.dt.int32
AF = mybir.ActivationFunctionType
ALU = mybir.AluOpType

COMPUTE = ("pe", "act", "dve", "pool")
OWN_SYNC = True
NROT = 8
D = 1024
KC = 8
DFF = 2816
NFF = 22
HW = 512
DIN = 8208
EPS = 1e-6
OFF = dict(sq=0, sk=512, sv=1024, mq=1536, mk=2048, mv=2560, mi=3072, mf=3080, mo=3088, cb=3600, cc=4112, ch=4624, g=5136)


class Prog:
    def __init__(self, nc):
        self.nc = nc
        self.ops = []
        self.cnt = {e: 0 for e in COMPUTE}
        self.dcnt = {"sp": 0, "pool": 0, "act": 0}
        self.lastw = {}
        self.readers = {}

    def _deps(self, reads, writes):
        deps = []
        for k in reads:
            if k in self.lastw:
                deps.append(self.lastw[k])
        for k in writes:
            if k in self.lastw:
                deps.append(self.lastw[k])
            deps.extend(self.readers.get(k, ()))
        return deps

    def _commit(self, tok, reads, writes):
        for k in reads:
            self.readers.setdefault(k, []).append(tok)
        for k in writes:
            self.lastw[k] = tok
            self.readers[k] = []

    def op(self, eng, fn, reads=(), writes=()):
        writes = list(writes) + [k for k in reads if k.startswith("ps") and k not in writes]
        deps = self._deps(reads, writes)
        self.cnt[eng] += 1
        tok = ("c", eng, self.cnt[eng])
        self.ops.append((eng, fn, deps, tok))
        self._commit(tok, reads, writes)

    def dma(self, eng, fn, reads=(), writes=()):
        deps = self._deps(reads, writes)
        k = self.dcnt[eng]
        self.dcnt[eng] += 1
        tok = ("d", eng, k % NROT, 16 * (k // NROT + 1))
        if k >= NROT:
            deps.append(("d", eng, k % NROT, 16 * (k // NROT)))
        self.ops.append((eng, fn, deps, tok))
        self._commit(tok, reads, writes)

    def emit(self, final_wait_eng="sp"):
        nc = self.nc
        sem_c = {e: nc.alloc_semaphore(f"c_{e}") for e in COMPUTE}
        sem_d = {e: [nc.alloc_semaphore(f"d_{e}{i}") for i in range(NROT)] for e in self.dcnt}
        streams = {"pe": [], "act": [], "dve": [], "pool": [], "sp": []}
        for o in self.ops:
            streams[o[0]].append(o)

        def run(ename, e):
            waited = {}
            for (_, fn, deps, tok) in streams[ename]:
                need = {}
                for d in deps:
                    if d[0] == "c":
                        if d[1] == ename and (ename == "pe" or not OWN_SYNC):
                            continue
                        key = ("c", d[1]); val = d[2]
                    else:
                        key = ("d", d[1], d[2]); val = d[3]
                    if waited.get(key, 0) >= val:
                        continue
                    need[key] = max(need.get(key, 0), val)
                for key, val in need.items():
                    s = sem_c[key[1]] if key[0] == "c" else sem_d[key[1]][key[2]]
                    e.wait_ge(s, val)
                    waited[key] = val
                ins = fn(e)
                if tok[0] == "c":
                    ins.then_inc(sem_c[tok[1]], 1)
                else:
                    ins.then_inc(sem_d[tok[1]][tok[2]], 16)
            if ename == final_wait_eng:
                for en in COMPUTE:
                    if self.cnt[en]:
                        e.wait_ge(sem_c[en], self.cnt[en])
                for en, k in self.dcnt.items():
                    for i in range(NROT):
                        n = (k - i + NROT - 1) // NROT if k > i else 0
                        if n:
                            e.wait_ge(sem_d[en][i], 16 * n)

        with nc.Block() as block:
            @block.tensor
            def _(e): run("pe", e)

            @block.scalar
            def _(e): run("act", e)

            @block.vector
            def _(e): run("dve", e)

            @block.gpsimd
            def _(e): run("pool", e)

            @block.sync
            def _(e): run("sp", e)


class _Stop(Exception):
    pass


def _ck(k):
    return None


def _ckq(k):
    return False


class Rot:
    def __init__(self, name, n):
        self.name, self.n, self.i = name, n, 0

    def next(self):
        i = self.i % self.n
        self.i += 1
        return i


def build(cfg):
    TP, NSEQ, NPG, NPHYS, DEPTH = cfg["TP"], cfg["NSEQ"], cfg["NPG"], cfg["NPHYS"], cfg["DEPTH"]
    DS = 8
    NS = NSEQ * DS
    NBLK = (TP + 127) // 128
    nc = bass.Bass("TRN2", target_bir_lowering=False)
    P = Prog(nc)

    def din(name, shape, dt=F32):
        return nc.dram_tensor(name, list(shape), dt, kind="ExternalInput").ap()

    def dout(name, shape, dt=F32):
        return nc.dram_tensor(name, list(shape), dt, kind="ExternalOutput").ap()

    xp = din("xp", [TP, D]); xs = din("xs", [NS, D])
    ck = din("ck", [DEPTH * NPHYS * 128, HW]); cv = din("cv", [DEPTH * NPHYS * 128, HW])
    pt_in = din("pt", [NSEQ, NPG], I32)
    sC = din("sC", [DEPTH, NSEQ, 8, 64, 64]); sn = din("sn", [DEPTH, NSEQ, 8, 64]); sm = din("sm", [DEPTH, NSEQ, 8])
    scv = din("scv", [DEPTH, NSEQ, 2, HW])
    w_g1 = din("g1", [DEPTH, D]); w_up1 = din("up1", [DEPTH, D, 2 * DFF]); w_dn1 = din("dn1", [DEPTH, DFF, D])
    w_gm = din("gm", [DEPTH, D]); w_in = din("win", [DEPTH, D, DIN])
    w_sbb = din("sbb", [DEPTH, 8]); w_ib = din("ib", [DEPTH, 8]); w_fb = din("fb", [DEPTH, 8]); w_hg = din("hg", [DEPTH, HW])
    w_cw = din("cw", [DEPTH, 3, HW])
    w_br = [din("brs", [DEPTH, HW, D]), din("brm", [DEPTH, HW, D]), din("brc", [DEPTH, HW, D])]
    w_o = din("wo", [DEPTH, D, D])
    w_g2 = din("g2", [DEPTH, D]); w_up2 = din("up2", [DEPTH, D, 2 * DFF]); w_dn2 = din("dn2", [DEPTH, DFF, D])
    w_gf = din("gf", [D])
    yp = dout("yp", [TP, D]); ys = dout("ys", [NS, D])
    kp = dout("kp", [DEPTH, TP, HW]); vp = dout("vp", [DEPTH, TP, HW])
    Cp = dout("Cp", [DEPTH, 8, 64, 64]); np_ = dout("np", [DEPTH, 8, 64]); mp = dout("mp", [DEPTH, 8]); cvp = dout("cvp", [DEPTH, 2, HW])
    ks = dout("ks", [DEPTH, NS, HW]); vs = dout("vs", [DEPTH, NS, HW])
    Cs = dout("Cs", [DEPTH, NSEQ, 8, 64, 64]); ns_ = dout("ns", [DEPTH, NSEQ, 8, 64]); ms_ = dout("ms", [DEPTH, NSEQ, 8])
    cvs = dout("cvs", [DEPTH, NSEQ, 2, HW])
    ktc = nc.dram_tensor("ktc", [DEPTH, NBLK, 128, 4, 128], BF16, kind="Internal").ap()
    vc = nc.dram_tensor("vc", [DEPTH, NBLK, 128, HW], BF16, kind="Internal").ap()
    NWC = 170 * DEPTH
    wcache = nc.dram_tensor("wcache", [NWC, 128, 2304], BF16, kind="Internal").ap()

    def sb(name, shape, dt=F32):
        return nc.alloc_sbuf_tensor(name, list(shape), dt)

    NTM = 512
    xT = sb("xT", [128, KC, NTM]); xnT = sb("xnT", [128, KC, NTM], BF16)
    hT = sb("hT", [128, 8, NTM], BF16)
    sqT = sb("sqT", [128, 4, NTM], BF16); skT = sb("skT", [128, 4, NTM], BF16)
    mqT = sb("mqT", [128, 4, NTM], BF16); mkT = sb("mkT", [128, 4, NTM], BF16)
    sv_tok = sb("sv_tok", [128, 4, HW], BF16); mk_tok = sb("mk_tok", [128, 4, HW], BF16)
    mv_ext = sb("mv_ext", [128, 4, 4, 132], BF16)
    moS = sb("moS", [128, 4, NTM], BF16); cbT = sb("cbT", [128, 4, NTM], BF16)
    uext_1 = sb("uext_1", [128, 4, 2 + NTM])
    uext_p = [uext_1 for l in range(DEPTH)]
    halo = [sb(f"halo{l}", [128, 4, 2]) for l in range(DEPTH)]
    uext_s = sb("uext_s", [128, 4, 2 + DS])
    ybT = [sb(f"ybT{j}", [128, 4, NTM], BF16) for j in range(3)]
    mergedT = hT
    liT = sb("liT", [8, NTM]); lfT = sb("lfT", [8, NTM]); mT = sb("mT", [8, NTM]); bT = sb("bT", [8, NTM])
    zeros8 = sb("zeros8", [8, NTM])
    piece = [sb(f"piece{i}", [128, 2304], BF16) for i in range(5)]
    tmpf = [sb(f"tmpf{i}", [128, 512]) for i in range(5)]
    tmpb = [sb(f"tmpb{i}", [128, 512], BF16) for i in range(4)]
    xin = [sb(f"xin{i}", [128, D]) for i in range(1)]
    ident = sb("ident", [128, 128]); ones_f = sb("ones_f", [128, 128])
    onesm = sb("onesm", [128, 128], BF16)
    ones_b = sb("ones_b", [128, 128], BF16)
    blk64 = sb("blk64", [128, 128], BF16)
    BDf = sb("BDf", [128, 128])
    mask_lt = sb("mask_lt", [128, 128])
    mask_le = sb("mask_le", [128, 128])
    UI = sb("UI", [128, 128], BF16)
    CM = sb("CM", [128, 128], BF16)
    zeros_b = sb("zeros_b", [128, 512], BF16)
    selH = sb("selH", [8, 8, 128]); selP = sb("selP", [8, 4, 128])
    gcol = sb("gcol", [128, 3 * DEPTH + 1, KC])
    hgcol = sb("hgcol", [128, DEPTH, 4]); cwcol = sb("cwcol", [128, DEPTH, 4, 3])
    b8 = sb("b8", [8, DEPTH, 4])
    sbb1 = sb("sbb1", [1, DEPTH, 8]); sbhi = sb("sbhi", [1, DEPTH, 8], BF16); sblo = sb("sblo", [1, DEPTH, 8], BF16)
    sbt = sb("sbt", [1, DEPTH, 8])
    bias2 = [sb(f"bias2_{l}", [2, 8 * 128], BF16) for l in range(DEPTH)]
    bias2s = [sb(f"bias2s_{l}", [2, 8 * 16], BF16) for l in range(DEPTH)]
    ones2 = sb("ones2", [2, 128], BF16)
    epsc = sb("epsc", [128, 1])
    blo_tmp = sb("blo_tmp", [1, 1024], BF16)
    Qbd = sb("Qbd", [128, 4, 256], BF16)
    ktb = [sb(f"ktb{i}", [128, 4, 128], BF16) for i in range(3)]
    vb = [sb(f"vb{i}", [128, HW], BF16) for i in range(3)]
    kpg = [sb(f"kpg{i}", [128, HW]) for i in range(1)]; vpg = [sb(f"vpg{i}", [128, HW]) for i in range(1)]
    Et = [sb(f"Et{i}", [128, 512]) for i in range(2)]; Xt = [sb(f"Xt{i}", [128, 512]) for i in range(1)]
    Lt = [sb(f"Lt{i}", [128, 512], BF16) for i in range(2)]; Wt = [sb(f"Wt{i}", [128, 512], BF16) for i in range(2)]
    Cpad = [sb(f"Cpad{i}", [128, 4, 128]) for i in range(DEPTH + 1)]
    Cb = [sb(f"Cb{i}", [128, 4, 128], BF16) for i in range(DEPTH + 1)]
    ncol = [sb(f"ncol{i}", [128, 4]) for i in range(DEPTH + 1)]
    Npad = [sb(f"Npad{i}", [128, 4, 128], BF16) for i in range(DEPTH + 1)]
    m8 = [sb(f"m8_{i}", [8, 1]) for i in range(DEPTH + 1)]
    rp8 = sb("rp8", [8, 1])
    acol = sb("acol", [128, 8]); wend = sb("wend", [128, 8]); rend = sb("rend", [128, 8])
    r_bc = sb("r_bc", [128, 8, 128]); rp_bc = sb("rp_bc", [128, 4, 128]); nm_bc = sb("nm_bc", [128, 4, 128])
    nrprev = sb("nrprev", [128, 4]); decay = sb("decay", [128, 4])
    Dt = sb("Dt", [128, 2, 128]); SWt = sb("SWt", [128, 2, 128], BF16)
    kw = sb("kw", [128, HW], BF16)
    ptb = sb("ptb", [128, NSEQ, NPG], I32); ptf = sb("ptf", [128, NSEQ, NPG]); pidx = sb("pidx", [128, DEPTH, NSEQ, NPG], I32)
    iota_p = sb("iota_p", [128, 1])
    ps = [nc.alloc_psum_tensor(f"ps{i}", [128, 512], F32) for i in range(8)]
    psrot = Rot("ps", 4)
    rot = {n: Rot(n, k) for n, k in [("piece", 5), ("tmpf", 5), ("tmpb", 4), ("xin", 1), ("ktb", 3), ("vb", 3),
                                     ("pg", 1), ("E", 2), ("X", 1), ("L", 2), ("W", 2)]}

    def PS():
        i = psrot.next()
        return ps[i], f"ps{i}"

    def TF():
        i = rot["tmpf"].next()
        return tmpf[i], f"tmpf{i}"

    def TB():
        i = rot["tmpb"].next()
        return tmpb[i], f"tmpb{i}"

    def fin():
        P.emit()
        return nc

    P.op("pool", lambda e: e.memset(ones_f[:], 1.0), writes=["ones_f"])
    P.op("pool", lambda e: e.memset(zeros_b[:], 0.0), writes=["zeros_b"])
    P.op("pool", lambda e: e.memset(zeros8[:], 0.0), writes=["zeros8"])
    P.op("pool", lambda e: e.memset(onesm[:], 1.0 / 1024), writes=["onesm"])
    P.op("pool", lambda e: e.memset(ones_b[:], 1.0), writes=["ones_b"])
    P.op("pool", lambda e: e.memset(ones2[:], 1.0), writes=["ones2"])
    P.op("pool", lambda e: e.memset(epsc[:], EPS), writes=["epsc"])

    def asel(out, pattern, op, cm, key, base=0, src=None, skey="ones_f"):
        src = ones_f[:] if src is None else src
        P.op("pool", lambda e: e.affine_select(out=out, in_=src, pattern=pattern, compare_op=op, fill=0.0, base=base,
                                               channel_multiplier=cm), reads=[skey], writes=[key])
    asel(ident[:], [[-1, 128]], ALU.is_equal, 1, "ident")
    asel(mask_lt[:], [[1, 128]], ALU.is_gt, -1, "mask_lt")
    asel(mask_le[:], [[1, 128]], ALU.is_ge, -1, "mask_le")
    asel(UI[:], [[-1, 128]], ALU.is_ge, 1, "UI")
    asel(CM[:], [[1, 128]], ALU.is_gt, -1, "CM")
    P.op("pool", lambda e: e.memset(BDf[:], 0.0), writes=["BDf"])
    P.op("pool", lambda e: e.memset(BDf[0:64, 0:64], 1.0), writes=["BDf"])
    P.op("pool", lambda e: e.memset(BDf[64:128, 64:128], 1.0), writes=["BDf"])
    P.op("pool", lambda e: e.tensor_scalar(out=blk64[:], in0=BDf[:], scalar1=1.0 / 64, scalar2=None, op0=ALU.mult),
         reads=["BDf"], writes=["blk64"])
    for h in range(8):
        asel(selH[:, h, :], [[0, 128]], ALU.is_equal, 1, "selH", base=-h, src=ones_f[0:8, :])
    for p in range(4):
        asel(selP[:, p, 0:64], [[0, 64]], ALU.is_equal, 1, "selP", base=-2 * p, src=ones_f[0:8, 0:64])
        asel(selP[:, p, 64:128], [[0, 64]], ALU.is_equal, 1, "selP", base=-2 * p - 1, src=ones_f[0:8, 0:64])
    P.op("pool", lambda e: e.iota(iota_p[:], pattern=[[0, 1]], base=0, channel_multiplier=1, allow_small_or_imprecise_dtypes=True),
         writes=["iota_p"])
    if _ckq(-1): return fin()
    gl = []
    for l in range(DEPTH):
        gl += [w_g1[l], w_gm[l], w_g2[l]]
    gl.append(w_gf)
    for i, g in enumerate(gl):
        P.dma("sp", lambda e, i=i, g=g: e.dma_start(out=gcol[:, i, :], in_=g.rearrange("(k p) -> p k", p=128), allow_slow_non_contiguous=True),
              writes=["gcol"])
    for l in range(DEPTH):
        P.dma("sp", lambda e, l=l: e.dma_start(out=hgcol[:, l, :], in_=w_hg[l].rearrange("(k p) -> p k", p=128), allow_slow_non_contiguous=True),
              writes=["hgcol"])
        for j in range(3):
            P.dma("sp", lambda e, l=l, j=j: e.dma_start(out=cwcol[:, l, :, j], in_=w_cw[l, j].rearrange("(k p) -> p k", p=128), allow_slow_non_contiguous=True),
                  writes=["cwcol"])
        P.dma("sp", lambda e, l=l: e.dma_start(out=b8[:, l, 0:1], in_=w_ib[l].rearrange("(h o) -> h o", o=1)), writes=["b8"])
        P.dma("sp", lambda e, l=l: e.dma_start(out=b8[:, l, 1:2], in_=w_fb[l].rearrange("(h o) -> h o", o=1)), writes=["b8"])
        P.dma("sp", lambda e, l=l: e.dma_start(out=sbb1[:, l, :], in_=w_sbb[l].rearrange("(o h) -> o h", o=1)), writes=["sbb1"])
    P.op("dve", lambda e: e.tensor_scalar(out=b8[:, :, 2:3], in0=b8[:, :, 1:2], scalar1=-1.0, scalar2=None, op0=ALU.mult),
         reads=["b8"], writes=["b8"])
    if _ckq(-2): return fin()
    P.op("dve", lambda e: e.tensor_scalar(out=sbt[:], in0=sbb1[:], scalar1=8.0, scalar2=None, op0=ALU.mult), reads=["sbb1"], writes=["sbt"])
    P.op("dve", lambda e: e.tensor_copy(out=sbhi[:], in_=sbt[:]), reads=["sbt"], writes=["sbhi"])
    P.op("dve", lambda e: e.tensor_tensor(out=sbt[:], in0=sbt[:], in1=sbhi[:], op=ALU.subtract), reads=["sbhi", "sbt"], writes=["sbt"])
    P.op("dve", lambda e: e.tensor_copy(out=sblo[:], in_=sbt[:]), reads=["sbt"], writes=["sblo"])
    for l in range(DEPTH):
        for (bt, L) in ((bias2[l], 128), (bias2s[l], 16)):
            v = bt[0:1, :].rearrange("p (h l) -> p h l", l=L)
            P.op("dve", lambda e, v=v, l=l, L=L: e.tensor_copy(out=v, in_=sbhi[0:1, l, :].unsqueeze(2).to_broadcast([1, 8, L])),
                 reads=["sbhi"], writes=[f"bias2_{l}_{L}"])
            tb, tk = blo_tmp, "blo_tmp"
            v2 = tb[0:1, 0:8 * L].rearrange("p (h l) -> p h l", l=L)
            P.op("dve", lambda e, v2=v2, l=l, L=L: e.tensor_copy(out=v2, in_=sblo[0:1, l, :].unsqueeze(2).to_broadcast([1, 8, L])),
                 reads=["sblo"], writes=[tk])
            P.dma("sp", lambda e, bt=bt, tb=tb, L=L: e.dma_start(out=bt[1:2, :], in_=tb[0:1, 0:8 * L]), reads=[tk], writes=[f"bias2_{l}_{L}"])
    if _ckq(-3): return fin()
    P.dma("sp", lambda e: e.dma_start(out=ptb[:].rearrange("p s j -> p (s j)"),
                                      in_=pt_in.rearrange("s j -> (s j)").partition_broadcast(128)), writes=["ptb"])
    P.op("dve", lambda e: e.tensor_copy(out=ptf[:], in_=ptb[:]), reads=["ptb"], writes=["ptf"])
    P.op("dve", lambda e: e.tensor_scalar(out=ptf[:], in0=ptf[:], scalar1=128.0, scalar2=iota_p[:, 0:1], op0=ALU.mult, op1=ALU.add),
         reads=["ptf", "iota_p"], writes=["ptf"])
    for l in range(DEPTH):
        P.op("dve", lambda e, l=l: e.tensor_copy(out=pidx[:, l], in_=ptf[:]), reads=["ptf"], writes=["pidx"])
        P.op("dve", lambda e: e.tensor_scalar(out=ptf[:], in0=ptf[:], scalar1=float(NPHYS * 128), scalar2=None, op0=ALU.add),
             reads=["ptf", "pidx"], writes=["ptf"])
    for l in range(DEPTH):
        P.op("pool", lambda e, l=l: e.memset(mv_ext[:, :, :, 128:132], 1.0), writes=["mv_ext"])

    if _ckq(-4): return fin()
    wstate = dict(pass_idx=-1, pid=0)

    def begin_pass():
        wstate["pass_idx"] += 1
        wstate["pid"] = 0

    def wpiece(srcs):
        n = srcs[0].shape[1]
        kt = sum(s.shape[0] // 128 for s in srcs)
        pid = wstate["pid"]
        wstate["pid"] += 1
        assert pid < NWC
        pi = rot["piece"].next()
        pv = piece[pi][:, 0:kt * n].rearrange("p (k n) -> p k n", n=n)
        if wstate["pass_idx"] == 0:
            ko = 0
            for s in srcs:
                kk = s.shape[0] // 128
                P.dma("pool", lambda e, s=s, ko=ko, kk=kk: e.dma_start(out=pv[:, ko:ko + kk, :], in_=s.rearrange("(k p) n -> p k n", p=128)),
                      writes=[f"piece{pi}"])
                ko += kk
            P.dma("sp", lambda e: e.dma_start(out=wcache[pid, :, 0:kt * n], in_=piece[pi][:, 0:kt * n]), reads=[f"piece{pi}"], writes=[f"wc{pid}"])
        else:
            P.dma("pool", lambda e: e.dma_start(out=piece[pi][:, 0:kt * n], in_=wcache[pid, :, 0:kt * n]), reads=[f"wc{pid}"], writes=[f"piece{pi}"])
        return pv, f"piece{pi}"

    def mm_fm(pst, pkey, pv, pkey_w, k0, nk, m0, mw, act, akey, NT, first=True, last=True, outp=None):
        for k in range(nk):
            o = pst[0:mw, 0:NT] if outp is None else outp
            P.op("pe", lambda e, k=k, o=o: e.matmul(o, lhsT=pv[:, k0 + k, m0:m0 + mw], rhs=act[:, k, 0:NT],
                                                    start=(first and k == 0), stop=(last and k == nk - 1)),
                 reads=[pkey_w, akey], writes=[pkey])

    def rmsnorm(gi, NT, inplace=False):
        pst, pk = PS()
        for k in range(KC):
            tb, tk = TB()
            P.op("act", lambda e, k=k, tb=tb: e.activation(out=tb[:, 0:NT], in_=xT[:, k, 0:NT], func=AF.Square), reads=["xT"], writes=[tk])
            P.op("pe", lambda e, k=k, tb=tb: e.matmul(pst[:, 0:NT], lhsT=onesm[:], rhs=tb[:, 0:NT], start=(k == 0), stop=(k == KC - 1)),
                 reads=[tk, "onesm"], writes=[pk])
        tf, tfk = TF()
        P.op("act", lambda e: e.activation(out=tf[:, 0:NT], in_=pst[:, 0:NT], func=AF.Sqrt, bias=epsc[:, 0:1], scale=1.0), reads=[pk, "epsc"], writes=[tfk])
        P.op("dve", lambda e: e.reciprocal(out=tf[:, 0:NT], in_=tf[:, 0:NT]), reads=[tfk], writes=[tfk])
        for k in range(KC):
            o = xT[:, k, 0:NT] if inplace else xnT[:, k, 0:NT]
            P.op("dve", lambda e, k=k, o=o: e.scalar_tensor_tensor(out=o, in0=xT[:, k, 0:NT], scalar=gcol[:, gi, k:k + 1], in1=tf[:, 0:NT],
                                                                   op0=ALU.mult, op1=ALU.mult),
                 reads=["xT", "gcol", tfk], writes=["xT" if inplace else "xnT"])

    def ffn(l, w_up, w_dn, NT):
        f0 = 0
        for ng in (6, 6, 6, 4):
            for fl in range(ng):
                f = f0 + fl
                pv, pk = wpiece([w_up[l][:, f * 128:(f + 1) * 128], w_up[l][:, DFF + f * 128:DFF + (f + 1) * 128]])
                pg, pgk = PS()
                mm_fm(pg, pgk, pv, pk, 0, KC, 0, 128, xnT, "xnT", NT)
                pu, puk = PS()
                mm_fm(pu, puk, pv, pk, KC, KC, 0, 128, xnT, "xnT", NT)
                tf, tfk = TF()
                P.op("act", lambda e, tf=tf, pg=pg: e.activation(out=tf[:, 0:NT], in_=pg[:, 0:NT], func=AF.Silu), reads=[pgk], writes=[tfk])
                P.op("dve", lambda e, tf=tf, pu=pu, fl=fl: e.tensor_tensor(out=hT[:, fl, 0:NT], in0=tf[:, 0:NT], in1=pu[:, 0:NT], op=ALU.mult),
                     reads=[tfk, puk], writes=["hT"])
            for c in range(KC):
                pv, pk = wpiece([w_dn[l][f0 * 128:(f0 + ng) * 128, c * 128:(c + 1) * 128]])
                po, pok = PS()
                mm_fm(po, pok, pv, pk, 0, ng, 0, 128, hT, "hT", NT)
                P.op("dve", lambda e, c=c, po=po: e.scalar_tensor_tensor(out=xT[:, c, 0:NT], in0=po[:, 0:NT], scalar=0.5, in1=xT[:, c, 0:NT],
                                                                         op0=ALU.mult, op1=ALU.add),
                     reads=[pok, "xT"], writes=["xT"])
            f0 += ng

    def proj_fm(l, col0, nchunk, NT, evac):
        i = 0
        while i < nchunk:
            g = min(2, nchunk - i)
            pv, pk = wpiece([w_in[l][:, col0 + i * 128: col0 + (i + g) * 128]])
            for j in range(g):
                pst, pkey = PS()
                mm_fm(pst, pkey, pv, pk, 0, KC, j * 128, 128, xnT, "xnT", NT)
                evac(i + j, pst, pkey)
            i += g

    def proj_tm(l, col0, chunks, evac):
        for half in range(2):
            pv, pk = wpiece([w_in[l][:, col0 + half * 256: col0 + (half + 1) * 256]])
            for ci, (c0, L) in enumerate(chunks):
                pst, pkey = PS()
                for k in range(KC):
                    P.op("pe", lambda e, k=k, c0=c0, L=L, pst=pst, pv=pv: e.matmul(pst[0:L, 0:256], lhsT=xnT[:, k, c0:c0 + L], rhs=pv[:, k, :],
                                                                           start=(k == 0), stop=(k == KC - 1)),
                         reads=[pk, "xnT"], writes=[pkey])
                evac(ci, half, pst, pkey)

    def attention(l, c0, L, blocks, NT, pre=None):
        NP = 2 if L > 64 else 4
        NU = NP * 2 * L
        nunit = 4 // NP
        LB = 128 if L == 128 else 16
        b2 = bias2[l] if L == 128 else bias2s[l]
        b2k = f"bias2_{l}_{LB}"
        assert L <= 16 or L == 128
        P.op("dve", lambda e: e.memset(Qbd[:], 0.0), writes=["Qbd"])
        P.op("dve", lambda e: e.tensor_copy(out=Qbd[0:64, :, 0:L], in_=sqT[0:64, :, c0:c0 + L]), reads=["sqT"], writes=["Qbd"])
        P.op("dve", lambda e: e.tensor_copy(out=Qbd[64:128, :, L:2 * L], in_=sqT[64:128, :, c0:c0 + L]), reads=["sqT"], writes=["Qbd"])
        Tb = [ps[4], ps[5]]
        Ob = ps[6]
        P.op("pe", lambda e: e.matmul(Ob[:, 0:512], lhsT=zeros_b[:, 0:128], rhs=zeros_b[:, 0:512], start=True, stop=True),
             reads=["zeros_b"], writes=["ps6"])

        def unit(bi, blk, u):
            kt, v, Lk = blk["kt"], blk["v"], blk["Lk"]
            kkey, vkey, diag = blk["kkey"], blk["vkey"], blk["diag"]
            pu = u * NP
            S, Sk = PS()
            if L == LB:
                P.op("pe", lambda e: e.matmul(S[0:Lk, 0:NU], lhsT=ones2[0:2, 0:Lk], rhs=b2[0:2, pu * 2 * L: pu * 2 * L + NU], start=True, stop=False),
                     reads=[b2k, "ones2"], writes=[Sk])
            else:
                for hh in range(2 * NP):
                    P.op("pe", lambda e, hh=hh: e.matmul(S[0:Lk, hh * L:(hh + 1) * L], lhsT=ones2[0:2, 0:Lk],
                                                         rhs=b2[0:2, ((pu * 2 + hh) * LB):((pu * 2 + hh) * LB) + L], start=(hh == 0), stop=False),
                         reads=[b2k, "ones2"], writes=[Sk])
            for j in range(NP):
                P.op("pe", lambda e, j=j: e.matmul(S[0:Lk, j * 2 * L:(j + 1) * 2 * L], lhsT=kt[:, pu + j, 0:Lk], rhs=Qbd[:, pu + j, 0:2 * L],
                                                   start=False, stop=(j == NP - 1)),
                     reads=[kkey, "Qbd"], writes=[Sk])
            ei = rot["E"].next(); li_ = rot["L"].next(); xi = rot["X"].next(); wi = rot["W"].next()
            E, Lp, X, Wm = Et[ei], Lt[li_], Xt[xi], Wt[wi]
            P.op("act", lambda e: e.activation(out=E[0:Lk, 0:NU], in_=S[0:Lk, 0:NU], func=AF.Exp, scale=0.125), reads=[Sk], writes=[f"E{ei}"])
            P.op("act", lambda e: e.activation(out=Lp[0:Lk, 0:NU], in_=E[0:Lk, 0:NU], func=AF.Ln, bias=1.0, scale=1.0),
                 reads=[f"E{ei}"], writes=[f"L{li_}"])
            if diag:
                mk3 = mask_lt[0:Lk, 0:L].unsqueeze(1).to_broadcast([Lk, 2 * NP, L])
                P.op("dve", lambda e: e.tensor_tensor(out=Lp[0:Lk, 0:NU].rearrange("p (h l) -> p h l", l=L),
                                                       in0=Lp[0:Lk, 0:NU].rearrange("p (h l) -> p h l", l=L), in1=mk3, op=ALU.mult),
                     reads=[f"L{li_}", "mask_lt"], writes=[f"L{li_}"])
                P.op("dve", lambda e: e.tensor_tensor(out=E[0:Lk, 0:NU].rearrange("p (h l) -> p h l", l=L),
                                                       in0=E[0:Lk, 0:NU].rearrange("p (h l) -> p h l", l=L), in1=mk3, op=ALU.mult),
                     reads=[f"E{ei}", "mask_lt"], writes=[f"E{ei}"])
            T = Tb[u]
            Tk = f"ps{4 + u}"
            P.op("pe", lambda e: e.matmul(T[:, 0:NU], lhsT=UI[0:Lk, :], rhs=Lp[0:Lk, 0:NU], start=(bi == 0), stop=True, skip_group_check=True),
                 reads=[f"L{li_}", "UI"], writes=[Tk])
            P.op("act", lambda e: e.activation(out=X[0:Lk, 0:NU], in_=T[0:Lk, 0:NU], func=AF.Exp, scale=-1.0), reads=[Tk], writes=[f"X{xi}"])
            P.op("pe", lambda e: e.matmul(T[:, 0:NU], lhsT=CM[0:Lk, :], rhs=Lp[0:Lk, 0:NU], start=False, stop=True, skip_group_check=True),
                 reads=[f"L{li_}", "CM"], writes=[Tk])
            P.op("dve", lambda e: e.tensor_tensor(out=Wm[0:Lk, 0:NU], in0=E[0:Lk, 0:NU], in1=X[0:Lk, 0:NU], op=ALU.mult),
                 reads=[f"E{ei}", f"X{xi}"], writes=[f"W{wi}"])
            for j in range(NP):
                for a in range(2):
                    h = 2 * (pu + j) + a
                    P.op("pe", lambda e, j=j, a=a, h=h: e.matmul(Ob[a * 64:(a + 1) * 64, (pu + j) * L:(pu + j + 1) * L],
                                                                 lhsT=v[0:Lk, h * 64:(h + 1) * 64], rhs=Wm[0:Lk, (2 * j + a) * L:(2 * j + a + 1) * L],
                                                                 start=False, stop=True, skip_group_check=True),
                         reads=[f"W{wi}", vkey], writes=["ps6"])

        for bi, blk in enumerate(blocks):
            if pre is not None:
                blk = pre(blk)
            for u in range(nunit):
                unit(bi, blk, u)
        P.op("act", lambda e: e.activation(out=ybT[0][:, :, c0:c0 + L], in_=Ob[:, 0:4 * L].rearrange("p (k l) -> p k l", l=L), func=AF.Identity),
             reads=["ps6"], writes=["ybT0"])

    def kv_pre(l, b):
        def pre(blk):
            if "cache" in blk:
                ki = rot["ktb"].next(); vi = rot["vb"].next()
                bk = blk["cache"]
                P.dma("sp", lambda e: e.dma_start(out=ktb[ki][:], in_=ktc[l, bk]), reads=[f"kvc{l}"], writes=[f"ktb{ki}"])
                P.dma("sp", lambda e: e.dma_start(out=vb[vi][:], in_=vc[l, bk]), reads=[f"kvc{l}"], writes=[f"vb{vi}"])
                return dict(blk, kt=ktb[ki][:], v=vb[vi][:], kkey=f"ktb{ki}", vkey=f"vb{vi}")
            if "page" in blk:
                ki = rot["ktb"].next(); vi = rot["vb"].next(); gi = rot["pg"].next()
                pg = blk["page"]
                P.dma("pool", lambda e: e.indirect_dma_start(out=kpg[gi][:], out_offset=None, in_=ck,
                                                             in_offset=bass.IndirectOffsetOnAxis(ap=pidx[:, l, b, pg:pg + 1], axis=0)),
                      reads=["pidx"], writes=[f"kpg{gi}"])
                P.dma("pool", lambda e: e.indirect_dma_start(out=vpg[gi][:], out_offset=None, in_=cv,
                                                             in_offset=bass.IndirectOffsetOnAxis(ap=pidx[:, l, b, pg:pg + 1], axis=0)),
                      reads=["pidx"], writes=[f"vpg{gi}"])
                pst, pk = PS()
                for p in range(4):
                    P.op("pe", lambda e, p=p: e.transpose(out=pst[:, p * 128:(p + 1) * 128], in_=kpg[gi][:, p * 128:(p + 1) * 128], identity=ident[:]),
                         reads=[f"kpg{gi}", "ident"], writes=[pk])
                P.op("dve", lambda e: e.tensor_copy(out=ktb[ki][:], in_=pst[:, :].rearrange("p (k n) -> p k n", n=128)), reads=[pk], writes=[f"ktb{ki}"])
                P.op("dve", lambda e: e.tensor_copy(out=vb[vi][:], in_=vpg[gi][:]), reads=[f"vpg{gi}"], writes=[f"vb{vi}"])
                return dict(blk, kt=ktb[ki][:], v=vb[vi][:], kkey=f"ktb{ki}", vkey=f"vb{vi}")
            return blk
        return pre

    def mlstm_chunk(l, si, c0, L, ci, first):
        st = f"mst{si}"
        if first:
            P.op("dve", lambda e: e.tensor_scalar(out=rp8[:], in0=m8[si][:], scalar1=-1.0, scalar2=None, op0=ALU.mult), reads=[st + "m"], writes=["rp8"])
        else:
            P.op("dve", lambda e: e.tensor_copy(out=rp8[:], in_=bT[:, c0 - 1:c0]), reads=["bT"], writes=["rp8"])
        P.op("pe", lambda e: e.transpose(out=ps[7][0:L, 0:8], in_=liT[0:8, c0:c0 + L], identity=ident[0:8, 0:8]), reads=["liT", "ident"], writes=["ps7"])
        P.op("dve", lambda e: e.tensor_copy(out=acol[0:L, :], in_=ps[7][0:L, 0:8]), reads=["ps7"], writes=["acol"])
        for hq in range(2):
            pst, pk = PS()
            for hh in range(4):
                h = hq * 4 + hh
                P.op("pe", lambda e, pst=pst, hh=hh, h=h: e.matmul(pst[:, hh * L:(hh + 1) * L], lhsT=selH[:, h, :], rhs=bT[0:8, c0:c0 + L],
                                                                   start=(hh == 0), stop=(hh == 3)), reads=["selH", "bT"], writes=[pk])
            P.op("act", lambda e, pst=pst, hq=hq: e.activation(out=r_bc[:, hq * 4:(hq + 1) * 4, 0:L],
                                                                in_=pst[:, 0:4 * L].rearrange("p (h l) -> p h l", l=L), func=AF.Identity),
                 reads=[pk], writes=["r_bc"])
        for (src, skey, dst, dkey) in ((bT, "bT", rp_bc, "rp_bc"), (mT, "mT", nm_bc, "nm_bc")):
            pst, pk = PS()
            for p in range(4):
                P.op("pe", lambda e, pst=pst, p=p, src=src: e.matmul(pst[:, p * L:(p + 1) * L], lhsT=selP[:, p, :], rhs=src[0:8, c0:c0 + L],
                                                                     start=(p == 0), stop=(p == 3)), reads=["selP", skey], writes=[pk])
            P.op("dve", lambda e, pst=pst, dst=dst: e.tensor_copy(out=dst[:, :, 0:L], in_=pst[:, 0:4 * L].rearrange("p (h l) -> p h l", l=L)),
                 reads=[pk], writes=[dkey])
        pst, pk = PS()
        for p in range(4):
            P.op("pe", lambda e, pst=pst, p=p: e.matmul(pst[:, p:p + 1], lhsT=selP[:, p, :], rhs=rp8[0:8, 0:1], start=(p == 0), stop=(p == 3)),
                 reads=["selP", "rp8"], writes=[pk])
        P.op("dve", lambda e, pst=pst: e.tensor_scalar(out=nrprev[:], in0=pst[:, 0:4], scalar1=-1.0, scalar2=None, op0=ALU.mult),
             reads=[pk], writes=["nrprev"])
        P.op("dve", lambda e: e.tensor_tensor(out=rend[0:L, :], in0=acol[0:L, :], in1=r_bc[0:L, :, L - 1], op=ALU.add),
             reads=["acol", "r_bc"], writes=["rend"])
        P.op("act", lambda e: e.activation(out=wend[0:L, :], in_=rend[0:L, :], func=AF.Exp), reads=["rend"], writes=["wend"])
        P.op("dve", lambda e: e.tensor_tensor(out=kw[0:L, :].rearrange("p (h d) -> p h d", d=64),
                                              in0=mk_tok[0:L, ci, :].rearrange("p (h d) -> p h d", d=64),
                                              in1=wend[0:L, :].unsqueeze(2).to_broadcast([L, 8, 64]), op=ALU.mult),
             reads=["mk_tok", "wend"], writes=["kw"])
        for p in range(4):
            Sa = []
            for a in range(2):
                po = a * 64
                S, Sk = PS()
                Sa.append((S, Sk))
                P.op("pe", lambda e, S=S, a=a, po=po, p=p: e.matmul(S[0:L, 0:L], lhsT=mkT[po:po + 64, p, c0:c0 + L],
                                                                    rhs=mqT[po:po + 64, p, c0:c0 + L], start=True, stop=True),
                     reads=["mkT", "mqT"], writes=[Sk])
                P.op("act", lambda e, a=a, p=p: e.activation(out=Dt[0:L, a, 0:L], in_=r_bc[0:L, 2 * p + a, 0:L], func=AF.Exp,
                                                              bias=acol[0:L, 2 * p + a:2 * p + a + 1], scale=1.0),
                     reads=["r_bc", "acol"], writes=["Dt"])
            mk3 = mask_le[0:L, 0:L].unsqueeze(1).to_broadcast([L, 2, L])
            P.op("dve", lambda e, mk3=mk3: e.tensor_tensor(out=Dt[0:L, :, 0:L], in0=Dt[0:L, :, 0:L], in1=mk3, op=ALU.mult),
                 reads=["Dt", "mask_le"], writes=["Dt"])
            for a in range(2):
                S, Sk = Sa[a]
                P.op("dve", lambda e, S=S, a=a: e.tensor_tensor(out=SWt[0:L, a, 0:L], in0=Dt[0:L, a, 0:L], in1=S[0:L, 0:L], op=ALU.mult),
                     reads=["Dt", Sk], writes=["SWt"])
            wt, wtk = TF()
            P.op("act", lambda e, wt=wt, p=p: e.activation(out=wt[:, 0:L], in_=rp_bc[:, p, 0:L], func=AF.Exp, bias=nrprev[:, p:p + 1], scale=1.0),
                 reads=["rp_bc", "nrprev"], writes=[wtk])
            qw, qwk = TB()
            P.op("dve", lambda e, wt=wt, qw=qw, p=p: e.tensor_tensor(out=qw[:, 0:L], in0=mqT[:, p, c0:c0 + L], in1=wt[:, 0:L], op=ALU.mult),
                 reads=[wtk, "mqT"], writes=[qwk])
            N_, Nk = PS()
            for a in range(2):
                P.op("pe", lambda e, N_=N_, a=a, p=p: e.matmul(N_[a * 64:(a + 1) * 64, 0:L], lhsT=mv_ext[0:L, ci, p, a * 64:(a + 1) * 64],
                                                               rhs=SWt[0:L, a, 0:L], start=True, stop=False, skip_group_check=True),
                     reads=["mv_ext", "SWt"], writes=[Nk])
            P.op("pe", lambda e, N_=N_, qw=qw, p=p: e.matmul(N_[:, 0:L], lhsT=Cb[si][:, p, :], rhs=qw[:, 0:L], start=False, stop=True,
                                                             skip_group_check=True), reads=[st + "Cb", qwk], writes=[Nk])
            Dn, Dk = PS()
            for a in range(2):
                P.op("pe", lambda e, Dn=Dn, a=a: e.matmul(Dn[a * 64:(a + 1) * 64, 0:L], lhsT=ones_b[0:L, 0:64], rhs=SWt[0:L, a, 0:L],
                                                          start=True, stop=False, skip_group_check=True), reads=["ones_b", "SWt"], writes=[Dk])
            P.op("pe", lambda e, Dn=Dn, qw=qw, p=p: e.matmul(Dn[:, 0:L], lhsT=Npad[si][:, p, :], rhs=qw[:, 0:L], start=False, stop=True,
                                                             skip_group_check=True), reads=[st + "Np", qwk], writes=[Dk])
            en, enk = TF()
            P.op("act", lambda e, en=en, p=p: e.activation(out=en[:, 0:L], in_=nm_bc[:, p, 0:L], func=AF.Exp), reads=["nm_bc"], writes=[enk])
            ad, adk = TF()
            P.op("act", lambda e, ad=ad, Dn=Dn: e.activation(out=ad[:, 0:L], in_=Dn[:, 0:L], func=AF.Abs), reads=[Dk], writes=[adk])
            P.op("dve", lambda e, en=en, ad=ad: e.tensor_tensor(out=en[:, 0:L], in0=ad[:, 0:L], in1=en[:, 0:L], op=ALU.max),
                 reads=[adk, enk], writes=[enk])
            P.op("dve", lambda e, en=en: e.reciprocal(out=en[:, 0:L], in_=en[:, 0:L]), reads=[enk], writes=[enk])
            hr, hrk = TF()
            P.op("dve", lambda e, en=en, hr=hr, N_=N_: e.tensor_tensor(out=hr[:, 0:L], in0=N_[:, 0:L], in1=en[:, 0:L], op=ALU.mult),
                 reads=[Nk, enk], writes=[hrk])
            hs, hsk = TB()
            P.op("act", lambda e, hr=hr, hs=hs: e.activation(out=hs[:, 0:L], in_=hr[:, 0:L], func=AF.Square), reads=[hrk], writes=[hsk])
            M_, Mk = PS()
            P.op("pe", lambda e, M_=M_, hs=hs: e.matmul(M_[:, 0:L], lhsT=blk64[:], rhs=hs[:, 0:L], start=True, stop=True), reads=["blk64", hsk], writes=[Mk])
            rs, rsk = TF()
            P.op("act", lambda e, rs=rs, M_=M_: e.activation(out=rs[:, 0:L], in_=M_[:, 0:L], func=AF.Sqrt, bias=epsc[:, 0:1], scale=1.0), reads=[Mk, "epsc"], writes=[rsk])
            P.op("dve", lambda e, rs=rs: e.reciprocal(out=rs[:, 0:L], in_=rs[:, 0:L]), reads=[rsk], writes=[rsk])
            P.op("dve", lambda e, rs=rs, hr=hr: e.tensor_tensor(out=hr[:, 0:L], in0=hr[:, 0:L], in1=rs[:, 0:L], op=ALU.mult), reads=[hrk, rsk], writes=[hrk])
            P.op("dve", lambda e, hr=hr, p=p: e.scalar_tensor_tensor(out=ybT[1][:, p, c0:c0 + L], in0=hr[:, 0:L], scalar=hgcol[:, l, p:p + 1],
                                                                      in1=moS[:, p, c0:c0 + L], op0=ALU.mult, op1=ALU.mult),
                 reads=[hrk, "hgcol", "moS"], writes=["ybT1"])
            Cn, Ck = PS()
            P.op("pe", lambda e, Cn=Cn, p=p: e.matmul(Cn[:, 0:129], lhsT=kw[0:L, p * 128:(p + 1) * 128], rhs=mv_ext[0:L, ci, p, 0:129], start=True, stop=True),
                 reads=["kw", "mv_ext"], writes=[Ck])
            P.op("act", lambda e, p=p: e.activation(out=decay[:, p:p + 1], in_=rp_bc[:, p, L - 1:L], func=AF.Exp, bias=nrprev[:, p:p + 1], scale=1.0),
                 reads=["rp_bc", "nrprev"], writes=["decay"])
            tc_, tck = TF()
            P.op("dve", lambda e, tc_=tc_, Cn=Cn: e.tensor_tensor(out=tc_[:, 0:128], in0=Cn[:, 0:128], in1=BDf[:], op=ALU.mult), reads=[Ck, "BDf"], writes=[tck])
            P.op("dve", lambda e, tc_=tc_, p=p: e.scalar_tensor_tensor(out=Cpad[si][:, p, :], in0=Cpad[si][:, p, :], scalar=decay[:, p:p + 1],
                                                                       in1=tc_[:, 0:128], op0=ALU.mult, op1=ALU.add),
                 reads=[tck, "decay", st + "C"], writes=[st + "C"])
            P.op("act", lambda e, p=p: e.activation(out=Cb[si][:, p, :], in_=Cpad[si][:, p, :], func=AF.Identity), reads=[st + "C"], writes=[st + "Cb"])
            P.op("dve", lambda e, Cn=Cn, p=p: e.scalar_tensor_tensor(out=ncol[si][:, p:p + 1], in0=ncol[si][:, p:p + 1], scalar=decay[:, p:p + 1],
                                                                     in1=Cn[:, 128:129], op0=ALU.mult, op1=ALU.add),
                 reads=[Ck, "decay", st + "n"], writes=[st + "n"])
            P.op("dve", lambda e, p=p: e.tensor_scalar(out=Npad[si][:, p, :], in0=BDf[:], scalar1=ncol[si][:, p:p + 1], scalar2=None, op0=ALU.mult),
                 reads=[st + "n", "BDf"], writes=[st + "Np"])

    def mlstm_state_init(si, zero, l=None, b=None):
        st = f"mst{si}"
        P.op("dve", lambda e: e.memset(Cpad[si][:], 0.0), writes=[st + "C"])
        P.op("dve", lambda e: e.memset(ncol[si][:], 0.0), writes=[st + "n"])
        if zero:
            P.op("dve", lambda e: e.memset(m8[si][:], 0.0), writes=[st + "m"])
        else:
            for h in range(8):
                po = (h % 2) * 64
                P.dma("sp", lambda e, h=h, po=po: e.dma_start(out=Cpad[si][po:po + 64, h // 2, po:po + 64], in_=sC[l, b, h]), writes=[st + "C"])
                P.dma("sp", lambda e, h=h, po=po: e.dma_start(out=ncol[si][po:po + 64, h // 2:h // 2 + 1], in_=sn[l, b, h].rearrange("(d o) -> d o", o=1)),
                      writes=[st + "n"])
            P.dma("sp", lambda e: e.dma_start(out=m8[si][:], in_=sm[l, b].rearrange("(h o) -> h o", o=1)), writes=[st + "m"])
        P.op("act", lambda e: e.activation(out=Cb[si][:], in_=Cpad[si][:], func=AF.Identity), reads=[st + "C"], writes=[st + "Cb"])
        for p in range(4):
            P.op("dve", lambda e, p=p: e.tensor_scalar(out=Npad[si][:, p, :], in0=BDf[:], scalar1=ncol[si][:, p:p + 1], scalar2=None, op0=ALU.mult),
                 reads=[st + "n", "BDf"], writes=[st + "Np"])

    def mlstm_state_store(si, oC, on, om):
        st = f"mst{si}"
        for h in range(8):
            po = (h % 2) * 64
            P.dma("sp", lambda e, h=h, po=po: e.dma_start(out=oC[h], in_=Cpad[si][po:po + 64, h // 2, po:po + 64]), reads=[st + "C"])
            P.dma("sp", lambda e, h=h, po=po: e.dma_start(out=on[h].rearrange("(d o) -> d o", o=1), in_=ncol[si][po:po + 64, h // 2:h // 2 + 1]),
                  reads=[st + "n"])
        P.dma("sp", lambda e: e.dma_start(out=om.rearrange("(h o) -> h o", o=1), in_=m8[si][:]), reads=[st + "m"])

    def run_st(kind, t0, NT, chunks, segs):
        begin_pass()
        xsrc = xp if kind == "p" else xs
        ydst = yp if kind == "p" else ys
        ntile = (NT + 127) // 128
        for t in range(ntile):
            pt_ = min(128, NT - t * 128)
            xi = rot["xin"].next()
            P.dma("sp", lambda e, t=t, pt_=pt_, xi=xi: e.dma_start(out=xin[xi][0:pt_, :], in_=xsrc[t0 + t * 128:t0 + t * 128 + pt_, :]), writes=[f"xin{xi}"])
            DBG = ""
            for g in range(2):
                if DBG == "a":
                    break
                pst, pk = PS()
                for kk in range(4):
                    k = g * 4 + kk
                    P.op("pe", lambda e, k=k, kk=kk, pst=pst, xi=xi, pt_=pt_: e.transpose(out=pst[:, kk * 128:kk * 128 + pt_], in_=xin[xi][0:pt_, k * 128:(k + 1) * 128],
                                                                                             identity=ident[0:pt_, 0:pt_]),
                         reads=[f"xin{xi}", "ident"], writes=[pk])
                if DBG == "b":
                    continue
                if DBG == "c":
                    P.op("dve", lambda e, g=g, pst=pst, t=t, pt_=pt_: e.tensor_copy(out=xT[:, g * 4:(g + 1) * 4, t * 128:t * 128 + pt_],
                                                                                    in_=pst[:, :].rearrange("p (k n) -> p k n", n=128)[:, :, 0:pt_]),
                         reads=[pk], writes=["xT"])
                    continue
                if DBG == "d":
                    P.op("act", lambda e, g=g, pst=pst, t=t, pt_=pt_: e.activation(out=xT[:, g * 4:(g + 1) * 4, t * 128:t * 128 + pt_],
                                                                                    in_=pst[:, :].rearrange("p (k n) -> p k n", n=128)[:, :, 0:pt_], func=AF.Identity),
                         reads=[pk], writes=["xT"])
                    continue
                if DBG == "e":
                    for kk in range(4):
                        P.op("act", lambda e, g=g, kk=kk, pst=pst, t=t, pt_=pt_: e.activation(out=xT[:, g * 4 + kk, t * 128:t * 128 + pt_],
                                                                                        in_=pst[:, kk * 128:kk * 128 + pt_], func=AF.Identity),
                             reads=[pk], writes=["xT"])
                    continue
                P.op("act", lambda e, g=g, pst=pst, t=t, pt_=pt_: e.activation(out=xT[:, g * 4:(g + 1) * 4, t * 128:t * 128 + pt_],
                                                                                in_=pst[:, :].rearrange("p (k n) -> p k n", n=128)[:, :, 0:pt_], func=AF.Identity),
                     reads=[pk], writes=["xT"])
        _ck(1)
        for l in range(DEPTH):
            run_layer(kind, t0, NT, chunks, segs, l)
        final_store(kind, t0, NT, ntile, ydst)

    def run_layer(kind, t0, NT, chunks, segs, l):
        if True:
            rmsnorm(3 * l, NT)
            ffn(l, w_up1, w_dn1, NT)
            _ck(2)
            rmsnorm(3 * l + 1, NT)
            def ev_bf(dst, dkey, scale=None):
                def f(i, pst, pkey):
                    if scale is None:
                        P.op("act", lambda e: e.activation(out=dst[:, i, 0:NT], in_=pst[:, 0:NT], func=AF.Identity), reads=[pkey], writes=[dkey])
                    else:
                        P.op("act", lambda e: e.activation(out=dst[:, i, 0:NT], in_=pst[:, 0:NT], func=AF.Identity, scale=scale), reads=[pkey], writes=[dkey])
                return f
            proj_fm(l, OFF["sq"], 4, NT, ev_bf(sqT, "sqT"))
            proj_fm(l, OFF["sk"], 4, NT, ev_bf(skT, "skT"))
            proj_fm(l, OFF["mq"], 4, NT, ev_bf(mqT, "mqT"))
            proj_fm(l, OFF["mk"], 4, NT, ev_bf(mkT, "mkT", 0.125))
            _ck(21)
            proj_fm(l, OFF["mo"], 4, NT, lambda i, pst, pkey: P.op("act", lambda e: e.activation(out=moS[:, i, 0:NT], in_=pst[:, 0:NT], func=AF.Sigmoid),
                                                                   reads=[pkey], writes=["moS"]))
            proj_fm(l, OFF["cb"], 4, NT, lambda i, pst, pkey: P.op("act", lambda e: e.activation(out=cbT[:, i, 0:NT], in_=pst[:, 0:NT], func=AF.Identity),
                                                                   reads=[pkey], writes=["cbT"]))
            _ck(22)
            for i in range(4):
                pv, pk = wpiece([w_in[l][:, OFF["cc"] + i * 128:OFF["cc"] + (i + 1) * 128], w_in[l][:, OFF["ch"] + i * 128:OFF["ch"] + (i + 1) * 128]])
                pc, pck = PS()
                mm_fm(pc, pck, pv, pk, 0, KC, 0, 128, xnT, "xnT", NT)
                ph, phk = PS()
                mm_fm(ph, phk, pv, pk, KC, KC, 0, 128, xnT, "xnT", NT)
                tf, tfk = TF()
                P.op("act", lambda e, tf=tf, pc=pc: e.activation(out=tf[:, 0:NT], in_=pc[:, 0:NT], func=AF.Identity), reads=[pck], writes=[tfk])
                if kind == "p":
                    P.op("dve", lambda e, tf=tf, ph=ph, i=i: e.tensor_tensor(out=uext_p[l][:, i, 2:2 + NT], in0=tf[:, 0:NT], in1=ph[:, 0:NT], op=ALU.mult),
                         reads=[tfk, phk], writes=["uext_1"])
                else:
                    P.op("dve", lambda e, tf=tf, ph=ph: e.tensor_tensor(out=tf[:, 0:NT], in0=tf[:, 0:NT], in1=ph[:, 0:NT], op=ALU.mult),
                         reads=[tfk, phk], writes=[tfk])
                    P.op("dve", lambda e, tf=tf, i=i: e.tensor_copy(out=cbT[:, i, NT:2 * NT], in_=tf[:, 0:NT]), reads=[tfk], writes=["cbT"])
            _ck(23)
            pv, pk = wpiece([w_in[l][:, OFF["mi"]:OFF["mi"] + 16]])
            pi_, pik = PS()
            mm_fm(pi_, pik, pv, pk, 0, KC, 0, 8, xnT, "xnT", NT)
            pf_, pfk = PS()
            mm_fm(pf_, pfk, pv, pk, 0, KC, 8, 8, xnT, "xnT", NT)
            P.op("dve", lambda e, pi_=pi_: e.tensor_scalar(out=liT[:, 0:NT], in0=pi_[0:8, 0:NT], scalar1=b8[:, l, 0:1], scalar2=None, op0=ALU.add),
                 reads=[pik, "b8"], writes=["liT"])
            P.op("act", lambda e, pf_=pf_: e.activation(out=lfT[:, 0:NT], in_=pf_[0:8, 0:NT], func=AF.Exp, bias=b8[:, l, 2:3], scale=-1.0),
                 reads=[pfk, "b8"], writes=["lfT"])
            P.op("act", lambda e: e.activation(out=lfT[:, 0:NT], in_=lfT[:, 0:NT], func=AF.Ln, bias=1.0, scale=1.0), reads=["lfT"], writes=["lfT"])
            P.op("dve", lambda e: e.tensor_scalar(out=lfT[:, 0:NT], in0=lfT[:, 0:NT], scalar1=-1.0, scalar2=None, op0=ALU.mult), reads=["lfT"], writes=["lfT"])
            _ck(24)
            kout = kp if kind == "p" else ks
            vout = vp if kind == "p" else vs

            def ev_out(dst):
                def f(ci, half, pst, pkey):
                    c0, L = chunks[ci]
                    tf, tfk = TF()
                    P.op("act", lambda e: e.activation(out=tf[0:L, 0:256], in_=pst[0:L, 0:256], func=AF.Identity), reads=[pkey], writes=[tfk])
                    P.dma("sp", lambda e: e.dma_start(out=dst[l, t0 + c0:t0 + c0 + L, half * 256:(half + 1) * 256], in_=tf[0:L, 0:256]), reads=[tfk])
                return f
            proj_tm(l, OFF["sk"], chunks, ev_out(kout))

            def ev_v(ci, half, pst, pkey):
                c0, L = chunks[ci]
                ev_out(vout)(ci, half, pst, pkey)
                P.op("dve", lambda e: e.tensor_copy(out=sv_tok[0:L, ci, half * 256:(half + 1) * 256], in_=pst[0:L, 0:256]), reads=[pkey], writes=["sv_tok"])
            _ck(25)
            proj_tm(l, OFF["sv"], chunks, ev_v)
            _ck(26)
            proj_tm(l, OFF["mk"], chunks, lambda ci, half, pst, pkey: P.op(
                "act", lambda e: e.activation(out=mk_tok[0:chunks[ci][1], ci, half * 256:(half + 1) * 256], in_=pst[0:chunks[ci][1], 0:256], func=AF.Identity, scale=0.125),
                reads=[pkey], writes=["mk_tok"]))
            proj_tm(l, OFF["mv"], chunks, lambda ci, half, pst, pkey: P.op(
                "dve", lambda e: e.tensor_copy(out=mv_ext[0:chunks[ci][1], ci, half * 2:(half + 1) * 2, 0:128],
                                               in_=pst[0:chunks[ci][1], 0:256].rearrange("p (a n) -> p a n", n=128)),
                reads=[pkey], writes=["mv_ext"]))
            _ck(3)
            if kind == "p":
                for ci, (c0, L) in enumerate(chunks):
                    blocks = [dict(kt=skT[:, :, c0:c0 + L], v=sv_tok[:, ci, :], Lk=L, diag=True, kkey="skT", vkey="sv_tok")]
                    for cj in range(ci - 1, -1, -1):
                        cc0, LL = chunks[cj]
                        blocks.append(dict(kt=skT[:, :, cc0:cc0 + LL], v=sv_tok[:, cj, :], Lk=LL, diag=False, kkey="skT", vkey="sv_tok"))
                    for bk in range(t0 // 128 - 1, -1, -1):
                        blocks.append(dict(cache=bk, Lk=128, diag=False))
                    attention(l, c0, L, blocks, NT, kv_pre(l, None))
                for ci, (c0, L) in enumerate(chunks):
                    if L == 128:
                        bk = (t0 + c0) // 128
                        P.dma("sp", lambda e, bk=bk, c0=c0: e.dma_start(out=ktc[l, bk], in_=skT[:, :, c0:c0 + 128]), reads=["skT"], writes=[f"kvc{l}"])
                        P.dma("sp", lambda e, bk=bk, ci=ci: e.dma_start(out=vc[l, bk], in_=sv_tok[:, ci, :]), reads=["sv_tok"], writes=[f"kvc{l}"])
            else:
                for (s0, Ls, cis, b) in segs:
                    blocks = [dict(kt=skT[:, :, s0:s0 + Ls], v=sv_tok[:, cis[0], :], Lk=Ls, diag=True, kkey="skT", vkey="sv_tok")]
                    for pg in range(NPG - 1, -1, -1):
                        blocks.append(dict(page=pg, Lk=128, diag=False))
                    attention(l, s0, Ls, blocks, NT, kv_pre(l, b))
            _ck(4)
            for (s0, Ls, cis, b) in segs:
                si = l if kind == "p" else DEPTH
                st = f"mst{si}"
                if kind == "s":
                    mlstm_state_init(si, False, l, b)
                elif t0 == 0:
                    mlstm_state_init(si, True)
                P.op("dve", lambda e, s0=s0, Ls=Ls, si=si: e.tensor_tensor_scan(out=mT[:, s0:s0 + Ls], data0=lfT[:, s0:s0 + Ls], data1=liT[:, s0:s0 + Ls],
                                                                               initial=m8[si][:, 0:1], op0=ALU.add, op1=ALU.max),
                     reads=["lfT", "liT", st + "m"], writes=["mT"])
                P.op("dve", lambda e, s0=s0, Ls=Ls: e.tensor_tensor_scan(out=bT[:, s0:s0 + Ls], data0=lfT[:, s0:s0 + Ls], data1=zeros8[:, s0:s0 + Ls],
                                                                        initial=0.0, op0=ALU.add, op1=ALU.add),
                     reads=["lfT", "zeros8"], writes=["bT"])
                P.op("dve", lambda e, s0=s0, Ls=Ls: e.tensor_tensor(out=liT[:, s0:s0 + Ls], in0=liT[:, s0:s0 + Ls], in1=bT[:, s0:s0 + Ls], op=ALU.subtract),
                     reads=["liT", "bT"], writes=["liT"])
                P.op("dve", lambda e, s0=s0, Ls=Ls: e.tensor_tensor(out=bT[:, s0:s0 + Ls], in0=bT[:, s0:s0 + Ls], in1=mT[:, s0:s0 + Ls], op=ALU.subtract),
                     reads=["mT", "bT"], writes=["bT"])
                P.op("dve", lambda e, s0=s0, Ls=Ls: e.tensor_scalar(out=mT[:, s0:s0 + Ls], in0=mT[:, s0:s0 + Ls], scalar1=-1.0, scalar2=None, op0=ALU.mult),
                     reads=["mT"], writes=["mT"])
                first = True
                for ci in cis:
                    c0, L = chunks[ci]
                    mlstm_chunk(l, si, c0, L, ci, first)
                    first = False
                P.op("dve", lambda e, s0=s0, Ls=Ls, si=si: e.tensor_scalar(out=m8[si][:], in0=mT[:, s0 + Ls - 1:s0 + Ls], scalar1=-1.0, scalar2=None, op0=ALU.mult),
                     reads=["mT"], writes=[st + "m"])
                if kind == "s":
                    mlstm_state_store(si, Cs[l, b], ns_[l, b], ms_[l, b])
            _ck(5)
            for (s0, Ls, cis, b) in segs:
                if kind == "p":
                    ue = uext_1; uk = "uext_1"
                    if t0 == 0:
                        P.op("dve", lambda e, ue=ue: e.memset(ue[:, :, 0:2], 0.0), writes=[uk])
                    else:
                        P.op("dve", lambda e, ue=ue: e.tensor_copy(out=ue[:, :, 0:2], in_=halo[l][:]), reads=[f"halo{l}"], writes=[uk])
                else:
                    ue = uext_s; uk = "uext_s"
                    for tt in range(2):
                        P.dma("sp", lambda e, b=b, tt=tt: e.dma_start(out=ue[:, :, tt], in_=scv[l, b, tt].rearrange("(k p) -> p k", p=128), allow_slow_non_contiguous=True),
                              writes=[uk])
                    P.op("dve", lambda e, s0=s0, Ls=Ls: e.tensor_copy(out=ue[:, :, 2:2 + Ls], in_=cbT[:, :, NT + s0:NT + s0 + Ls]), reads=["cbT"], writes=[uk])
                for k in range(4):
                    tf, tfk = TF()
                    P.op("dve", lambda e, tf=tf, k=k, ue=ue, Ls=Ls: e.tensor_scalar(out=tf[:, 0:Ls], in0=ue[:, k, 0:Ls], scalar1=cwcol[:, l, k, 0:1], scalar2=None, op0=ALU.mult),
                         reads=[uk, "cwcol"], writes=[tfk])
                    for j in (1, 2):
                        P.op("dve", lambda e, tf=tf, k=k, j=j, ue=ue, Ls=Ls: e.scalar_tensor_tensor(out=tf[:, 0:Ls], in0=ue[:, k, j:j + Ls], scalar=cwcol[:, l, k, j:j + 1],
                                                                                                    in1=tf[:, 0:Ls], op0=ALU.mult, op1=ALU.add),
                             reads=[uk, "cwcol", tfk], writes=[tfk])
                    P.op("dve", lambda e, tf=tf, k=k, s0=s0, Ls=Ls: e.tensor_tensor(out=ybT[2][:, k, s0:s0 + Ls], in0=tf[:, 0:Ls], in1=cbT[:, k, s0:s0 + Ls], op=ALU.mult),
                         reads=[tfk, "cbT"], writes=["ybT2"])
                last_p = (kind == "p" and t0 + NT == TP)
                if kind == "s" or last_p:
                    dst = cvs[l, b] if kind == "s" else cvp[l]
                    for tt in range(2):
                        P.dma("sp", lambda e, dst=dst, ue=ue, Ls=Ls, tt=tt: e.dma_start(out=dst[tt].rearrange("(k p) -> p k", p=128), in_=ue[:, :, Ls + tt],
                                                                                       allow_slow_non_contiguous=True), reads=[uk])
                if kind == "p" and not last_p:
                    P.op("dve", lambda e, ue=ue, Ls=Ls: e.tensor_copy(out=halo[l][:], in_=ue[:, :, Ls:Ls + 2]), reads=[uk], writes=[f"halo{l}"])
            _ck(6)
            for c in range(KC):
                acc, acck = TF()
                for j in range(3):
                    pv, pk = wpiece([w_in[l][:, OFF["g"] + j * D + c * 128: OFF["g"] + j * D + (c + 1) * 128], w_br[j][l][:, c * 128:(c + 1) * 128]])
                    pg, pgk = PS()
                    mm_fm(pg, pgk, pv, pk, 0, KC, 0, 128, xnT, "xnT", NT)
                    pb, pbk = PS()
                    mm_fm(pb, pbk, pv, pk, KC, 4, 0, 128, ybT[j], f"ybT{j}", NT)
                    sg, sgk = TF()
                    P.op("act", lambda e, sg=sg, pg=pg: e.activation(out=sg[:, 0:NT], in_=pg[:, 0:NT], func=AF.Sigmoid), reads=[pgk], writes=[sgk])
                    if j == 0:
                        P.op("dve", lambda e, sg=sg, pb=pb, acc=acc: e.tensor_tensor(out=acc[:, 0:NT], in0=sg[:, 0:NT], in1=pb[:, 0:NT], op=ALU.mult),
                             reads=[sgk, pbk], writes=[acck])
                    else:
                        P.op("dve", lambda e, sg=sg, pb=pb: e.tensor_tensor(out=sg[:, 0:NT], in0=sg[:, 0:NT], in1=pb[:, 0:NT], op=ALU.mult),
                             reads=[sgk, pbk], writes=[sgk])
                        if j == 1:
                            P.op("dve", lambda e, sg=sg, acc=acc: e.tensor_tensor(out=acc[:, 0:NT], in0=acc[:, 0:NT], in1=sg[:, 0:NT], op=ALU.add),
                                 reads=[sgk, acck], writes=[acck])
                        else:
                            P.op("dve", lambda e, sg=sg, acc=acc, c=c: e.tensor_tensor(out=mergedT[:, c, 0:NT], in0=acc[:, 0:NT], in1=sg[:, 0:NT], op=ALU.add),
                                 reads=[sgk, acck], writes=["hT"])
            for c in range(KC):
                pv, pk = wpiece([w_o[l][:, c * 128:(c + 1) * 128]])
                po, pok = PS()
                mm_fm(po, pok, pv, pk, 0, KC, 0, 128, mergedT, "hT", NT)
                P.op("dve", lambda e, c=c, po=po: e.tensor_tensor(out=xT[:, c, 0:NT], in0=po[:, 0:NT], in1=xT[:, c, 0:NT], op=ALU.add),
                     reads=[pok, "xT"], writes=["xT"])
            _ck(7)
            rmsnorm(3 * l + 2, NT)
            ffn(l, w_up2, w_dn2, NT)
    def final_store(kind, t0, NT, ntile, ydst):
        rmsnorm(3 * DEPTH, NT, inplace=True)
        for t in range(ntile):
            pt_ = min(128, NT - t * 128)
            xi = rot["xin"].next()
            for g in range(2):
                pst, pk = PS()
                for kk in range(4):
                    k = g * 4 + kk
                    P.op("pe", lambda e, k=k, kk=kk, pst=pst, t=t, pt_=pt_: e.transpose(out=pst[0:pt_, kk * 128:(kk + 1) * 128], in_=xT[:, k, t * 128:t * 128 + pt_],
                                                                                         identity=ident[:]),
                         reads=["xT", "ident"], writes=[pk])
                P.op("act", lambda e, g=g, pst=pst, xi=xi, pt_=pt_: e.activation(out=xin[xi][0:pt_, g * 512:(g + 1) * 512], in_=pst[0:pt_, 0:512], func=AF.Identity),
                     reads=[pk], writes=[f"xin{xi}"])
            P.dma("sp", lambda e, t=t, pt_=pt_, xi=xi: e.dma_start(out=ydst[t0 + t * 128:t0 + t * 128 + pt_, :], in_=xin[xi][0:pt_, :]), reads=[f"xin{xi}"])

    def _sched():
        t0 = 0
        while t0 < TP:
            NT = min(512, TP - t0)
            chunks = [(c, min(128, NT - c)) for c in range(0, NT, 128)]
            run_st("p", t0, NT, chunks, [(0, NT, list(range(len(chunks))), None)])
            t0 += NT
        for l in range(DEPTH):
            mlstm_state_store(l, Cp[l], np_[l], mp[l])
        chunks = [(s * DS, DS) for s in range(NSEQ)]
        run_st("s", 0, NS, chunks, [(s * DS, DS, [s], s) for s in range(NSEQ)])

    try:
        _sched()
    except _Stop:
        print('STOPPED early; ops:', len(P.ops))
    P.emit()
    return nc


_NC_CACHE = {}


def _run(inp, n_cores):
    f32 = np.float32
    xpr = np.asarray(inp["x_prompt"], f32); xsa = np.asarray(inp["x_sample"], f32)
    B, SEQ, _ = xpr.shape
    DB, DS_, _ = xsa.shape
    ckk = np.asarray(inp["cache_sb_k"], f32); cvv = np.asarray(inp["cache_sb_v"], f32)
    DEPTH, NPHYS = ckk.shape[0], ckk.shape[1]
    ptab = np.asarray(inp["page_table"], np.int32)
    NPG = ptab.shape[1]
    NSEQ = DB // n_cores
    TP = SEQ + 16
    cfg = dict(TP=TP, NSEQ=NSEQ, NPG=NPG, NPHYS=NPHYS, DEPTH=DEPTH)
    key = tuple(sorted(cfg.items()))
    if key not in _NC_CACHE:
        _NC_CACHE[key] = build(cfg)
    nc = _NC_CACHE[key]
    meta = np.asarray(inp["meta_tokens"], f32)
    ck2 = np.ascontiguousarray(ckk.reshape(DEPTH * NPHYS * 128, 512)); cv2 = np.ascontiguousarray(cvv.reshape(DEPTH * NPHYS * 128, 512))
    A = lambda k: np.ascontiguousarray(np.asarray(inp[k], f32))
    common = dict(ck=ck2, cv=cv2, g1=A("norm_ff1"), up1=A("ffn1_w_up"), dn1=A("ffn1_w_down"), gm=A("norm_mix"), win=A("w_in"),
                  sbb=A("sb_logit_bias"), ib=A("ml_igate_bias"), fb=A("ml_fgate_bias"), hg=A("ml_head_norm"), cw=A("conv_w"),
                  brs=A("w_branch_sb"), brm=A("w_branch_ml"), brc=A("w_branch_cv"), wo=A("w_out"), g2=A("norm_ff2"),
                  up2=A("ffn2_w_up"), dn2=A("ffn2_w_down"), gf=A("norm_final"))
    sC = A("state_ml_C"); sn = A("state_ml_n"); sm = A("state_ml_m"); scv = A("state_conv")
    in_maps = []
    for c in range(n_cores):
        if c < B:
            xpc = np.concatenate([meta, xpr[c]], axis=0)
        else:
            xpc = np.zeros((TP, D), f32)
        sl = slice(c * NSEQ, (c + 1) * NSEQ)
        m = dict(common)
        m.update(xp=np.ascontiguousarray(xpc), xs=np.ascontiguousarray(xsa[sl].reshape(NSEQ * DS_, D)), pt=np.ascontiguousarray(ptab[sl]),
                 sC=np.ascontiguousarray(sC[:, sl]), sn=np.ascontiguousarray(sn[:, sl]), sm=np.ascontiguousarray(sm[:, sl]),
                 scv=np.ascontiguousarray(scv[:, sl]))
        in_maps.append(m)
    res = run_bass_kernel_spmd(nc, in_maps, core_ids=list(range(n_cores))).results
    R = lambda k, cs: [np.asarray(res[c][k], f32) for c in cs]
    pb = list(range(B)); ac = list(range(n_cores))
    y_prompt = np.stack([r[16:] for r in R("yp", pb)])
    y_sample = np.concatenate([r.reshape(NSEQ, DS_, D) for r in R("ys", ac)])
    kpo = np.stack(R("kp", pb), axis=1).reshape(DEPTH, B, TP, 8, 64)
    vpo = np.stack(R("vp", pb), axis=1).reshape(DEPTH, B, TP, 8, 64)
    Cpo = np.stack(R("Cp", pb), axis=1); npo = np.stack(R("np", pb), axis=1); mpo = np.stack(R("mp", pb), axis=1)
    cvpo = np.stack(R("cvp", pb), axis=1)
    kso = np.concatenate([r.reshape(DEPTH, NSEQ, DS_, 8, 64) for r in R("ks", ac)], axis=1)
    vso = np.concatenate([r.reshape(DEPTH, NSEQ, DS_, 8, 64) for r in R("vs", ac)], axis=1)
    Cso = np.concatenate(R("Cs", ac), axis=1); nso = np.concatenate(R("ns", ac), axis=1); mso = np.concatenate(R("ms", ac), axis=1)
    cvso = np.concatenate(R("cvs", ac), axis=1)
    return (y_prompt, y_sample, kpo, vpo, Cpo, npo, mpo, cvpo, kso, vso, Cso, nso, mso, cvso)


def kernel(**inputs):
    return _run(inputs, 8)
```

```python
import numpy as np
import concourse.bass as bass
import concourse.mybir as mybir
from concourse.bass_utils import run_bass_kernel_spmd

F32 = mybir.dt.float32
BF16 = mybir.dt.bfloat16
I32 = mybir.dt.int32
AF = mybir.ActivationFunctionType
ALU = mybir.AluOpType

COMPUTE = ("pe", "act", "dve", "pool")
OWN_SYNC = True
NROT = 8
D = 1024
KC = 8
DFF = 2816
NFF = 22
HW = 512
DIN = 8208
EPS = 1e-6
OFF = dict(sq=0, sk=512, sv=1024, mq=1536, mk=2048, mv=2560, mi=3072, mf=3080, mo=3088, cb=3600, cc=4112, ch=4624, g=5136)


class Prog:
    def __init__(self, nc):
        self.nc = nc
        self.ops = []
        self.cnt = {e: 0 for e in COMPUTE}
        self.dcnt = {"sp": 0, "pool": 0, "act": 0}
        self.lastw = {}
        self.readers = {}

    def _deps(self, reads, writes):
        deps = []
        for k in reads:
            if k in self.lastw:
                deps.append(self.lastw[k])
        for k in writes:
            if k in self.lastw:
                deps.append(self.lastw[k])
            deps.extend(self.readers.get(k, ()))
        return deps

    def _commit(self, tok, reads, writes):
        for k in reads:
            self.readers.setdefault(k, []).append(tok)
        for k in writes:
            self.lastw[k] = tok
            self.readers[k] = []

    def op(self, eng, fn, reads=(), writes=()):
        writes = list(writes) + [k for k in reads if k.startswith("ps") and k not in writes]
        deps = self._deps(reads, writes)
        self.cnt[eng] += 1
        tok = ("c", eng, self.cnt[eng])
        self.ops.append((eng, fn, deps, tok))
        self._commit(tok, reads, writes)

    def dma(self, eng, fn, reads=(), writes=()):
        deps = self._deps(reads, writes)
        k = self.dcnt[eng]
        self.dcnt[eng] += 1
        tok = ("d", eng, k % NROT, 16 * (k // NROT + 1))
        if k >= NROT:
            deps.append(("d", eng, k % NROT, 16 * (k // NROT)))
        self.ops.append((eng, fn, deps, tok))
        self._commit(tok, reads, writes)

    def emit(self, final_wait_eng="sp"):
        nc = self.nc
        sem_c = {e: nc.alloc_semaphore(f"c_{e}") for e in COMPUTE}
        sem_d = {e: [nc.alloc_semaphore(f"d_{e}{i}") for i in range(NROT)] for e in self.dcnt}
        streams = {"pe": [], "act": [], "dve": [], "pool": [], "sp": []}
        for o in self.ops:
            streams[o[0]].append(o)

        def run(ename, e):
            waited = {}
            for (_, fn, deps, tok) in streams[ename]:
                need = {}
                for d in deps:
                    if d[0] == "c":
                        if d[1] == ename and (ename == "pe" or not OWN_SYNC):
                            continue
                        key = ("c", d[1]); val = d[2]
                    else:
                        key = ("d", d[1], d[2]); val = d[3]
                    if waited.get(key, 0) >= val:
                        continue
                    need[key] = max(need.get(key, 0), val)
                for key, val in need.items():
                    s = sem_c[key[1]] if key[0] == "c" else sem_d[key[1]][key[2]]
                    e.wait_ge(s, val)
                    waited[key] = val
                ins = fn(e)
                if tok[0] == "c":
                    ins.then_inc(sem_c[tok[1]], 1)
                else:
                    ins.then_inc(sem_d[tok[1]][tok[2]], 16)
            if ename == final_wait_eng:
                for en in COMPUTE:
                    if self.cnt[en]:
                        e.wait_ge(sem_c[en], self.cnt[en])
                for en, k in self.dcnt.items():
                    for i in range(NROT):
                        n = (k - i + NROT - 1) // NROT if k > i else 0
                        if n:
                            e.wait_ge(sem_d[en][i], 16 * n)

        with nc.Block() as block:
            @block.tensor
            def _(e): run("pe", e)

            @block.scalar
            def _(e): run("act", e)

            @block.vector
            def _(e): run("dve", e)

            @block.gpsimd
            def _(e): run("pool", e)

            @block.sync
            def _(e): run("sp", e)


class _Stop(Exception):
    pass


def _ck(k):
    return None


def _ckq(k):
    return False


class Rot:
    def __init__(self, name, n):
        self.name, self.n, self.i = name, n, 0

    def next(self):
        i = self.i % self.n
        self.i += 1
        return i


def build(cfg):
    TP, NSEQ, NPG, NPHYS, DEPTH = cfg["TP"], cfg["NSEQ"], cfg["NPG"], cfg["NPHYS"], cfg["DEPTH"]
    DS = 8
    NS = NSEQ * DS
    NBLK = (TP + 127) // 128
    nc = bass.Bass("TRN2", target_bir_lowering=False)
    P = Prog(nc)

    def din(name, shape, dt=F32):
        return nc.dram_tensor(name, list(shape), dt, kind="ExternalInput").ap()

    def dout(name, shape, dt=F32):
        return nc.dram_tensor(name, list(shape), dt, kind="ExternalOutput").ap()

    xp = din("xp", [TP, D]); xs = din("xs", [NS, D])
    ck = din("ck", [DEPTH * NPHYS * 128, HW]); cv = din("cv", [DEPTH * NPHYS * 128, HW])
    pt_in = din("pt", [NSEQ, NPG], I32)
    sC = din("sC", [DEPTH, NSEQ, 8, 64, 64]); sn = din("sn", [DEPTH, NSEQ, 8, 64]); sm = din("sm", [DEPTH, NSEQ, 8])
    scv = din("scv", [DEPTH, NSEQ, 2, HW])
    w_g1 = din("g1", [DEPTH, D]); w_up1 = din("up1", [DEPTH, D, 2 * DFF]); w_dn1 = din("dn1", [DEPTH, DFF, D])
    w_gm = din("gm", [DEPTH, D]); w_in = din("win", [DEPTH, D, DIN])
    w_sbb = din("sbb", [DEPTH, 8]); w_ib = din("ib", [DEPTH, 8]); w_fb = din("fb", [DEPTH, 8]); w_hg = din("hg", [DEPTH, HW])
    w_cw = din("cw", [DEPTH, 3, HW])
    w_br = [din("brs", [DEPTH, HW, D]), din("brm", [DEPTH, HW, D]), din("brc", [DEPTH, HW, D])]
    w_o = din("wo", [DEPTH, D, D])
    w_g2 = din("g2", [DEPTH, D]); w_up2 = din("up2", [DEPTH, D, 2 * DFF]); w_dn2 = din("dn2", [DEPTH, DFF, D])
    w_gf = din("gf", [D])
    yp = dout("yp", [TP, D]); ys = dout("ys", [NS, D])
    kp = dout("kp", [DEPTH, TP, HW]); vp = dout("vp", [DEPTH, TP, HW])
    Cp = dout("Cp", [DEPTH, 8, 64, 64]); np_ = dout("np", [DEPTH, 8, 64]); mp = dout("mp", [DEPTH, 8]); cvp = dout("cvp", [DEPTH, 2, HW])
    ks = dout("ks", [DEPTH, NS, HW]); vs = dout("vs", [DEPTH, NS, HW])
    Cs = dout("Cs", [DEPTH, NSEQ, 8, 64, 64]); ns_ = dout("ns", [DEPTH, NSEQ, 8, 64]); ms_ = dout("ms", [DEPTH, NSEQ, 8])
    cvs = dout("cvs", [DEPTH, NSEQ, 2, HW])
    ktc = nc.dram_tensor("ktc", [DEPTH, NBLK, 128, 4, 128], BF16, kind="Internal").ap()
    vc = nc.dram_tensor("vc", [DEPTH, NBLK, 128, HW], BF16, kind="Internal").ap()
    NWC = 170 * DEPTH
    wcache = nc.dram_tensor("wcache", [NWC, 128, 2304], BF16, kind="Internal").ap()

    def sb(name, shape, dt=F32):
        return nc.alloc_sbuf_tensor(name, list(shape), dt)

    NTM = 512
    xT = sb("xT", [128, KC, NTM]); xnT = sb("xnT", [128, KC, NTM], BF16)
    hT = sb("hT", [128, 8, NTM], BF16)
    sqT = sb("sqT", [128, 4, NTM], BF16); skT = sb("skT", [128, 4, NTM], BF16)
    mqT = sb("mqT", [128, 4, NTM], BF16); mkT = sb("mkT", [128, 4, NTM], BF16)
    sv_tok = sb("sv_tok", [128, 4, HW], BF16); mk_tok = sb("mk_tok", [128, 4, HW], BF16)
    mv_ext = sb("mv_ext", [128, 4, 4, 132], BF16)
    moS = sb("moS", [128, 4, NTM], BF16); cbT = sb("cbT", [128, 4, NTM], BF16)
    uext_1 = sb("uext_1", [128, 4, 2 + NTM])
    uext_p = [uext_1 for l in range(DEPTH)]
    halo = [sb(f"halo{l}", [128, 4, 2]) for l in range(DEPTH)]
    uext_s = sb("uext_s", [128, 4, 2 + DS])
    ybT = [sb(f"ybT{j}", [128, 4, NTM], BF16) for j in range(3)]
    mergedT = hT
    liT = sb("liT", [8, NTM]); lfT = sb("lfT", [8, NTM]); mT = sb("mT", [8, NTM]); bT = sb("bT", [8, NTM])
    zeros8 = sb("zeros8", [8, NTM])
    piece = [sb(f"piece{i}", [128, 2304], BF16) for i in range(5)]
    tmpf = [sb(f"tmpf{i}", [128, 512]) for i in range(4)]
    tmpb = [sb(f"tmpb{i}", [128, 512], BF16) for i in range(3)]
    xin = [sb(f"xin{i}", [128, D]) for i in range(1)]
    ident = sb("ident", [128, 128]); ones_f = sb("ones_f", [128, 128])
    onesm = sb("onesm", [128, 128], BF16)
    ones_b = sb("ones_b", [128, 128], BF16)
    blk64 = sb("blk64", [128, 128], BF16)
    BDf = sb("BDf", [128, 128])
    mask_lt = sb("mask_lt", [128, 128])
    mask_le = sb("mask_le", [128, 128])
    UI = sb("UI", [128, 128], BF16)
    CM = sb("CM", [128, 128], BF16)
    zeros_b = sb("zeros_b", [128, 512], BF16)
    selH = sb("selH", [8, 8, 128]); selP = sb("selP", [8, 4, 128])
    gcol = sb("gcol", [128, 3 * DEPTH + 1, KC])
    hgcol = sb("hgcol", [128, DEPTH, 4]); cwcol = sb("cwcol", [128, DEPTH, 4, 3])
    b8 = sb("b8", [8, DEPTH, 4])
    sbb1 = sb("sbb1", [1, DEPTH, 8]); sbhi = sb("sbhi", [1, DEPTH, 8], BF16); sblo = sb("sblo", [1, DEPTH, 8], BF16)
    sbt = sb("sbt", [1, DEPTH, 8])
    bias2 = [sb(f"bias2_{l}", [2, 8 * 128], BF16) for l in range(DEPTH)]
    bias2s = [sb(f"bias2s_{l}", [2, 8 * 16], BF16) for l in range(DEPTH)]
    ones2 = sb("ones2", [2, 128], BF16)
    epsc = sb("epsc", [128, 1])
    blo_tmp = sb("blo_tmp", [1, 1024], BF16)
    Qbd = sb("Qbd", [128, 4, 256], BF16)
    ktb = [sb(f"ktb{i}", [128, 4, 128], BF16) for i in range(3)]
    vb = [sb(f"vb{i}", [128, HW], BF16) for i in range(3)]
    kpg = [sb(f"kpg{i}", [128, HW]) for i in range(1)]; vpg = [sb(f"vpg{i}", [128, HW]) for i in range(1)]
    Et = [sb(f"Et{i}", [128, 512]) for i in range(3)]; Xt = [sb(f"Xt{i}", [128, 512], BF16) for i in range(2)]
    Lt = [sb(f"Lt{i}", [128, 512], BF16) for i in range(3)]; Wt = [sb(f"Wt{i}", [128, 512], BF16) for i in range(2)]
    Cpad = [sb(f"Cpad{i}", [128, 4, 128]) for i in range(DEPTH + 1)]
    Cb = [sb(f"Cb{i}", [128, 4, 128], BF16) for i in range(DEPTH + 1)]
    ncol = [sb(f"ncol{i}", [128, 4]) for i in range(DEPTH + 1)]
    Npad = [sb(f"Npad{i}", [128, 4, 128], BF16) for i in range(DEPTH + 1)]
    m8 = [sb(f"m8_{i}", [8, 1]) for i in range(DEPTH + 1)]
    rp8 = sb("rp8", [8, 1])
    acol = sb("acol", [128, 8]); wend = sb("wend", [128, 8]); rend = sb("rend", [128, 8])
    r_bc = sb("r_bc", [128, 8, 128]); rp_bc = sb("rp_bc", [128, 4, 128]); nm_bc = sb("nm_bc", [128, 4, 128])
    nrprev = sb("nrprev", [128, 4]); decay = sb("decay", [128, 4])
    Dt = sb("Dt", [128, 2, 128]); SWt = sb("SWt", [128, 2, 128], BF16)
    kw = sb("kw", [128, HW], BF16)
    ptb = sb("ptb", [128, NSEQ, NPG], I32); ptf = sb("ptf", [128, NSEQ, NPG]); pidx = sb("pidx", [128, DEPTH, NSEQ, NPG], I32)
    iota_p = sb("iota_p", [128, 1])
    ps = [nc.alloc_psum_tensor(f"ps{i}", [128, 512], F32) for i in range(8)]
    psrot = Rot("ps", 4)
    rot = {n: Rot(n, k) for n, k in [("piece", 5), ("tmpf", 4), ("tmpb", 3), ("xin", 1), ("ktb", 3), ("vb", 3),
                                     ("pg", 1), ("E", 2), ("X", 1), ("L", 2), ("W", 2)]}

    def PS():
        i = psrot.next()
        return ps[i], f"ps{i}"

    def TF():
        i = rot["tmpf"].next()
        return tmpf[i], f"tmpf{i}"

    def TB():
        i = rot["tmpb"].next()
        return tmpb[i], f"tmpb{i}"

    def fin():
        P.emit()
        return nc

    P.op("pool", lambda e: e.memset(ones_f[:], 1.0), writes=["ones_f"])
    P.op("pool", lambda e: e.memset(zeros_b[:], 0.0), writes=["zeros_b"])
    P.op("pool", lambda e: e.memset(zeros8[:], 0.0), writes=["zeros8"])
    P.op("pool", lambda e: e.memset(onesm[:], 1.0 / 1024), writes=["onesm"])
    P.op("pool", lambda e: e.memset(ones_b[:], 1.0), writes=["ones_b"])
    P.op("pool", lambda e: e.memset(ones2[:], 1.0), writes=["ones2"])
    P.op("pool", lambda e: e.memset(epsc[:], EPS), writes=["epsc"])

    def asel(out, pattern, op, cm, key, base=0, src=None, skey="ones_f"):
        src = ones_f[:] if src is None else src
        P.op("pool", lambda e: e.affine_select(out=out, in_=src, pattern=pattern, compare_op=op, fill=0.0, base=base,
                                               channel_multiplier=cm), reads=[skey], writes=[key])
    asel(ident[:], [[-1, 128]], ALU.is_equal, 1, "ident")
    asel(mask_lt[:], [[1, 128]], ALU.is_gt, -1, "mask_lt")
    asel(mask_le[:], [[1, 128]], ALU.is_ge, -1, "mask_le")
    asel(UI[:], [[-1, 128]], ALU.is_ge, 1, "UI")
    asel(CM[:], [[1, 128]], ALU.is_gt, -1, "CM")
    P.op("pool", lambda e: e.memset(BDf[:], 0.0), writes=["BDf"])
    P.op("pool", lambda e: e.memset(BDf[0:64, 0:64], 1.0), writes=["BDf"])
    P.op("pool", lambda e: e.memset(BDf[64:128, 64:128], 1.0), writes=["BDf"])
    P.op("pool", lambda e: e.tensor_scalar(out=blk64[:], in0=BDf[:], scalar1=1.0 / 64, scalar2=None, op0=ALU.mult),
         reads=["BDf"], writes=["blk64"])
    for h in range(8):
        asel(selH[:, h, :], [[0, 128]], ALU.is_equal, 1, "selH", base=-h, src=ones_f[0:8, :])
    for p in range(4):
        asel(selP[:, p, 0:64], [[0, 64]], ALU.is_equal, 1, "selP", base=-2 * p, src=ones_f[0:8, 0:64])
        asel(selP[:, p, 64:128], [[0, 64]], ALU.is_equal, 1, "selP", base=-2 * p - 1, src=ones_f[0:8, 0:64])
    P.op("pool", lambda e: e.iota(iota_p[:], pattern=[[0, 1]], base=0, channel_multiplier=1, allow_small_or_imprecise_dtypes=True),
         writes=["iota_p"])
    if _ckq(-1): return fin()
    gl = []
    for l in range(DEPTH):
        gl += [w_g1[l], w_gm[l], w_g2[l]]
    gl.append(w_gf)
    for i, g in enumerate(gl):
        P.dma("sp", lambda e, i=i, g=g: e.dma_start(out=gcol[:, i, :], in_=g.rearrange("(k p) -> p k", p=128), allow_slow_non_contiguous=True),
              writes=["gcol"])
    for l in range(DEPTH):
        P.dma("sp", lambda e, l=l: e.dma_start(out=hgcol[:, l, :], in_=w_hg[l].rearrange("(k p) -> p k", p=128), allow_slow_non_contiguous=True),
              writes=["hgcol"])
        for j in range(3):
            P.dma("sp", lambda e, l=l, j=j: e.dma_start(out=cwcol[:, l, :, j], in_=w_cw[l, j].rearrange("(k p) -> p k", p=128), allow_slow_non_contiguous=True),
                  writes=["cwcol"])
        P.dma("sp", lambda e, l=l: e.dma_start(out=b8[:, l, 0:1], in_=w_ib[l].rearrange("(h o) -> h o", o=1)), writes=["b8"])
        P.dma("sp", lambda e, l=l: e.dma_start(out=b8[:, l, 1:2], in_=w_fb[l].rearrange("(h o) -> h o", o=1)), writes=["b8"])
        P.dma("sp", lambda e, l=l: e.dma_start(out=sbb1[:, l, :], in_=w_sbb[l].rearrange("(o h) -> o h", o=1)), writes=["sbb1"])
    P.op("dve", lambda e: e.tensor_scalar(out=b8[:, :, 2:3], in0=b8[:, :, 1:2], scalar1=-1.0, scalar2=None, op0=ALU.mult),
         reads=["b8"], writes=["b8"])
    if _ckq(-2): return fin()
    P.op("dve", lambda e: e.tensor_scalar(out=sbt[:], in0=sbb1[:], scalar1=8.0, scalar2=None, op0=ALU.mult), reads=["sbb1"], writes=["sbt"])
    P.op("dve", lambda e: e.tensor_copy(out=sbhi[:], in_=sbt[:]), reads=["sbt"], writes=["sbhi"])
    P.op("dve", lambda e: e.tensor_tensor(out=sbt[:], in0=sbt[:], in1=sbhi[:], op=ALU.subtract), reads=["sbhi", "sbt"], writes=["sbt"])
    P.op("dve", lambda e: e.tensor_copy(out=sblo[:], in_=sbt[:]), reads=["sbt"], writes=["sblo"])
    for l in range(DEPTH):
        for (bt, L) in ((bias2[l], 128), (bias2s[l], 16)):
            v = bt[0:1, :].rearrange("p (h l) -> p h l", l=L)
            P.op("dve", lambda e, v=v, l=l, L=L: e.tensor_copy(out=v, in_=sbhi[0:1, l, :].unsqueeze(2).to_broadcast([1, 8, L])),
                 reads=["sbhi"], writes=[f"bias2_{l}_{L}"])
            tb, tk = blo_tmp, "blo_tmp"
            v2 = tb[0:1, 0:8 * L].rearrange("p (h l) -> p h l", l=L)
            P.op("dve", lambda e, v2=v2, l=l, L=L: e.tensor_copy(out=v2, in_=sblo[0:1, l, :].unsqueeze(2).to_broadcast([1, 8, L])),
                 reads=["sblo"], writes=[tk])
            P.dma("sp", lambda e, bt=bt, tb=tb, L=L: e.dma_start(out=bt[1:2, :], in_=tb[0:1, 0:8 * L]), reads=[tk], writes=[f"bias2_{l}_{L}"])
    if _ckq(-3): return fin()
    P.dma("sp", lambda e: e.dma_start(out=ptb[:].rearrange("p s j -> p (s j)"),
                                      in_=pt_in.rearrange("s j -> (s j)").partition_broadcast(128)), writes=["ptb"])
    P.op("dve", lambda e: e.tensor_copy(out=ptf[:], in_=ptb[:]), reads=["ptb"], writes=["ptf"])
    P.op("dve", lambda e: e.tensor_scalar(out=ptf[:], in0=ptf[:], scalar1=128.0, scalar2=iota_p[:, 0:1], op0=ALU.mult, op1=ALU.add),
         reads=["ptf", "iota_p"], writes=["ptf"])
    for l in range(DEPTH):
        P.op("dve", lambda e, l=l: e.tensor_copy(out=pidx[:, l], in_=ptf[:]), reads=["ptf"], writes=["pidx"])
        P.op("dve", lambda e: e.tensor_scalar(out=ptf[:], in0=ptf[:], scalar1=float(NPHYS * 128), scalar2=None, op0=ALU.add),
             reads=["ptf", "pidx"], writes=["ptf"])
    for l in range(DEPTH):
        P.op("pool", lambda e, l=l: e.memset(mv_ext[:, :, :, 128:132], 1.0), writes=["mv_ext"])

    if _ckq(-4): return fin()
    wstate = dict(pass_idx=-1, pid=0)

    def begin_pass():
        wstate["pass_idx"] += 1
        wstate["pid"] = 0

    def wpiece(srcs):
        n = srcs[0].shape[1]
        kt = sum(s.shape[0] // 128 for s in srcs)
        pid = wstate["pid"]
        wstate["pid"] += 1
        assert pid < NWC
        pi = rot["piece"].next()
        pv = piece[pi][:, 0:kt * n].rearrange("p (k n) -> p k n", n=n)
        if wstate["pass_idx"] == 0:
            ko = 0
            for s in srcs:
                kk = s.shape[0] // 128
                P.dma("pool", lambda e, s=s, ko=ko, kk=kk: e.dma_start(out=pv[:, ko:ko + kk, :], in_=s.rearrange("(k p) n -> p k n", p=128)),
                      writes=[f"piece{pi}"])
                ko += kk
            P.dma("sp", lambda e: e.dma_start(out=wcache[pid, :, 0:kt * n], in_=piece[pi][:, 0:kt * n]), reads=[f"piece{pi}"], writes=[f"wc{pid}"])
        else:
            P.dma("pool", lambda e: e.dma_start(out=piece[pi][:, 0:kt * n], in_=wcache[pid, :, 0:kt * n]), reads=[f"wc{pid}"], writes=[f"piece{pi}"])
        return pv, f"piece{pi}"

    def mm_fm(pst, pkey, pv, pkey_w, k0, nk, m0, mw, act, akey, NT, first=True, last=True, outp=None):
        for k in range(nk):
            o = pst[0:mw, 0:NT] if outp is None else outp
            P.op("pe", lambda e, k=k, o=o: e.matmul(o, lhsT=pv[:, k0 + k, m0:m0 + mw], rhs=act[:, k, 0:NT],
                                                    start=(first and k == 0), stop=(last and k == nk - 1)),
                 reads=[pkey_w, akey], writes=[pkey])

    def rmsnorm(gi, NT, inplace=False):
        pst, pk = PS()
        for k in range(KC):
            tb, tk = TB()
            P.op("act", lambda e, k=k, tb=tb: e.activation(out=tb[:, 0:NT], in_=xT[:, k, 0:NT], func=AF.Square), reads=["xT"], writes=[tk])
            P.op("pe", lambda e, k=k, tb=tb: e.matmul(pst[:, 0:NT], lhsT=onesm[:], rhs=tb[:, 0:NT], start=(k == 0), stop=(k == KC - 1)),
                 reads=[tk, "onesm"], writes=[pk])
        tf, tfk = TF()
        P.op("act", lambda e: e.activation(out=tf[:, 0:NT], in_=pst[:, 0:NT], func=AF.Sqrt, bias=epsc[:, 0:1], scale=1.0), reads=[pk, "epsc"], writes=[tfk])
        P.op("dve", lambda e: e.reciprocal(out=tf[:, 0:NT], in_=tf[:, 0:NT]), reads=[tfk], writes=[tfk])
        for k in range(KC):
            o = xT[:, k, 0:NT] if inplace else xnT[:, k, 0:NT]
            P.op("dve", lambda e, k=k, o=o: e.scalar_tensor_tensor(out=o, in0=xT[:, k, 0:NT], scalar=gcol[:, gi, k:k + 1], in1=tf[:, 0:NT],
                                                                   op0=ALU.mult, op1=ALU.mult),
                 reads=["xT", "gcol", tfk], writes=["xT" if inplace else "xnT"])

    def ffn(l, w_up, w_dn, NT):
        f0 = 0
        for ng in (6, 6, 6, 4):
            for fl in range(ng):
                f = f0 + fl
                pv, pk = wpiece([w_up[l][:, f * 128:(f + 1) * 128], w_up[l][:, DFF + f * 128:DFF + (f + 1) * 128]])
                pg, pgk = PS()
                mm_fm(pg, pgk, pv, pk, 0, KC, 0, 128, xnT, "xnT", NT)
                pu, puk = PS()
                mm_fm(pu, puk, pv, pk, KC, KC, 0, 128, xnT, "xnT", NT)
                tf, tfk = TF()
                P.op("act", lambda e, tf=tf, pg=pg: e.activation(out=tf[:, 0:NT], in_=pg[:, 0:NT], func=AF.Silu), reads=[pgk], writes=[tfk])
                P.op("dve", lambda e, tf=tf, pu=pu, fl=fl: e.tensor_tensor(out=hT[:, fl, 0:NT], in0=tf[:, 0:NT], in1=pu[:, 0:NT], op=ALU.mult),
                     reads=[tfk, puk], writes=["hT"])
            for c in range(KC):
                pv, pk = wpiece([w_dn[l][f0 * 128:(f0 + ng) * 128, c * 128:(c + 1) * 128]])
                po, pok = PS()
                mm_fm(po, pok, pv, pk, 0, ng, 0, 128, hT, "hT", NT)
                P.op("dve", lambda e, c=c, po=po: e.scalar_tensor_tensor(out=xT[:, c, 0:NT], in0=po[:, 0:NT], scalar=0.5, in1=xT[:, c, 0:NT],
                                                                         op0=ALU.mult, op1=ALU.add),
                     reads=[pok, "xT"], writes=["xT"])
            f0 += ng

    def proj_fm(l, col0, nchunk, NT, evac):
        i = 0
        while i < nchunk:
            g = min(2, nchunk - i)
            pv, pk = wpiece([w_in[l][:, col0 + i * 128: col0 + (i + g) * 128]])
            for j in range(g):
                pst, pkey = PS()
                mm_fm(pst, pkey, pv, pk, 0, KC, j * 128, 128, xnT, "xnT", NT)
                evac(i + j, pst, pkey)
            i += g

    def proj_tm(l, col0, chunks, evac):
        for half in range(2):
            pv, pk = wpiece([w_in[l][:, col0 + half * 256: col0 + (half + 1) * 256]])
            for ci, (c0, L) in enumerate(chunks):
                pst, pkey = PS()
                for k in range(KC):
                    P.op("pe", lambda e, k=k, c0=c0, L=L, pst=pst, pv=pv: e.matmul(pst[0:L, 0:256], lhsT=xnT[:, k, c0:c0 + L], rhs=pv[:, k, :],
                                                                           start=(k == 0), stop=(k == KC - 1)),
                         reads=[pk, "xnT"], writes=[pkey])
                evac(ci, half, pst, pkey)

    def attention(l, c0, L, blocks, NT, pre=None):
        NP = 2 if L > 64 else 4
        NU = NP * 2 * L
        nunit = 4 // NP
        LB = 128 if L == 128 else 16
        b2 = bias2[l] if L == 128 else bias2s[l]
        b2k = f"bias2_{l}_{LB}"
        assert L <= 16 or L == 128
        P.op("dve", lambda e: e.memset(Qbd[:], 0.0), writes=["Qbd"])
        P.op("dve", lambda e: e.tensor_copy(out=Qbd[0:64, :, 0:L], in_=sqT[0:64, :, c0:c0 + L]), reads=["sqT"], writes=["Qbd"])
        P.op("dve", lambda e: e.tensor_copy(out=Qbd[64:128, :, L:2 * L], in_=sqT[64:128, :, c0:c0 + L]), reads=["sqT"], writes=["Qbd"])
        Tb = [ps[4], ps[5]]
        Ob = ps[6]
        P.op("pe", lambda e: e.matmul(Ob[:, 0:512], lhsT=zeros_b[:, 0:128], rhs=zeros_b[:, 0:512], start=True, stop=True),
             reads=["zeros_b"], writes=["ps6"])

        blist = list(blocks)
        seq = [(bi, u) for bi in range(len(blist)) for u in range(nunit)]
        NUN = len(seq)
        stt = {}
        ready = {}

        def getblk(bi):
            if bi not in ready:
                ready[bi] = pre(blist[bi]) if pre is not None else blist[bi]
            return ready[bi]

        def stA(n):
            bi, u = seq[n]
            blk = getblk(bi)
            kt, Lk, kkey, diag = blk["kt"], blk["Lk"], blk["kkey"], blk["diag"]
            pu = u * NP
            S, Sk = PS()
            if L == LB:
                P.op("pe", lambda e: e.matmul(S[0:Lk, 0:NU], lhsT=ones2[0:2, 0:Lk], rhs=b2[0:2, pu * 2 * L: pu * 2 * L + NU], start=True, stop=False),
                     reads=[b2k, "ones2"], writes=[Sk])
            else:
                for hh in range(2 * NP):
                    P.op("pe", lambda e, hh=hh: e.matmul(S[0:Lk, hh * L:(hh + 1) * L], lhsT=ones2[0:2, 0:Lk],
                                                         rhs=b2[0:2, ((pu * 2 + hh) * LB):((pu * 2 + hh) * LB) + L], start=(hh == 0), stop=False),
                         reads=[b2k, "ones2"], writes=[Sk])
            for j in range(NP):
                P.op("pe", lambda e, j=j: e.matmul(S[0:Lk, j * 2 * L:(j + 1) * 2 * L], lhsT=kt[:, pu + j, 0:Lk], rhs=Qbd[:, pu + j, 0:2 * L],
                                                   start=False, stop=(j == NP - 1)),
                     reads=[kkey, "Qbd"], writes=[Sk])
            ei = n % 3
            E, Lp = Et[ei], Lt[ei]
            P.op("act", lambda e: e.activation(out=E[0:Lk, 0:NU], in_=S[0:Lk, 0:NU], func=AF.Exp, scale=0.125), reads=[Sk], writes=[f"E{ei}"])
            P.op("act", lambda e: e.activation(out=Lp[0:Lk, 0:NU], in_=E[0:Lk, 0:NU], func=AF.Ln, bias=1.0, scale=1.0),
                 reads=[f"E{ei}"], writes=[f"L{ei}"])
            if diag:
                mk3 = mask_lt[0:Lk, 0:L].unsqueeze(1).to_broadcast([Lk, 2 * NP, L])
                P.op("dve", lambda e: e.tensor_tensor(out=Lp[0:Lk, 0:NU].rearrange("p (h l) -> p h l", l=L),
                                                      in0=Lp[0:Lk, 0:NU].rearrange("p (h l) -> p h l", l=L), in1=mk3, op=ALU.mult),
                     reads=[f"L{ei}", "mask_lt"], writes=[f"L{ei}"])
                P.op("dve", lambda e: e.tensor_tensor(out=E[0:Lk, 0:NU].rearrange("p (h l) -> p h l", l=L),
                                                      in0=E[0:Lk, 0:NU].rearrange("p (h l) -> p h l", l=L), in1=mk3, op=ALU.mult),
                     reads=[f"E{ei}", "mask_lt"], writes=[f"E{ei}"])

        def stB1(n):
            bi, u = seq[n]
            Lk = getblk(bi)["Lk"]
            ei = n % 3; xi = n % 2
            Lp, X = Lt[ei], Xt[xi]
            T = Tb[u]; Tk = f"ps{4 + u}"
            P.op("pe", lambda e: e.matmul(T[:, 0:NU], lhsT=UI[0:Lk, :], rhs=Lp[0:Lk, 0:NU], start=(bi == 0), stop=True, skip_group_check=True),
                 reads=[f"L{ei}", "UI"], writes=[Tk])
            P.op("act", lambda e: e.activation(out=X[0:Lk, 0:NU], in_=T[0:Lk, 0:NU], func=AF.Exp, scale=-1.0), reads=[Tk], writes=[f"X{xi}"])

        def stB2(n):
            bi, u = seq[n]
            Lk = getblk(bi)["Lk"]
            ei = n % 3; xi = n % 2
            E, Lp, X, Wm = Et[ei], Lt[ei], Xt[xi], Wt[xi]
            T = Tb[u]; Tk = f"ps{4 + u}"
            P.op("pe", lambda e: e.matmul(T[:, 0:NU], lhsT=CM[0:Lk, :], rhs=Lp[0:Lk, 0:NU], start=False, stop=True, skip_group_check=True),
                 reads=[f"L{ei}", "CM"], writes=[Tk])
            P.op("dve", lambda e: e.tensor_tensor(out=Wm[0:Lk, 0:NU], in0=E[0:Lk, 0:NU], in1=X[0:Lk, 0:NU], op=ALU.mult),
                 reads=[f"E{ei}", f"X{xi}"], writes=[f"W{xi}"])

        def stC(n):
            bi, u = seq[n]
            blk = getblk(bi)
            v, Lk, vkey = blk["v"], blk["Lk"], blk["vkey"]
            pu = u * NP
            xi = n % 2
            Wm = Wt[xi]
            for j in range(NP):
                for a in range(2):
                    h = 2 * (pu + j) + a
                    P.op("pe", lambda e, j=j, a=a, h=h: e.matmul(Ob[a * 64:(a + 1) * 64, (pu + j) * L:(pu + j + 1) * L],
                                                                 lhsT=v[0:Lk, h * 64:(h + 1) * 64], rhs=Wm[0:Lk, (2 * j + a) * L:(2 * j + a + 1) * L],
                                                                 start=False, stop=True, skip_group_check=True),
                         reads=[f"W{xi}", vkey], writes=["ps6"])

        for step in range(NUN + 3):
            if 0 <= step - 3 < NUN:
                stC(step - 3)
            if 0 <= step - 2 < NUN:
                stB2(step - 2)
            if 0 <= step - 1 < NUN:
                stB1(step - 1)
            if step < NUN:
                stA(step)
        P.op("act", lambda e: e.activation(out=ybT[0][:, :, c0:c0 + L], in_=Ob[:, 0:4 * L].rearrange("p (k l) -> p k l", l=L), func=AF.Identity),
             reads=["ps6"], writes=["ybT0"])

    def kv_pre(l, b):
        def pre(blk):
            if "cache" in blk:
                ki = rot["ktb"].next(); vi = rot["vb"].next()
                bk = blk["cache"]
                P.dma("sp", lambda e: e.dma_start(out=ktb[ki][:], in_=ktc[l, bk]), reads=[f"kvc{l}"], writes=[f"ktb{ki}"])
                P.dma("sp", lambda e: e.dma_start(out=vb[vi][:], in_=vc[l, bk]), reads=[f"kvc{l}"], writes=[f"vb{vi}"])
                return dict(blk, kt=ktb[ki][:], v=vb[vi][:], kkey=f"ktb{ki}", vkey=f"vb{vi}")
            if "page" in blk:
                ki = rot["ktb"].next(); vi = rot["vb"].next(); gi = rot["pg"].next()
                pg = blk["page"]
                P.dma("pool", lambda e: e.indirect_dma_start(out=kpg[gi][:], out_offset=None, in_=ck,
                                                             in_offset=bass.IndirectOffsetOnAxis(ap=pidx[:, l, b, pg:pg + 1], axis=0)),
                      reads=["pidx"], writes=[f"kpg{gi}"])
                P.dma("pool", lambda e: e.indirect_dma_start(out=vpg[gi][:], out_offset=None, in_=cv,
                                                             in_offset=bass.IndirectOffsetOnAxis(ap=pidx[:, l, b, pg:pg + 1], axis=0)),
                      reads=["pidx"], writes=[f"vpg{gi}"])
                pst, pk = PS()
                for p in range(4):
                    P.op("pe", lambda e, p=p: e.transpose(out=pst[:, p * 128:(p + 1) * 128], in_=kpg[gi][:, p * 128:(p + 1) * 128], identity=ident[:]),
                         reads=[f"kpg{gi}", "ident"], writes=[pk])
                P.op("dve", lambda e: e.tensor_copy(out=ktb[ki][:], in_=pst[:, :].rearrange("p (k n) -> p k n", n=128)), reads=[pk], writes=[f"ktb{ki}"])
                P.op("dve", lambda e: e.tensor_copy(out=vb[vi][:], in_=vpg[gi][:]), reads=[f"vpg{gi}"], writes=[f"vb{vi}"])
                return dict(blk, kt=ktb[ki][:], v=vb[vi][:], kkey=f"ktb{ki}", vkey=f"vb{vi}")
            return blk
        return pre

    def mlstm_chunk(l, si, c0, L, ci, first):
        st = f"mst{si}"
        if first:
            P.op("dve", lambda e: e.tensor_scalar(out=rp8[:], in0=m8[si][:], scalar1=-1.0, scalar2=None, op0=ALU.mult), reads=[st + "m"], writes=["rp8"])
        else:
            P.op("dve", lambda e: e.tensor_copy(out=rp8[:], in_=bT[:, c0 - 1:c0]), reads=["bT"], writes=["rp8"])
        P.op("pe", lambda e: e.transpose(out=ps[7][0:L, 0:8], in_=liT[0:8, c0:c0 + L], identity=ident[0:8, 0:8]), reads=["liT", "ident"], writes=["ps7"])
        P.op("dve", lambda e: e.tensor_copy(out=acol[0:L, :], in_=ps[7][0:L, 0:8]), reads=["ps7"], writes=["acol"])
        for hq in range(2):
            pst, pk = PS()
            for hh in range(4):
                h = hq * 4 + hh
                P.op("pe", lambda e, pst=pst, hh=hh, h=h: e.matmul(pst[:, hh * L:(hh + 1) * L], lhsT=selH[:, h, :], rhs=bT[0:8, c0:c0 + L],
                                                                   start=(hh == 0), stop=(hh == 3)), reads=["selH", "bT"], writes=[pk])
            P.op("act", lambda e, pst=pst, hq=hq: e.activation(out=r_bc[:, hq * 4:(hq + 1) * 4, 0:L],
                                                                in_=pst[:, 0:4 * L].rearrange("p (h l) -> p h l", l=L), func=AF.Identity),
                 reads=[pk], writes=["r_bc"])
        for (src, skey, dst, dkey) in ((bT, "bT", rp_bc, "rp_bc"), (mT, "mT", nm_bc, "nm_bc")):
            pst, pk = PS()
            for p in range(4):
                P.op("pe", lambda e, pst=pst, p=p, src=src: e.matmul(pst[:, p * L:(p + 1) * L], lhsT=selP[:, p, :], rhs=src[0:8, c0:c0 + L],
                                                                     start=(p == 0), stop=(p == 3)), reads=["selP", skey], writes=[pk])
            P.op("dve", lambda e, pst=pst, dst=dst: e.tensor_copy(out=dst[:, :, 0:L], in_=pst[:, 0:4 * L].rearrange("p (h l) -> p h l", l=L)),
                 reads=[pk], writes=[dkey])
        pst, pk = PS()
        for p in range(4):
            P.op("pe", lambda e, pst=pst, p=p: e.matmul(pst[:, p:p + 1], lhsT=selP[:, p, :], rhs=rp8[0:8, 0:1], start=(p == 0), stop=(p == 3)),
                 reads=["selP", "rp8"], writes=[pk])
        P.op("dve", lambda e, pst=pst: e.tensor_scalar(out=nrprev[:], in0=pst[:, 0:4], scalar1=-1.0, scalar2=None, op0=ALU.mult),
             reads=[pk], writes=["nrprev"])
        P.op("dve", lambda e: e.tensor_tensor(out=rend[0:L, :], in0=acol[0:L, :], in1=r_bc[0:L, :, L - 1], op=ALU.add),
             reads=["acol", "r_bc"], writes=["rend"])
        P.op("act", lambda e: e.activation(out=wend[0:L, :], in_=rend[0:L, :], func=AF.Exp), reads=["rend"], writes=["wend"])
        P.op("dve", lambda e: e.tensor_tensor(out=kw[0:L, :].rearrange("p (h d) -> p h d", d=64),
                                              in0=mk_tok[0:L, ci, :].rearrange("p (h d) -> p h d", d=64),
                                              in1=wend[0:L, :].unsqueeze(2).to_broadcast([L, 8, 64]), op=ALU.mult),
             reads=["mk_tok", "wend"], writes=["kw"])
        for p in range(4):
            Sa = []
            for a in range(2):
                po = a * 64
                S, Sk = PS()
                Sa.append((S, Sk))
                P.op("pe", lambda e, S=S, a=a, po=po, p=p: e.matmul(S[0:L, 0:L], lhsT=mkT[po:po + 64, p, c0:c0 + L],
                                                                    rhs=mqT[po:po + 64, p, c0:c0 + L], start=True, stop=True),
                     reads=["mkT", "mqT"], writes=[Sk])
                P.op("act", lambda e, a=a, p=p: e.activation(out=Dt[0:L, a, 0:L], in_=r_bc[0:L, 2 * p + a, 0:L], func=AF.Exp,
                                                              bias=acol[0:L, 2 * p + a:2 * p + a + 1], scale=1.0),
                     reads=["r_bc", "acol"], writes=["Dt"])
            mk3 = mask_le[0:L, 0:L].unsqueeze(1).to_broadcast([L, 2, L])
            P.op("dve", lambda e, mk3=mk3: e.tensor_tensor(out=Dt[0:L, :, 0:L], in0=Dt[0:L, :, 0:L], in1=mk3, op=ALU.mult),
                 reads=["Dt", "mask_le"], writes=["Dt"])
            for a in range(2):
                S, Sk = Sa[a]
                P.op("dve", lambda e, S=S, a=a: e.tensor_tensor(out=SWt[0:L, a, 0:L], in0=Dt[0:L, a, 0:L], in1=S[0:L, 0:L], op=ALU.mult),
                     reads=["Dt", Sk], writes=["SWt"])
            wt, wtk = TF()
            P.op("act", lambda e, wt=wt, p=p: e.activation(out=wt[:, 0:L], in_=rp_bc[:, p, 0:L], func=AF.Exp, bias=nrprev[:, p:p + 1], scale=1.0),
                 reads=["rp_bc", "nrprev"], writes=[wtk])
            qw, qwk = TB()
            P.op("dve", lambda e, wt=wt, qw=qw, p=p: e.tensor_tensor(out=qw[:, 0:L], in0=mqT[:, p, c0:c0 + L], in1=wt[:, 0:L], op=ALU.mult),
                 reads=[wtk, "mqT"], writes=[qwk])
            N_, Nk = PS()
            for a in range(2):
                P.op("pe", lambda e, N_=N_, a=a, p=p: e.matmul(N_[a * 64:(a + 1) * 64, 0:L], lhsT=mv_ext[0:L, ci, p, a * 64:(a + 1) * 64],
                                                               rhs=SWt[0:L, a, 0:L], start=True, stop=False, skip_group_check=True),
                     reads=["mv_ext", "SWt"], writes=[Nk])
            P.op("pe", lambda e, N_=N_, qw=qw, p=p: e.matmul(N_[:, 0:L], lhsT=Cb[si][:, p, :], rhs=qw[:, 0:L], start=False, stop=True,
                                                             skip_group_check=True), reads=[st + "Cb", qwk], writes=[Nk])
            Dn, Dk = PS()
            for a in range(2):
                P.op("pe", lambda e, Dn=Dn, a=a: e.matmul(Dn[a * 64:(a + 1) * 64, 0:L], lhsT=ones_b[0:L, 0:64], rhs=SWt[0:L, a, 0:L],
                                                          start=True, stop=False, skip_group_check=True), reads=["ones_b", "SWt"], writes=[Dk])
            P.op("pe", lambda e, Dn=Dn, qw=qw, p=p: e.matmul(Dn[:, 0:L], lhsT=Npad[si][:, p, :], rhs=qw[:, 0:L], start=False, stop=True,
                                                             skip_group_check=True), reads=[st + "Np", qwk], writes=[Dk])
            en, enk = TF()
            P.op("act", lambda e, en=en, p=p: e.activation(out=en[:, 0:L], in_=nm_bc[:, p, 0:L], func=AF.Exp), reads=["nm_bc"], writes=[enk])
            ad, adk = TF()
            P.op("act", lambda e, ad=ad, Dn=Dn: e.activation(out=ad[:, 0:L], in_=Dn[:, 0:L], func=AF.Abs), reads=[Dk], writes=[adk])
            P.op("dve", lambda e, en=en, ad=ad: e.tensor_tensor(out=en[:, 0:L], in0=ad[:, 0:L], in1=en[:, 0:L], op=ALU.max),
                 reads=[adk, enk], writes=[enk])
            P.op("dve", lambda e, en=en: e.reciprocal(out=en[:, 0:L], in_=en[:, 0:L]), reads=[enk], writes=[enk])
            hr, hrk = TF()
            P.op("dve", lambda e, en=en, hr=hr, N_=N_: e.tensor_tensor(out=hr[:, 0:L], in0=N_[:, 0:L], in1=en[:, 0:L], op=ALU.mult),
                 reads=[Nk, enk], writes=[hrk])
            hs, hsk = TB()
            P.op("act", lambda e, hr=hr, hs=hs: e.activation(out=hs[:, 0:L], in_=hr[:, 0:L], func=AF.Square), reads=[hrk], writes=[hsk])
            M_, Mk = PS()
            P.op("pe", lambda e, M_=M_, hs=hs: e.matmul(M_[:, 0:L], lhsT=blk64[:], rhs=hs[:, 0:L], start=True, stop=True), reads=["blk64", hsk], writes=[Mk])
            rs, rsk = TF()
            P.op("act", lambda e, rs=rs, M_=M_: e.activation(out=rs[:, 0:L], in_=M_[:, 0:L], func=AF.Sqrt, bias=epsc[:, 0:1], scale=1.0), reads=[Mk, "epsc"], writes=[rsk])
            P.op("dve", lambda e, rs=rs: e.reciprocal(out=rs[:, 0:L], in_=rs[:, 0:L]), reads=[rsk], writes=[rsk])
            P.op("dve", lambda e, rs=rs, hr=hr: e.tensor_tensor(out=hr[:, 0:L], in0=hr[:, 0:L], in1=rs[:, 0:L], op=ALU.mult), reads=[hrk, rsk], writes=[hrk])
            P.op("dve", lambda e, hr=hr, p=p: e.scalar_tensor_tensor(out=ybT[1][:, p, c0:c0 + L], in0=hr[:, 0:L], scalar=hgcol[:, l, p:p + 1],
                                                                      in1=moS[:, p, c0:c0 + L], op0=ALU.mult, op1=ALU.mult),
                 reads=[hrk, "hgcol", "moS"], writes=["ybT1"])
            Cn, Ck = PS()
            P.op("pe", lambda e, Cn=Cn, p=p: e.matmul(Cn[:, 0:129], lhsT=kw[0:L, p * 128:(p + 1) * 128], rhs=mv_ext[0:L, ci, p, 0:129], start=True, stop=True),
                 reads=["kw", "mv_ext"], writes=[Ck])
            P.op("act", lambda e, p=p: e.activation(out=decay[:, p:p + 1], in_=rp_bc[:, p, L - 1:L], func=AF.Exp, bias=nrprev[:, p:p + 1], scale=1.0),
                 reads=["rp_bc", "nrprev"], writes=["decay"])
            tc_, tck = TF()
            P.op("dve", lambda e, tc_=tc_, Cn=Cn: e.tensor_tensor(out=tc_[:, 0:128], in0=Cn[:, 0:128], in1=BDf[:], op=ALU.mult), reads=[Ck, "BDf"], writes=[tck])
            P.op("dve", lambda e, tc_=tc_, p=p: e.scalar_tensor_tensor(out=Cpad[si][:, p, :], in0=Cpad[si][:, p, :], scalar=decay[:, p:p + 1],
                                                                       in1=tc_[:, 0:128], op0=ALU.mult, op1=ALU.add),
                 reads=[tck, "decay", st + "C"], writes=[st + "C"])
            P.op("act", lambda e, p=p: e.activation(out=Cb[si][:, p, :], in_=Cpad[si][:, p, :], func=AF.Identity), reads=[st + "C"], writes=[st + "Cb"])
            P.op("dve", lambda e, Cn=Cn, p=p: e.scalar_tensor_tensor(out=ncol[si][:, p:p + 1], in0=ncol[si][:, p:p + 1], scalar=decay[:, p:p + 1],
                                                                     in1=Cn[:, 128:129], op0=ALU.mult, op1=ALU.add),
                 reads=[Ck, "decay", st + "n"], writes=[st + "n"])
            P.op("dve", lambda e, p=p: e.tensor_scalar(out=Npad[si][:, p, :], in0=BDf[:], scalar1=ncol[si][:, p:p + 1], scalar2=None, op0=ALU.mult),
                 reads=[st + "n", "BDf"], writes=[st + "Np"])

    def mlstm_state_init(si, zero, l=None, b=None):
        st = f"mst{si}"
        P.op("dve", lambda e: e.memset(Cpad[si][:], 0.0), writes=[st + "C"])
        P.op("dve", lambda e: e.memset(ncol[si][:], 0.0), writes=[st + "n"])
        if zero:
            P.op("dve", lambda e: e.memset(m8[si][:], 0.0), writes=[st + "m"])
        else:
            for h in range(8):
                po = (h % 2) * 64
                P.dma("sp", lambda e, h=h, po=po: e.dma_start(out=Cpad[si][po:po + 64, h // 2, po:po + 64], in_=sC[l, b, h]), writes=[st + "C"])
                P.dma("sp", lambda e, h=h, po=po: e.dma_start(out=ncol[si][po:po + 64, h // 2:h // 2 + 1], in_=sn[l, b, h].rearrange("(d o) -> d o", o=1)),
                      writes=[st + "n"])
            P.dma("sp", lambda e: e.dma_start(out=m8[si][:], in_=sm[l, b].rearrange("(h o) -> h o", o=1)), writes=[st + "m"])
        P.op("act", lambda e: e.activation(out=Cb[si][:], in_=Cpad[si][:], func=AF.Identity), reads=[st + "C"], writes=[st + "Cb"])
        for p in range(4):
            P.op("dve", lambda e, p=p: e.tensor_scalar(out=Npad[si][:, p, :], in0=BDf[:], scalar1=ncol[si][:, p:p + 1], scalar2=None, op0=ALU.mult),
                 reads=[st + "n", "BDf"], writes=[st + "Np"])

    def mlstm_state_store(si, oC, on, om):
        st = f"mst{si}"
        for h in range(8):
            po = (h % 2) * 64
            P.dma("sp", lambda e, h=h, po=po: e.dma_start(out=oC[h], in_=Cpad[si][po:po + 64, h // 2, po:po + 64]), reads=[st + "C"])
            P.dma("sp", lambda e, h=h, po=po: e.dma_start(out=on[h].rearrange("(d o) -> d o", o=1), in_=ncol[si][po:po + 64, h // 2:h // 2 + 1]),
                  reads=[st + "n"])
        P.dma("sp", lambda e: e.dma_start(out=om.rearrange("(h o) -> h o", o=1), in_=m8[si][:]), reads=[st + "m"])

    def run_st(kind, t0, NT, chunks, segs):
        begin_pass()
        xsrc = xp if kind == "p" else xs
        ydst = yp if kind == "p" else ys
        ntile = (NT + 127) // 128
        for t in range(ntile):
            pt_ = min(128, NT - t * 128)
            xi = rot["xin"].next()
            P.dma("sp", lambda e, t=t, pt_=pt_, xi=xi: e.dma_start(out=xin[xi][0:pt_, :], in_=xsrc[t0 + t * 128:t0 + t * 128 + pt_, :]), writes=[f"xin{xi}"])
            DBG = ""
            for g in range(2):
                if DBG == "a":
                    break
                pst, pk = PS()
                for kk in range(4):
                    k = g * 4 + kk
                    P.op("pe", lambda e, k=k, kk=kk, pst=pst, xi=xi, pt_=pt_: e.transpose(out=pst[:, kk * 128:kk * 128 + pt_], in_=xin[xi][0:pt_, k * 128:(k + 1) * 128],
                                                                                             identity=ident[0:pt_, 0:pt_]),
                         reads=[f"xin{xi}", "ident"], writes=[pk])
                if DBG == "b":
                    continue
                if DBG == "c":
                    P.op("dve", lambda e, g=g, pst=pst, t=t, pt_=pt_: e.tensor_copy(out=xT[:, g * 4:(g + 1) * 4, t * 128:t * 128 + pt_],
                                                                                    in_=pst[:, :].rearrange("p (k n) -> p k n", n=128)[:, :, 0:pt_]),
                         reads=[pk], writes=["xT"])
                    continue
                if DBG == "d":
                    P.op("act", lambda e, g=g, pst=pst, t=t, pt_=pt_: e.activation(out=xT[:, g * 4:(g + 1) * 4, t * 128:t * 128 + pt_],
                                                                                    in_=pst[:, :].rearrange("p (k n) -> p k n", n=128)[:, :, 0:pt_], func=AF.Identity),
                         reads=[pk], writes=["xT"])
                    continue
                if DBG == "e":
                    for kk in range(4):
                        P.op("act", lambda e, g=g, kk=kk, pst=pst, t=t, pt_=pt_: e.activation(out=xT[:, g * 4 + kk, t * 128:t * 128 + pt_],
                                                                                        in_=pst[:, kk * 128:kk * 128 + pt_], func=AF.Identity),
                             reads=[pk], writes=["xT"])
                    continue
                P.op("act", lambda e, g=g, pst=pst, t=t, pt_=pt_: e.activation(out=xT[:, g * 4:(g + 1) * 4, t * 128:t * 128 + pt_],
                                                                                in_=pst[:, :].rearrange("p (k n) -> p k n", n=128)[:, :, 0:pt_], func=AF.Identity),
                     reads=[pk], writes=["xT"])
        _ck(1)
        for l in range(DEPTH):
            run_layer(kind, t0, NT, chunks, segs, l)
        final_store(kind, t0, NT, ntile, ydst)

    def run_layer(kind, t0, NT, chunks, segs, l):
        if True:
            rmsnorm(3 * l, NT)
            ffn(l, w_up1, w_dn1, NT)
            _ck(2)
            rmsnorm(3 * l + 1, NT)
            def ev_bf(dst, dkey, scale=None):
                def f(i, pst, pkey):
                    if scale is None:
                        P.op("act", lambda e: e.activation(out=dst[:, i, 0:NT], in_=pst[:, 0:NT], func=AF.Identity), reads=[pkey], writes=[dkey])
                    else:
                        P.op("act", lambda e: e.activation(out=dst[:, i, 0:NT], in_=pst[:, 0:NT], func=AF.Identity, scale=scale), reads=[pkey], writes=[dkey])
                return f
            proj_fm(l, OFF["sq"], 4, NT, ev_bf(sqT, "sqT"))
            proj_fm(l, OFF["sk"], 4, NT, ev_bf(skT, "skT"))
            proj_fm(l, OFF["mq"], 4, NT, ev_bf(mqT, "mqT"))
            proj_fm(l, OFF["mk"], 4, NT, ev_bf(mkT, "mkT", 0.125))
            _ck(21)
            proj_fm(l, OFF["mo"], 4, NT, lambda i, pst, pkey: P.op("act", lambda e: e.activation(out=moS[:, i, 0:NT], in_=pst[:, 0:NT], func=AF.Sigmoid),
                                                                   reads=[pkey], writes=["moS"]))
            proj_fm(l, OFF["cb"], 4, NT, lambda i, pst, pkey: P.op("act", lambda e: e.activation(out=cbT[:, i, 0:NT], in_=pst[:, 0:NT], func=AF.Identity),
                                                                   reads=[pkey], writes=["cbT"]))
            _ck(22)
            for i in range(4):
                pv, pk = wpiece([w_in[l][:, OFF["cc"] + i * 128:OFF["cc"] + (i + 1) * 128], w_in[l][:, OFF["ch"] + i * 128:OFF["ch"] + (i + 1) * 128]])
                pc, pck = PS()
                mm_fm(pc, pck, pv, pk, 0, KC, 0, 128, xnT, "xnT", NT)
                ph, phk = PS()
                mm_fm(ph, phk, pv, pk, KC, KC, 0, 128, xnT, "xnT", NT)
                tf, tfk = TF()
                P.op("act", lambda e, tf=tf, pc=pc: e.activation(out=tf[:, 0:NT], in_=pc[:, 0:NT], func=AF.Identity), reads=[pck], writes=[tfk])
                if kind == "p":
                    P.op("dve", lambda e, tf=tf, ph=ph, i=i: e.tensor_tensor(out=uext_p[l][:, i, 2:2 + NT], in0=tf[:, 0:NT], in1=ph[:, 0:NT], op=ALU.mult),
                         reads=[tfk, phk], writes=["uext_1"])
                else:
                    P.op("dve", lambda e, tf=tf, ph=ph: e.tensor_tensor(out=tf[:, 0:NT], in0=tf[:, 0:NT], in1=ph[:, 0:NT], op=ALU.mult),
                         reads=[tfk, phk], writes=[tfk])
                    P.op("dve", lambda e, tf=tf, i=i: e.tensor_copy(out=cbT[:, i, NT:2 * NT], in_=tf[:, 0:NT]), reads=[tfk], writes=["cbT"])
            _ck(23)
            pv, pk = wpiece([w_in[l][:, OFF["mi"]:OFF["mi"] + 16]])
            pi_, pik = PS()
            mm_fm(pi_, pik, pv, pk, 0, KC, 0, 8, xnT, "xnT", NT)
            pf_, pfk = PS()
            mm_fm(pf_, pfk, pv, pk, 0, KC, 8, 8, xnT, "xnT", NT)
            P.op("dve", lambda e, pi_=pi_: e.tensor_scalar(out=liT[:, 0:NT], in0=pi_[0:8, 0:NT], scalar1=b8[:, l, 0:1], scalar2=None, op0=ALU.add),
                 reads=[pik, "b8"], writes=["liT"])
            P.op("act", lambda e, pf_=pf_: e.activation(out=lfT[:, 0:NT], in_=pf_[0:8, 0:NT], func=AF.Exp, bias=b8[:, l, 2:3], scale=-1.0),
                 reads=[pfk, "b8"], writes=["lfT"])
            P.op("act", lambda e: e.activation(out=lfT[:, 0:NT], in_=lfT[:, 0:NT], func=AF.Ln, bias=1.0, scale=1.0), reads=["lfT"], writes=["lfT"])
            P.op("dve", lambda e: e.tensor_scalar(out=lfT[:, 0:NT], in0=lfT[:, 0:NT], scalar1=-1.0, scalar2=None, op0=ALU.mult), reads=["lfT"], writes=["lfT"])
            _ck(24)
            kout = kp if kind == "p" else ks
            vout = vp if kind == "p" else vs

            def ev_out(dst):
                def f(ci, half, pst, pkey):
                    c0, L = chunks[ci]
                    tf, tfk = TF()
                    P.op("act", lambda e: e.activation(out=tf[0:L, 0:256], in_=pst[0:L, 0:256], func=AF.Identity), reads=[pkey], writes=[tfk])
                    P.dma("sp", lambda e: e.dma_start(out=dst[l, t0 + c0:t0 + c0 + L, half * 256:(half + 1) * 256], in_=tf[0:L, 0:256]), reads=[tfk])
                return f
            proj_tm(l, OFF["sk"], chunks, ev_out(kout))

            def ev_v(ci, half, pst, pkey):
                c0, L = chunks[ci]
                ev_out(vout)(ci, half, pst, pkey)
                P.op("dve", lambda e: e.tensor_copy(out=sv_tok[0:L, ci, half * 256:(half + 1) * 256], in_=pst[0:L, 0:256]), reads=[pkey], writes=["sv_tok"])
            _ck(25)
            proj_tm(l, OFF["sv"], chunks, ev_v)
            _ck(26)
            proj_tm(l, OFF["mk"], chunks, lambda ci, half, pst, pkey: P.op(
                "act", lambda e: e.activation(out=mk_tok[0:chunks[ci][1], ci, half * 256:(half + 1) * 256], in_=pst[0:chunks[ci][1], 0:256], func=AF.Identity, scale=0.125),
                reads=[pkey], writes=["mk_tok"]))
            proj_tm(l, OFF["mv"], chunks, lambda ci, half, pst, pkey: P.op(
                "dve", lambda e: e.tensor_copy(out=mv_ext[0:chunks[ci][1], ci, half * 2:(half + 1) * 2, 0:128],
                                               in_=pst[0:chunks[ci][1], 0:256].rearrange("p (a n) -> p a n", n=128)),
                reads=[pkey], writes=["mv_ext"]))
            _ck(3)
            if kind == "p":
                for ci, (c0, L) in enumerate(chunks):
                    blocks = [dict(kt=skT[:, :, c0:c0 + L], v=sv_tok[:, ci, :], Lk=L, diag=True, kkey="skT", vkey="sv_tok")]
                    for cj in range(ci - 1, -1, -1):
                        cc0, LL = chunks[cj]
                        blocks.append(dict(kt=skT[:, :, cc0:cc0 + LL], v=sv_tok[:, cj, :], Lk=LL, diag=False, kkey="skT", vkey="sv_tok"))
                    for bk in range(t0 // 128 - 1, -1, -1):
                        blocks.append(dict(cache=bk, Lk=128, diag=False))
                    attention(l, c0, L, blocks, NT, kv_pre(l, None))
                for ci, (c0, L) in enumerate(chunks):
                    if L == 128:
                        bk = (t0 + c0) // 128
                        P.dma("sp", lambda e, bk=bk, c0=c0: e.dma_start(out=ktc[l, bk], in_=skT[:, :, c0:c0 + 128]), reads=["skT"], writes=[f"kvc{l}"])
                        P.dma("sp", lambda e, bk=bk, ci=ci: e.dma_start(out=vc[l, bk], in_=sv_tok[:, ci, :]), reads=["sv_tok"], writes=[f"kvc{l}"])
            else:
                for (s0, Ls, cis, b) in segs:
                    blocks = [dict(kt=skT[:, :, s0:s0 + Ls], v=sv_tok[:, cis[0], :], Lk=Ls, diag=True, kkey="skT", vkey="sv_tok")]
                    for pg in range(NPG - 1, -1, -1):
                        blocks.append(dict(page=pg, Lk=128, diag=False))
                    attention(l, s0, Ls, blocks, NT, kv_pre(l, b))
            _ck(4)
            for (s0, Ls, cis, b) in segs:
                si = l if kind == "p" else DEPTH
                st = f"mst{si}"
                if kind == "s":
                    mlstm_state_init(si, False, l, b)
                elif t0 == 0:
                    mlstm_state_init(si, True)
                P.op("dve", lambda e, s0=s0, Ls=Ls, si=si: e.tensor_tensor_scan(out=mT[:, s0:s0 + Ls], data0=lfT[:, s0:s0 + Ls], data1=liT[:, s0:s0 + Ls],
                                                                               initial=m8[si][:, 0:1], op0=ALU.add, op1=ALU.max),
                     reads=["lfT", "liT", st + "m"], writes=["mT"])
                P.op("dve", lambda e, s0=s0, Ls=Ls: e.tensor_tensor_scan(out=bT[:, s0:s0 + Ls], data0=lfT[:, s0:s0 + Ls], data1=zeros8[:, s0:s0 + Ls],
                                                                        initial=0.0, op0=ALU.add, op1=ALU.add),
                     reads=["lfT", "zeros8"], writes=["bT"])
                P.op("dve", lambda e, s0=s0, Ls=Ls: e.tensor_tensor(out=liT[:, s0:s0 + Ls], in0=liT[:, s0:s0 + Ls], in1=bT[:, s0:s0 + Ls], op=ALU.subtract),
                     reads=["liT", "bT"], writes=["liT"])
                P.op("dve", lambda e, s0=s0, Ls=Ls: e.tensor_tensor(out=bT[:, s0:s0 + Ls], in0=bT[:, s0:s0 + Ls], in1=mT[:, s0:s0 + Ls], op=ALU.subtract),
                     reads=["mT", "bT"], writes=["bT"])
                P.op("dve", lambda e, s0=s0, Ls=Ls: e.tensor_scalar(out=mT[:, s0:s0 + Ls], in0=mT[:, s0:s0 + Ls], scalar1=-1.0, scalar2=None, op0=ALU.mult),
                     reads=["mT"], writes=["mT"])
                first = True
                for ci in cis:
                    c0, L = chunks[ci]
                    mlstm_chunk(l, si, c0, L, ci, first)
                    first = False
                P.op("dve", lambda e, s0=s0, Ls=Ls, si=si: e.tensor_scalar(out=m8[si][:], in0=mT[:, s0 + Ls - 1:s0 + Ls], scalar1=-1.0, scalar2=None, op0=ALU.mult),
                     reads=["mT"], writes=[st + "m"])
                if kind == "s":
                    mlstm_state_store(si, Cs[l, b], ns_[l, b], ms_[l, b])
            _ck(5)
            for (s0, Ls, cis, b) in segs:
                if kind == "p":
                    ue = uext_1; uk = "uext_1"
                    if t0 == 0:
                        P.op("dve", lambda e, ue=ue: e.memset(ue[:, :, 0:2], 0.0), writes=[uk])
                    else:
                        P.op("dve", lambda e, ue=ue: e.tensor_copy(out=ue[:, :, 0:2], in_=halo[l][:]), reads=[f"halo{l}"], writes=[uk])
                else:
                    ue = uext_s; uk = "uext_s"
                    for tt in range(2):
                        P.dma("sp", lambda e, b=b, tt=tt: e.dma_start(out=ue[:, :, tt], in_=scv[l, b, tt].rearrange("(k p) -> p k", p=128), allow_slow_non_contiguous=True),
                              writes=[uk])
                    P.op("dve", lambda e, s0=s0, Ls=Ls: e.tensor_copy(out=ue[:, :, 2:2 + Ls], in_=cbT[:, :, NT + s0:NT + s0 + Ls]), reads=["cbT"], writes=[uk])
                for k in range(4):
                    tf, tfk = TF()
                    P.op("dve", lambda e, tf=tf, k=k, ue=ue, Ls=Ls: e.tensor_scalar(out=tf[:, 0:Ls], in0=ue[:, k, 0:Ls], scalar1=cwcol[:, l, k, 0:1], scalar2=None, op0=ALU.mult),
                         reads=[uk, "cwcol"], writes=[tfk])
                    for j in (1, 2):
                        P.op("dve", lambda e, tf=tf, k=k, j=j, ue=ue, Ls=Ls: e.scalar_tensor_tensor(out=tf[:, 0:Ls], in0=ue[:, k, j:j + Ls], scalar=cwcol[:, l, k, j:j + 1],
                                                                                                    in1=tf[:, 0:Ls], op0=ALU.mult, op1=ALU.add),
                             reads=[uk, "cwcol", tfk], writes=[tfk])
                    P.op("dve", lambda e, tf=tf, k=k, s0=s0, Ls=Ls: e.tensor_tensor(out=ybT[2][:, k, s0:s0 + Ls], in0=tf[:, 0:Ls], in1=cbT[:, k, s0:s0 + Ls], op=ALU.mult),
                         reads=[tfk, "cbT"], writes=["ybT2"])
                last_p = (kind == "p" and t0 + NT == TP)
                if kind == "s" or last_p:
                    dst = cvs[l, b] if kind == "s" else cvp[l]
                    for tt in range(2):
                        P.dma("sp", lambda e, dst=dst, ue=ue, Ls=Ls, tt=tt: e.dma_start(out=dst[tt].rearrange("(k p) -> p k", p=128), in_=ue[:, :, Ls + tt],
                                                                                       allow_slow_non_contiguous=True), reads=[uk])
                if kind == "p" and not last_p:
                    P.op("dve", lambda e, ue=ue, Ls=Ls: e.tensor_copy(out=halo[l][:], in_=ue[:, :, Ls:Ls + 2]), reads=[uk], writes=[f"halo{l}"])
            _ck(6)
            for c in range(KC):
                acc, acck = TF()
                for j in range(3):
                    pv, pk = wpiece([w_in[l][:, OFF["g"] + j * D + c * 128: OFF["g"] + j * D + (c + 1) * 128], w_br[j][l][:, c * 128:(c + 1) * 128]])
                    pg, pgk = PS()
                    mm_fm(pg, pgk, pv, pk, 0, KC, 0, 128, xnT, "xnT", NT)
                    pb, pbk = PS()
                    mm_fm(pb, pbk, pv, pk, KC, 4, 0, 128, ybT[j], f"ybT{j}", NT)
                    sg, sgk = TF()
                    P.op("act", lambda e, sg=sg, pg=pg: e.activation(out=sg[:, 0:NT], in_=pg[:, 0:NT], func=AF.Sigmoid), reads=[pgk], writes=[sgk])
                    if j == 0:
                        P.op("dve", lambda e, sg=sg, pb=pb, acc=acc: e.tensor_tensor(out=acc[:, 0:NT], in0=sg[:, 0:NT], in1=pb[:, 0:NT], op=ALU.mult),
                             reads=[sgk, pbk], writes=[acck])
                    else:
                        P.op("dve", lambda e, sg=sg, pb=pb: e.tensor_tensor(out=sg[:, 0:NT], in0=sg[:, 0:NT], in1=pb[:, 0:NT], op=ALU.mult),
                             reads=[sgk, pbk], writes=[sgk])
                        if j == 1:
                            P.op("dve", lambda e, sg=sg, acc=acc: e.tensor_tensor(out=acc[:, 0:NT], in0=acc[:, 0:NT], in1=sg[:, 0:NT], op=ALU.add),
                                 reads=[sgk, acck], writes=[acck])
                        else:
                            P.op("dve", lambda e, sg=sg, acc=acc, c=c: e.tensor_tensor(out=mergedT[:, c, 0:NT], in0=acc[:, 0:NT], in1=sg[:, 0:NT], op=ALU.add),
                                 reads=[sgk, acck], writes=["hT"])
            for c in range(KC):
                pv, pk = wpiece([w_o[l][:, c * 128:(c + 1) * 128]])
                po, pok = PS()
                mm_fm(po, pok, pv, pk, 0, KC, 0, 128, mergedT, "hT", NT)
                P.op("dve", lambda e, c=c, po=po: e.tensor_tensor(out=xT[:, c, 0:NT], in0=po[:, 0:NT], in1=xT[:, c, 0:NT], op=ALU.add),
                     reads=[pok, "xT"], writes=["xT"])
            _ck(7)
            rmsnorm(3 * l + 2, NT)
            ffn(l, w_up2, w_dn2, NT)
    def final_store(kind, t0, NT, ntile, ydst):
        rmsnorm(3 * DEPTH, NT, inplace=True)
        for t in range(ntile):
            pt_ = min(128, NT - t * 128)
            xi = rot["xin"].next()
            for g in range(2):
                pst, pk = PS()
                for kk in range(4):
                    k = g * 4 + kk
                    P.op("pe", lambda e, k=k, kk=kk, pst=pst, t=t, pt_=pt_: e.transpose(out=pst[0:pt_, kk * 128:(kk + 1) * 128], in_=xT[:, k, t * 128:t * 128 + pt_],
                                                                                         identity=ident[:]),
                         reads=["xT", "ident"], writes=[pk])
                P.op("act", lambda e, g=g, pst=pst, xi=xi, pt_=pt_: e.activation(out=xin[xi][0:pt_, g * 512:(g + 1) * 512], in_=pst[0:pt_, 0:512], func=AF.Identity),
                     reads=[pk], writes=[f"xin{xi}"])
            P.dma("sp", lambda e, t=t, pt_=pt_, xi=xi: e.dma_start(out=ydst[t0 + t * 128:t0 + t * 128 + pt_, :], in_=xin[xi][0:pt_, :]), reads=[f"xin{xi}"])

    def _sched():
        t0 = 0
        while t0 < TP:
            NT = min(512, TP - t0)
            chunks = [(c, min(128, NT - c)) for c in range(0, NT, 128)]
            run_st("p", t0, NT, chunks, [(0, NT, list(range(len(chunks))), None)])
            t0 += NT
        for l in range(DEPTH):
            mlstm_state_store(l, Cp[l], np_[l], mp[l])
        chunks = [(s * DS, DS) for s in range(NSEQ)]
        run_st("s", 0, NS, chunks, [(s * DS, DS, [s], s) for s in range(NSEQ)])

    try:
        _sched()
    except _Stop:
        print('STOPPED early; ops:', len(P.ops))
    P.emit()
    return nc


_NC_CACHE = {}


def _run(inp, n_cores):
    f32 = np.float32
    xpr = np.asarray(inp["x_prompt"], f32); xsa = np.asarray(inp["x_sample"], f32)
    B, SEQ, _ = xpr.shape
    DB, DS_, _ = xsa.shape
    ckk = np.asarray(inp["cache_sb_k"], f32); cvv = np.asarray(inp["cache_sb_v"], f32)
    DEPTH, NPHYS = ckk.shape[0], ckk.shape[1]
    ptab = np.asarray(inp["page_table"], np.int32)
    NPG = ptab.shape[1]
    NSEQ = DB // n_cores
    TP = SEQ + 16
    cfg = dict(TP=TP, NSEQ=NSEQ, NPG=NPG, NPHYS=NPHYS, DEPTH=DEPTH)
    key = tuple(sorted(cfg.items()))
    if key not in _NC_CACHE:
        _NC_CACHE[key] = build(cfg)
    nc = _NC_CACHE[key]
    meta = np.asarray(inp["meta_tokens"], f32)
    ck2 = np.ascontiguousarray(ckk.reshape(DEPTH * NPHYS * 128, 512)); cv2 = np.ascontiguousarray(cvv.reshape(DEPTH * NPHYS * 128, 512))
    A = lambda k: np.ascontiguousarray(np.asarray(inp[k], f32))
    common = dict(ck=ck2, cv=cv2, g1=A("norm_ff1"), up1=A("ffn1_w_up"), dn1=A("ffn1_w_down"), gm=A("norm_mix"), win=A("w_in"),
                  sbb=A("sb_logit_bias"), ib=A("ml_igate_bias"), fb=A("ml_fgate_bias"), hg=A("ml_head_norm"), cw=A("conv_w"),
                  brs=A("w_branch_sb"), brm=A("w_branch_ml"), brc=A("w_branch_cv"), wo=A("w_out"), g2=A("norm_ff2"),
                  up2=A("ffn2_w_up"), dn2=A("ffn2_w_down"), gf=A("norm_final"))
    sC = A("state_ml_C"); sn = A("state_ml_n"); sm = A("state_ml_m"); scv = A("state_conv")
    in_maps = []
    for c in range(n_cores):
        if c < B:
            xpc = np.concatenate([meta, xpr[c]], axis=0)
        else:
            xpc = np.zeros((TP, D), f32)
        sl = slice(c * NSEQ, (c + 1) * NSEQ)
        m = dict(common)
        m.update(xp=np.ascontiguousarray(xpc), xs=np.ascontiguousarray(xsa[sl].reshape(NSEQ * DS_, D)), pt=np.ascontiguousarray(ptab[sl]),
                 sC=np.ascontiguousarray(sC[:, sl]), sn=np.ascontiguousarray(sn[:, sl]), sm=np.ascontiguousarray(sm[:, sl]),
                 scv=np.ascontiguousarray(scv[:, sl]))
        in_maps.append(m)
    res = run_bass_kernel_spmd(nc, in_maps, core_ids=list(range(n_cores))).results
    R = lambda k, cs: [np.asarray(res[c][k], f32) for c in cs]
    pb = list(range(B)); ac = list(range(n_cores))
    y_prompt = np.stack([r[16:] for r in R("yp", pb)])
    y_sample = np.concatenate([r.reshape(NSEQ, DS_, D) for r in R("ys", ac)])
    kpo = np.stack(R("kp", pb), axis=1).reshape(DEPTH, B, TP, 8, 64)
    vpo = np.stack(R("vp", pb), axis=1).reshape(DEPTH, B, TP, 8, 64)
    Cpo = np.stack(R("Cp", pb), axis=1); npo = np.stack(R("np", pb), axis=1); mpo = np.stack(R("mp", pb), axis=1)
    cvpo = np.stack(R("cvp", pb), axis=1)
    kso = np.concatenate([r.reshape(DEPTH, NSEQ, DS_, 8, 64) for r in R("ks", ac)], axis=1)
    vso = np.concatenate([r.reshape(DEPTH, NSEQ, DS_, 8, 64) for r in R("vs", ac)], axis=1)
    Cso = np.concatenate(R("Cs", ac), axis=1); nso = np.concatenate(R("ns", ac), axis=1); mso = np.concatenate(R("ms", ac), axis=1)
    cvso = np.concatenate(R("cvs", ac), axis=1)
    return (y_prompt, y_sample, kpo, vpo, Cpo, npo, mpo, cvpo, kso, vso, Cso, nso, mso, cvso)


def kernel(**inputs):
    return _run(inputs, 8)
```
